# Optimizing a Trainium2 kernel written in Bass

```python
import math
import jax, jax.numpy as jnp
from jax import lax
import numpy as np

D_MODEL = 1024
BATCH = 8
SEQ = 4096
DEPTH = 4

MIX_WIDTH = D_MODEL
FOURIER_WIDTH = MIX_WIDTH // 2
ATTN_WIDTH = MIX_WIDTH - FOURIER_WIDTH
N_FOURIER_GROUPS = 4
FOURIER_GROUP_DIM = FOURIER_WIDTH // N_FOURIER_GROUPS
N_ATTN_HEADS = 4
HEAD_DIM = ATTN_WIDTH // (2 * N_ATTN_HEADS)
V_HEAD_DIM = 2 * HEAD_DIM
QK_WIDTH = N_ATTN_HEADS * 2 * HEAD_DIM
V_WIDTH = N_ATTN_HEADS * V_HEAD_DIM
IN_WIDTH = FOURIER_WIDTH + 2 * QK_WIDTH + V_WIDTH + MIX_WIDTH
ROT_DIM = HEAD_DIM // 4
ROPE_THETA = 500000.0
N_META = 16
Q_BLOCK = 128
NORM_EPS = 1e-6

kernel_name = "hybrid_fourier_diffattn_encoder"


def rms_norm(x, g, eps=NORM_EPS):
    xf = x.astype(jnp.float32)
    y = xf * lax.rsqrt(jnp.mean(xf * xf, axis=-1, keepdims=True) + eps)
    return y.astype(x.dtype) * g


def rope_tables(length):
    inv_freq = ROPE_THETA ** (-jnp.arange(0, ROT_DIM, 2, dtype=jnp.float32) / ROT_DIM)
    pos = jnp.arange(length, dtype=jnp.float32)
    ang = pos[:, None] * inv_freq[None, :]
    return jnp.cos(ang), jnp.sin(ang)


def apply_partial_rotary(x, cos, sin):
    xr = x[..., :ROT_DIM].astype(jnp.float32)
    half = ROT_DIM // 2
    x1, x2 = xr[..., :half], xr[..., half:]
    c = cos[None, :, None, None, :]
    s = sin[None, :, None, None, :]
    rot = jnp.concatenate([x1 * c - x2 * s, x2 * c + x1 * s], axis=-1)
    return jnp.concatenate([rot.astype(x.dtype), x[..., ROT_DIM:]], axis=-1)


def fourier_mixer(f_in, w_f):
    b, l, _ = f_in.shape
    u = f_in.reshape(b, l, N_FOURIER_GROUPS, FOURIER_GROUP_DIM).astype(jnp.float32)
    f = jnp.fft.fft2(u, axes=(1, 3), norm="ortho").real.astype(f_in.dtype)
    out = jnp.einsum('blgc,gce->blge', f, w_f)
    return out.reshape(b, l, FOURIER_WIDTH)


def diff_attention(q, k, v, lam):
    b, l = q.shape[0], q.shape[1]
    seq = l - N_META
    scale = HEAD_DIM ** -0.5
    vf = v.astype(jnp.float32)
    lam_f = lam.astype(jnp.float32)

    def attend(qb):
        s = jnp.einsum('bqhcd,bkhcd->bhcqk', qb, k).astype(jnp.float32) * scale
        p = jax.nn.softmax(s, axis=-1)
        a = p[:, :, 0] - lam_f * p[:, :, 1]
        o = jnp.einsum('bhqk,bkhd->bqhd', a, vf)
        return o.astype(v.dtype)

    o_meta = attend(q[:, :N_META])
    nb = seq // Q_BLOCK
    q_real = q[:, N_META:].reshape(b, nb, Q_BLOCK, N_ATTN_HEADS, 2, HEAD_DIM)
    o_real = lax.map(attend, jnp.moveaxis(q_real, 1, 0))
    o_real = jnp.moveaxis(o_real, 0, 1).reshape(b, seq, N_ATTN_HEADS, V_HEAD_DIM)
    return jnp.concatenate([o_meta, o_real], axis=1)


def setup_inputs(seed: int = 0) -> dict:
    key = jax.random.key(seed)
    ks = jax.random.split(key, 16)
    f32 = jnp.float32
    x = jax.random.normal(ks[0], (BATCH, SEQ, D_MODEL), f32)
    meta_tokens = jax.random.normal(ks[1], (N_META, D_MODEL), f32)
    norm_gain = 1.0 + 0.02 * jax.random.normal(ks[2], (DEPTH, D_MODEL), f32)
    w_in = jax.random.normal(ks[3], (DEPTH, D_MODEL, IN_WIDTH), f32) * D_MODEL ** -0.5
    w_fourier = jax.random.normal(ks[4], (DEPTH, N_FOURIER_GROUPS, FOURIER_GROUP_DIM, FOURIER_GROUP_DIM), f32) * FOURIER_GROUP_DIM ** -0.5
    q_norm_gain = 1.0 + 0.02 * jax.random.normal(ks[5], (DEPTH, HEAD_DIM), f32)
    k_norm_gain = 1.0 + 0.02 * jax.random.normal(ks[6], (DEPTH, HEAD_DIM), f32)
    lambda_q1 = 0.1 * jax.random.normal(ks[7], (DEPTH, HEAD_DIM), f32)
    lambda_k1 = 0.1 * jax.random.normal(ks[8], (DEPTH, HEAD_DIM), f32)
    lambda_q2 = 0.1 * jax.random.normal(ks[9], (DEPTH, HEAD_DIM), f32)
    lambda_k2 = 0.1 * jax.random.normal(ks[10], (DEPTH, HEAD_DIM), f32)
    subln_gain = 1.0 + 0.02 * jax.random.normal(ks[11], (DEPTH, V_HEAD_DIM), f32)
    w_out = jax.random.normal(ks[12], (DEPTH, MIX_WIDTH, D_MODEL), f32) * MIX_WIDTH ** -0.5
    return {"x": x, "meta_tokens": meta_tokens, "norm_gain": norm_gain, "w_in": w_in,
            "w_fourier": w_fourier, "q_norm_gain": q_norm_gain, "k_norm_gain": k_norm_gain,
            "lambda_q1": lambda_q1, "lambda_k1": lambda_k1, "lambda_q2": lambda_q2,
            "lambda_k2": lambda_k2, "subln_gain": subln_gain, "w_out": w_out}


def reference(x, meta_tokens, norm_gain, w_in, w_fourier, q_norm_gain, k_norm_gain,
              lambda_q1, lambda_k1, lambda_q2, lambda_k2, subln_gain, w_out):
    b = x.shape[0]
    meta = jnp.broadcast_to(meta_tokens[None].astype(x.dtype), (b, N_META, D_MODEL))
    h_res = jnp.concatenate([meta, x], axis=1)
    l = h_res.shape[1]
    cos, sin = rope_tables(l)
    splits = [FOURIER_WIDTH, FOURIER_WIDTH + QK_WIDTH, FOURIER_WIDTH + 2 * QK_WIDTH,
              FOURIER_WIDTH + 2 * QK_WIDTH + V_WIDTH]

    for li in range(DEPTH):
        lambda_init = 0.8 - 0.6 * math.exp(-0.3 * li)
        h = rms_norm(h_res, norm_gain[li])
        proj = h @ w_in[li]
        f_in, q, k, v, gate = jnp.split(proj, splits, axis=-1)

        f_out = fourier_mixer(f_in, w_fourier[li])

        q = q.reshape(b, l, N_ATTN_HEADS, 2, HEAD_DIM)
        k = k.reshape(b, l, N_ATTN_HEADS, 2, HEAD_DIM)
        v = v.reshape(b, l, N_ATTN_HEADS, V_HEAD_DIM)
        q = apply_partial_rotary(rms_norm(q, q_norm_gain[li]), cos, sin)
        k = apply_partial_rotary(rms_norm(k, k_norm_gain[li]), cos, sin)
        lam = (jnp.exp(jnp.sum(lambda_q1[li].astype(jnp.float32) * lambda_k1[li].astype(jnp.float32)))
               - jnp.exp(jnp.sum(lambda_q2[li].astype(jnp.float32) * lambda_k2[li].astype(jnp.float32)))
               + lambda_init)
        o = diff_attention(q, k, v, lam)
        o = rms_norm(o, subln_gain[li]) * (1.0 - lambda_init)
        a_out = o.reshape(b, l, V_WIDTH)

        y = jnp.concatenate([f_out, a_out], axis=-1) * jax.nn.silu(gate)
        h_res = h_res + y @ w_out[li]

    return h_res[:, N_META:]
```

```python
import math
import numpy as np
import ml_dtypes
import concourse.bass as bass
import concourse.mybir as mybir
from concourse.bass_utils import run_bass_kernel_spmd

F32 = mybir.dt.float32
BF16 = mybir.dt.bfloat16
AF = mybir.ActivationFunctionType
ALU = mybir.AluOpType
AX = mybir.AxisListType

D = 1024
NMETA = 16
HD = 64
NH = 4
EPS = 1e-6
WCOLS = 3584
SAME_ENG_SYNC = True


class Prog:
    ENG = ("pe", "act", "dve", "pool", "sp")

    def __init__(self, nc):
        self.nc = nc
        self.ops = {e: [] for e in self.ENG}
        self.lastw = {}
        self.readers = {}
        self.dma_count = {}
        self.dma_since = {}

    def op(self, eng, fn, reads=(), writes=(), dma=None):
        o = dict(eng=eng, fn=fn, deps=[], dma=dma, inc=False)
        if dma is not None:
            self.dma_count[dma] = self.dma_count.get(dma, 0) + 1
            o["dma_val"] = 16 * self.dma_count[dma]
            self.dma_since[dma] = o
        for k in reads:
            w = self.lastw.get(k)
            if w is not None:
                self._dep(o, w)
        for k in writes:
            w = self.lastw.get(k)
            if w is not None:
                self._dep(o, w)
            for r in self.readers.get(k, {}).values():
                self._dep(o, r)
        for k in reads:
            rk = ("d", dma) if dma is not None else ("e", eng)
            self.readers.setdefault(k, {})[rk] = o
        for k in writes:
            self.lastw[k] = o
            self.readers[k] = {}
        self.ops[eng].append(o)
        return o

    def _dep(self, o, p):
        if p is o:
            return
        if p["dma"] is None and o["dma"] is None and p["eng"] == o["eng"]:
            if o["eng"] in ("pe", "sp") or not SAME_ENG_SYNC:
                return
        if p["dma"] is None:
            p["inc"] = True
        o["deps"].append(p)

    def barrier(self):
        last = {e: (self.ops[e][-1] if self.ops[e] else None) for e in self.ENG}
        dmas = list(self.dma_since.values())
        for e in self.ENG:
            o = dict(eng=e, fn=None, deps=[], dma=None, inc=False)
            for e2 in self.ENG:
                p = last[e2]
                if p is not None and e2 != e and e2 != "sp":
                    if p["fn"] is None:
                        continue
                    p["inc"] = True
                    o["deps"].append(p)
            for p in dmas:
                o["deps"].append(p)
            self.ops[e].append(o)
        self.lastw = {}
        self.readers = {}
        self.dma_since = {}

    def emit_all(self):
        nc = self.nc
        sems = {}
        for e in ("pe", "act", "dve", "pool"):
            sems[e] = nc.alloc_semaphore(name="prog_" + e)
        dsem = {}
        for k in self.dma_count:
            dsem[k] = nc.alloc_semaphore(name="dma_" + str(k))
        for e in ("pe", "act", "dve", "pool"):
            c = 0
            for o in self.ops[e]:
                if o["inc"] and o["fn"] is not None and o["dma"] is None:
                    c += 1
                    o["cnt"] = c
        ops = self.ops

        def emit(ename, eng):
            waited = {}
            for o in ops[ename]:
                for p in o["deps"]:
                    if p["dma"] is not None:
                        s, v = dsem[p["dma"]], p["dma_val"]
                    else:
                        s, v = sems[p["eng"]], p["cnt"]
                    key = id(s)
                    if waited.get(key, 0) >= v:
                        continue
                    waited[key] = v
                    eng.wait_ge(s, v)
                if o["fn"] is None:
                    continue
                ins = o["fn"](eng)
                if o["dma"] is not None:
                    ins.then_inc(dsem[o["dma"]], 16)
                elif o["inc"]:
                    ins.then_inc(sems[ename], 1)

        with nc.Block() as block:
            @block.tensor
            def _(e):
                emit("pe", e)

            @block.scalar
            def _(e):
                emit("act", e)

            @block.vector
            def _(e):
                emit("dve", e)

            @block.gpsimd
            def _(e):
                emit("pool", e)

            @block.sync
            def _(e):
                emit("sp", e)


def _rows(t, nt):
    return 128 if t < nt - 1 else NMETA


def fr(ap, dims):
    return bass.AP(ap.tensor, ap.offset, [list(ap.ap[0])] + [list(d) for d in dims])


def fro(ap, off, dims):
    return bass.AP(ap.tensor, ap.offset + off, [list(ap.ap[0])] + [list(d) for d in dims])


def build_nc(SEQ, DEPTH, debug=None):
    NT = SEQ // 128 + 1
    LP = NT * 128
    nc = bass.Bass("TRN2", target_bir_lowering=False)
    P = Prog(nc)

    def din(name, shape, dt=F32):
        return nc.dram_tensor(name, list(shape), dt, kind="ExternalInput").ap()

    def dscr(name, shape, dt):
        return nc.dram_tensor(name, list(shape), dt, kind="ExternalOutput" if debug else "Internal").ap()

    def finish():
        if debug:
            print("ops:", {e: len(v) for e, v in P.ops.items()}, "dma sems:", len(P.dma_count))
        P.emit_all()
        return nc

    L = SEQ + NMETA
    N2 = L // 16
    NK2 = (N2 + 127) // 128
    k2rows = [min(128, N2 - 128 * i) for i in range(NK2)]
    x_d = din("x", [L, D])
    ng_d = din("ng", [DEPTH, 128, 8])
    win_d = din("w_in", [DEPTH, D, 3072])
    wf_d = din("w_f", [DEPTH, 4, 128, 128])
    gq_d = din("gq", [DEPTH, HD])
    gk_d = din("gk", [DEPTH, HD])
    lam_d = din("lam4", [4, DEPTH, HD])
    sl_d = din("subln", [DEPTH, 128])
    wout_d = din("w_out", [DEPTH, D, D])
    rope_d = din("rope", [128, NT, 32])
    chan_d = din("chan", [128, 256])
    bd_d = din("bd", [128, 3, 128], BF16)
    t2_d = din("t2", [128, NK2, 2, N2 + 1], BF16)
    linit_d = din("linit", [1, 2 * DEPTH])
    y_d = nc.dram_tensor("y", [L, D], F32, kind="ExternalOutput").ap()

    Wp_d = dscr("Wp", [DEPTH, 128, 8, WCOLS], BF16)
    Wo_d = dscr("Wo", [DEPTH, 128, 8, D], BF16)
    res_d = dscr("res", [LP, D], F32)
    P_d = dscr("Pd", [LP, D], BF16)
    G_d = dscr("Gd", [LP, D], BF16)
    Y_d = dscr("Yd", [LP, D], BF16)

    base0 = (nc.sbuf_base + 63) // 64 * 64
    top = nc.sbuf_top
    cur = [base0]
    cnt = [0]

    def alloc(shape, dt):
        nbytes = int(np.prod(shape)) * (4 if dt == F32 else 2)
        off = cur[0]
        cur[0] = (off + nbytes + 63) // 64 * 64
        assert cur[0] <= top, ("SBUF overflow", cur[0], top)
        cnt[0] += 1
        return nc.alloc_sbuf_tensor_at("t%d" % cnt[0], [128] + list(shape), dt, offset=off)

    psall = nc.alloc_psum_tensor("psall", [128, 4096], F32)
    ps = [psall[:, i * 512:(i + 1) * 512] for i in range(8)]

    def psb(i):
        return ps[i].bitcast(BF16)

    ident_b = alloc([128], BF16)
    gq_b = alloc([DEPTH, HD], F32)
    gk_b = alloc([DEPTH, HD], F32)
    slg_b = alloc([DEPTH, 128], F32)
    lsum = alloc([2 * DEPTH], F32)
    linit = alloc([2 * DEPTH], F32)
    neglam = alloc([DEPTH], F32)
    nshift = alloc([DEPTH], F32)
    gmax = alloc([2 * DEPTH], F32)
    rope = alloc([NT, 32], F32)
    bd = alloc([3, 128], BF16)
    mhalf = alloc([8], F32)
    const_end = cur[0]
    ident_f = alloc([128], F32)
    chan = alloc([256], F32)
    identsrc = alloc([128], F32)
    lamv = alloc([4, DEPTH, HD], F32)
    lprod = alloc([2, DEPTH, HD], F32)
    p0_start = cur[0]

    def bcast_rows(ap2d):
        return bass.AP(ap2d.tensor, ap2d.offset, [[0, 128]] + [list(d) for d in ap2d.ap[1:]])

    def mk_ident(e):
        return e.memset(identsrc[:, :], 0.0)
    P.op("pool", mk_ident, writes=["identsrc"])
    P.op("pool", lambda e: e.iota(identsrc[:, :], [[1, 128]], channel_multiplier=-1,
                                  allow_small_or_imprecise_dtypes=True),
         writes=["identsrc"])
    P.op("dve", lambda e: e.tensor_single_scalar(ident_f[:, :], identsrc[:, :], 0.0, ALU.is_equal),
         reads=["identsrc"], writes=["ident_f"])
    P.op("dve", lambda e: e.tensor_copy(ident_b[:, :], ident_f[:, :]), reads=["ident_f"], writes=["ident_b"])

    P.op("pool", lambda e: e.memset(mhalf[:, :], -0.5), writes=["mhalf"])

    def cload(dst, src, key, q="sp"):
        P.op(q, lambda e: e.dma_start(out=dst, in_=src), writes=[key], dma="c_" + key)

    cload(gq_b[:, :, :], bcast_rows(gq_d.rearrange("(o l) d -> o l d", o=1)), "gq_b")
    cload(gk_b[:, :, :], bcast_rows(gk_d.rearrange("(o l) d -> o l d", o=1)), "gk_b")
    cload(slg_b[:, :, :], bcast_rows(sl_d.rearrange("(o l) d -> o l d", o=1)), "slg_b")
    cload(lamv[:, :, :, :], bcast_rows(lam_d.rearrange("(o f) l d -> o f l d", o=1)), "lamv")
    cload(linit[:, :], bcast_rows(linit_d), "linit")
    cload(rope[:, :, :], rope_d, "rope")
    cload(chan[:, :], chan_d, "chan")
    cload(bd[:, :, :], bd_d, "bd")

    P.op("dve", lambda e: e.tensor_tensor(out=lprod[:, 0, :, :], in0=lamv[:, 0, :, :], in1=lamv[:, 1, :, :], op=ALU.mult),
         reads=["lamv"], writes=["lprod0"])
    P.op("dve", lambda e: e.tensor_tensor(out=lprod[:, 1, :, :], in0=lamv[:, 2, :, :], in1=lamv[:, 3, :, :], op=ALU.mult),
         reads=["lamv"], writes=["lprod1"])
    P.op("dve", lambda e: e.tensor_reduce(out=lsum[:, :], in_=lprod[:, :, :, :].rearrange("p a l d -> p (a l) d"),
                                          axis=AX.X, op=ALU.add),
         reads=["lprod0", "lprod1"], writes=["lsum"])
    P.op("act", lambda e: e.activation(out=lsum[:, :], in_=lsum[:, :], func=AF.Exp), reads=["lsum"], writes=["lsum"])
    P.op("dve", lambda e: e.tensor_tensor(out=neglam[:, :], in0=lsum[:, DEPTH:2 * DEPTH], in1=lsum[:, 0:DEPTH], op=ALU.subtract),
         reads=["lsum"], writes=["neglam"])
    P.op("dve", lambda e: e.tensor_tensor(out=neglam[:, :], in0=neglam[:, :], in1=linit[:, 0:DEPTH], op=ALU.subtract),
         reads=["neglam", "linit"], writes=["neglam"])
    P.op("dve", lambda e: e.tensor_reduce(out=gmax[:, 0:DEPTH], in_=gq_b[:, :, :], axis=AX.X, op=ALU.max,
                                          apply_absolute_value=True),
         reads=["gq_b"], writes=["gmaxq"])
    P.op("dve", lambda e: e.tensor_reduce(out=gmax[:, DEPTH:2 * DEPTH], in_=gk_b[:, :, :], axis=AX.X, op=ALU.max,
                                          apply_absolute_value=True),
         reads=["gk_b"], writes=["gmaxk"])
    P.op("dve", lambda e: e.scalar_tensor_tensor(out=nshift[:, :], in0=gmax[:, 0:DEPTH], scalar=-8.0,
                                                 in1=gmax[:, DEPTH:2 * DEPTH], op0=ALU.mult, op1=ALU.mult),
         reads=["gmaxq", "gmaxk"], writes=["nshift"])
    P.op("dve", lambda e: e.tensor_scalar(out=gq_b[:, :, :], in0=gq_b[:, :, :], scalar1=HD ** -0.5, scalar2=None, op0=ALU.mult),
         reads=["gq_b", "gmaxq"], writes=["gq_b"])
    for l in range(DEPTH):
        P.op("dve", lambda e, l=l: e.tensor_scalar(out=slg_b[:, l, :], in0=slg_b[:, l, :],
                                                   scalar1=linit[:, DEPTH + l:DEPTH + l + 1], scalar2=None, op0=ALU.mult),
             reads=["slg_b", "linit"], writes=["slg_b"])

    cur[0] = const_end
    qT = alloc([NH, LP], BF16)
    kT = alloc([NH, LP], BF16)
    vall = alloc([NT, NH, 130], BF16)
    qkv_end = cur[0]
    Wsb = alloc([8, WCOLS], BF16)
    w_end = cur[0]

    cur[0] = p0_start
    ngt = alloc([DEPTH, 8], F32)
    wst = [alloc([3072], F32) for _ in range(2)]
    wpb = [alloc([2560], BF16) for _ in range(2)]
    wcs = [alloc([1024], BF16) for _ in range(2)]
    wfT = [alloc([128], F32) for _ in range(2)]
    MM = alloc([4, 256], F32)
    wfl = alloc([4, 128], F32)
    wos = [alloc([1024], F32) for _ in range(2)]
    wob = [alloc([1024], BF16) for _ in range(2)]
    early_wsb = cur[0] <= qkv_end
    P.op("sp", lambda e: e.dma_start(out=ngt[:, :, :], in_=ng_d.rearrange("l p k -> p l k")), writes=["ngt"], dma="ngt")
    def loads0(l, kt):
        s = kt % 2
        P.op("sp", lambda e, l=l, kt=kt, s=s: e.dma_start(out=wst[s][:, :], in_=win_d[l, kt * 128:(kt + 1) * 128, :]),
             writes=[("wst", s)], dma="wst%d" % s)
        P.op("pool", lambda e, l=l, kt=kt, s=s: e.dma_start(out=wos[s][:, :], in_=wout_d[l, kt * 128:(kt + 1) * 128, :]),
             writes=[("wos", s)], dma="wos%d" % s)

    for l in range(DEPTH):
        P.op("sp", lambda e, l=l: e.dma_start(out=wfl[:, :, :], in_=wf_d[l].rearrange("g c e -> c g e")),
             writes=["wfl"], dma="wfl")
        for g in range(4):
            for cs in range(2):
                P.op("pe", lambda e, g=g, cs=cs: e.matmul(ps[g][:, cs * 128:(cs + 1) * 128], lhsT=chan[:, cs * 128:(cs + 1) * 128],
                                                          rhs=wfl[:, g, :], start=True, stop=True),
                     reads=["chan", "wfl"], writes=[("ps", g, cs)])
            P.op("dve", lambda e, g=g: e.tensor_copy(MM[:, g, :], ps[g][:, 0:256]),
                 reads=[("ps", g, 0), ("ps", g, 1)], writes=[("MM", g)])
        if early_wsb and l == min(1, DEPTH - 1):
            P.op("pool", lambda e: e.dma_start(out=Wsb[:, :, :], in_=Wp_d[0]), reads=[("Wp", 0)], writes=["Wsb"], dma="Wsb")
        for kt in range(8):
            s = kt % 2
            if l == 0 and kt == 0:
                loads0(0, 0)
            nxt = l * 8 + kt + 1
            if nxt < DEPTH * 8:
                loads0(nxt // 8, nxt % 8)
            sc = ngt[:, l, kt:kt + 1]
            P.op("act", lambda e, s=s, sc=sc: e.activation(out=wpb[s][:, 0:1280], in_=wst[s][:, 512:1792], func=AF.Copy, scale=sc),
                 reads=[("wst", s), "ngt"], writes=[("wpb", s, 0)])
            P.op("dve", lambda e, s=s, sc=sc: e.tensor_scalar(out=wpb[s][:, 1280:2560], in0=wst[s][:, 1792:3072], scalar1=sc,
                                                              scalar2=None, op0=ALU.mult),
                 reads=[("wst", s), "ngt"], writes=[("wpb", s, 1)])
            P.op("sp", lambda e, l=l, kt=kt, s=s: e.dma_start(out=Wp_d[l, :, kt, 1024:WCOLS], in_=wpb[s][:, :]),
                 reads=[("wpb", s, 0), ("wpb", s, 1)], writes=[("Wp", l)], dma="wpbst%d" % s)
            P.op("dve", lambda e, s=s, sc=sc: e.tensor_scalar(out=wst[s][:, 0:512], in0=wst[s][:, 0:512], scalar1=sc,
                                                              scalar2=None, op0=ALU.mult),
                 reads=[("wst", s), "ngt"], writes=[("wst", s)])
            def tr0(g, s=s):
                s2 = g % 2
                P.op("pe", lambda e, s=s, g=g, s2=s2: e.transpose(out=ps[4 + s2][:, 0:128], in_=wst[s][:, g * 128:(g + 1) * 128],
                                                                 identity=ident_f[:, :]),
                     reads=[("wst", s), "ident_f"], writes=[("psT", s2)])
                P.op("act", lambda e, s2=s2: e.activation(out=wfT[s2][:, :], in_=ps[4 + s2][:, 0:128], func=AF.Copy),
                     reads=[("psT", s2)], writes=[("wfT", s2)])

            tr0(0)
            for g in range(4):
                s2 = g % 2
                if g + 1 < 4:
                    tr0(g + 1)
                P.op("pe", lambda e, s2=s2, g=g: e.matmul(ps[6 + s2][:, 0:256], lhsT=wfT[s2][:, :], rhs=MM[:, g, :],
                                                          start=True, stop=True),
                     reads=[("wfT", s2), ("MM", g)], writes=[("psM", s2)])
                P.op("dve", lambda e, s=s, s2=s2, g=g: e.tensor_copy(
                    fro(wcs[s][:, :], g * 128, [[512, 2], [1, 128]]),
                    ps[6 + s2][:, 0:256].rearrange("p (a b) -> p a b", a=2)),
                     reads=[("psM", s2)], writes=[("wcs", s, g)])
            P.op("sp", lambda e, l=l, kt=kt, s=s: e.dma_start(out=Wp_d[l, :, kt, 0:1024], in_=wcs[s][:, :]),
                 reads=[("wcs", s, g) for g in range(4)], writes=[("Wp", l)], dma="wcsst%d" % s)
            P.op("act", lambda e, s=s: e.activation(out=wob[s][:, :], in_=wos[s][:, :], func=AF.Copy),
                 reads=[("wos", s)], writes=[("wob", s)])
            P.op("sp", lambda e, l=l, kt=kt, s=s: e.dma_start(out=Wo_d[l, :, kt, :], in_=wob[s][:, :]),
                 reads=[("wob", s)], writes=[("Wo", l)], dma="wobst%d" % s)
    P.barrier()
    if debug == "p0":
        return finish()


    cur[0] = w_end
    xt = [alloc([D], F32) for _ in range(2)]
    hb = [alloc([D], BF16) for _ in range(2)]
    hT = [alloc([8, 128], BF16) for _ in range(2)]
    pst = [alloc([D], BF16) for _ in range(2)]
    hst = [alloc([D], BF16) for _ in range(2)]
    sg = [alloc([D], BF16) for _ in range(2)]
    tmp1 = [alloc([512], F32) for _ in range(2)]
    tmp3 = [alloc([512], F32) for _ in range(2)]
    qb = [alloc([512], BF16) for _ in range(2)]
    ssx = [alloc([4], F32) for _ in range(2)]
    ss8 = [alloc([8], F32) for _ in range(2)]
    tA = [alloc([128], F32) for _ in range(2)]
    tB = [alloc([128], F32) for _ in range(2)]

    cur[0] = w_end
    NPT = 3
    PT = [alloc([1024], BF16) for _ in range(NPT)]
    gl = [alloc([128], BF16) for _ in range(4)]
    rz = [alloc([4], F32) for _ in range(4)]
    o1 = [alloc([128], F32) for _ in range(4)]
    o2 = [alloc([128], F32) for _ in range(4)]
    o3 = [alloc([128], F32) for _ in range(4)]
    ya = [alloc([128], BF16) for _ in range(4)]
    accS = [alloc([258], F32) for _ in range(4)]
    p2_end = cur[0]
    cur[0] = qkv_end
    HallQ = [alloc([NK2, 4, D], BF16) for _ in range(2)]
    assert cur[0] <= w_end
    cur[0] = p2_end
    Wo = alloc([8, D], BF16)
    wo_end = cur[0]
    NCS = 3
    need3 = 2 * NK2 * 4 * D * 2 + (NCS + 2) * 1024 + NK2 * 2 * (N2 + 1) * 2 + 1024
    cur[0] = const_end if const_end + need3 <= qkv_end else wo_end
    HallQ += [alloc([NK2, 4, D], BF16) for _ in range(2)]
    gt = [alloc([512], BF16) for _ in range(NCS)]
    yf = [alloc([512], BF16) for _ in range(2)]
    T2 = alloc([NK2, 2, N2 + 1], BF16)
    p3_end = cur[0]
    need4 = 2 * (2 * D + 2 * D + 4 * D + 4 * D) + 512
    cur[0] = const_end if const_end + need4 <= qkv_end else p3_end
    yt = [alloc([D], BF16) for _ in range(2)]
    yT = [alloc([8, 128], BF16) for _ in range(2)]
    xr = [alloc([D], F32) for _ in range(2)]
    ot = [alloc([D], F32) for _ in range(2)]
    for l in range(DEPTH):
        last = (l == DEPTH - 1)

        def res_src(t, r, l=l):
            if l == 0:
                return x_d[t * 128:t * 128 + r, :]
            return res_d[t * 128:t * 128 + r, :]

        if l == 0 and not early_wsb:
            P.op("sp", lambda e: e.dma_start(out=Wsb[:, :, :], in_=Wp_d[0]), reads=[("Wp", 0)], writes=["Wsb"], dma="Wsb")
        P.op("pool", lambda e: e.memset(vall[:, :, :, 128:130], 1.0), writes=["vones"])

        deferred = []
        deferredC = []
        deferredCP = []

        def flushC():
            for f in deferredC:
                f()
            del deferredC[:]

        def flush_qkT():
            flushC()
            for (r_, u_, t_, dstT_) in deferred:
                for h in range(NH):
                    P.op("pe", lambda e, r=r_, u=u_, h=h: e.transpose(out=psb(6)[:, u * 512 + h * 128:u * 512 + h * 128 + r],
                                                                    in_=qb[u][0:r, h * 128:(h + 1) * 128],
                                                                    identity=ident_b[0:r, 0:r]),
                         reads=[("qb", u_, 0), ("qb", u_, 1), "ident_b"], writes=[("ps", 6)])
                P.op("dve", lambda e, r=r_, u=u_, t=t_, dstT=dstT_: e.tensor_copy(
                    dstT[:, :, t * 128:t * 128 + r],
                    psb(6)[:, u * 512:(u + 1) * 512].rearrange("p (h m) -> p h m", h=NH)[:, :, 0:r]),
                     reads=[("ps", 6)], writes=[("qkT", u_, t_)])
            del deferred[:]

        def head_a(t):
            r = _rows(t, NT)
            s = t % 2
            P.op("sp", lambda e, r=r, s=s, src=res_src(t, r): e.dma_start(out=xt[s][0:r, :], in_=src),
                 reads=[("res", t)], writes=[("xt", s)], dma="xt%d" % s)
            P.op("act", lambda e, r=r, s=s: e.activation(out=hb[s][0:r, :], in_=xt[s][0:r, :], func=AF.Square,
                                                         accum_out=ssx[s][0:r, 0:1]),
                 reads=[("xt", s)], writes=[("hb", s), ("ssx", s)])
            P.op("dve", lambda e, r=r, s=s: e.tensor_scalar(out=ssx[s][0:r, 1:2], in0=ssx[s][0:r, 0:1], scalar1=1.0 / D,
                                                            scalar2=EPS, op0=ALU.mult, op1=ALU.add),
                 reads=[("ssx", s)], writes=[("ssx1", s)])
            P.op("pool", lambda e, r=r, s=s: e.tensor_tensor(out=ssx[s][0:r, 2:3], in0=ssx[s][0:r, 1:2], in1=mhalf[0:r, 0:1], op=ALU.pow),
                 reads=[("ssx1", s)], writes=[("rstd", s)])
            P.op("act", lambda e, r=r, s=s: e.activation(out=hb[s][0:r, :], in_=xt[s][0:r, :], func=AF.Copy,
                                                         scale=ssx[s][0:r, 2:3]),
                 reads=[("xt", s), ("rstd", s)], writes=[("hb", s)])

        def head_b(t):
            r = _rows(t, NT)
            s = t % 2
            for k in range(8):
                P.op("pe", lambda e, r=r, s=s, k=k: e.transpose(out=psb(7)[:, k * 128:k * 128 + r],
                                                                in_=hb[s][0:r, k * 128:(k + 1) * 128],
                                                                identity=ident_b[0:r, 0:r]),
                     reads=[("hb", s), "ident_b"], writes=[("ps", 7)])
            P.op("dve", lambda e, r=r, s=s: e.tensor_copy(hT[s][:, :, 0:r],
                                                          psb(7)[:, :].rearrange("p (k m) -> p k m", k=8)[:, :, 0:r]),
                 reads=[("ps", 7)], writes=[("hT", s)])


        head_a(0)
        head_b(0)
        for t in range(NT):
            r = _rows(t, NT)
            s = t % 2
            if t + 1 < NT:
                head_a(t + 1)
            flushC()

            def proj(c, bank, r=r, s=s):
                for k in range(8):
                    P.op("pe", lambda e, k=k: e.matmul(ps[bank][0:r, :], lhsT=hT[s][:, k, 0:r],
                                                       rhs=Wsb[:, k, c * 512:(c + 1) * 512], start=(k == 0), stop=(k == 7)),
                         reads=[("hT", s), "Wsb"], writes=[("ps", bank)])

            proj(0, 0)
            proj(1, 1)
            P.op("act", lambda e, r=r, s=s: e.activation(out=pst[s][0:r, 0:512], in_=ps[0][0:r, :], func=AF.Copy),
                 reads=[("ps", 0)], writes=[("pst", s, 0)])
            P.op("dve", lambda e, r=r, s=s: e.tensor_copy(pst[s][0:r, 512:1024], ps[1][0:r, :]),
                 reads=[("ps", 1)], writes=[("pst", s, 1)])
            if t + 1 < NT:
                head_b(t + 1)
            flush_qkT()
            for qi, (c, bank, gb, dstT) in enumerate(((2, 2, gq_b, qT), (3, 3, gk_b, kT))):
                proj(c, bank)
                u = qi
                P.op("act", lambda e, r=r, bank=bank, u=u: e.activation(out=tmp1[u][0:r, :], in_=ps[bank][0:r, :], func=AF.Square),
                     reads=[("ps", bank)], writes=[("tmp1", u)])
                P.op("dve", lambda e, r=r, u=u: e.tensor_reduce(out=ss8[u][0:r, :],
                                                                in_=tmp1[u][0:r, :].rearrange("p (b d) -> p b d", b=8),
                                                                axis=AX.X, op=ALU.add),
                     reads=[("tmp1", u)], writes=[("ss8", u)])
                P.op("dve", lambda e, r=r, u=u: e.tensor_scalar(out=ss8[u][0:r, :], in0=ss8[u][0:r, :], scalar1=1.0 / HD,
                                                                scalar2=EPS, op0=ALU.mult, op1=ALU.add),
                     reads=[("ss8", u)], writes=[("ss8", u)])
                P.op("pool", lambda e, r=r, u=u: e.tensor_tensor(out=ss8[u][0:r, :], in0=ss8[u][0:r, :], in1=mhalf[0:r, 0:8], op=ALU.pow),
                     reads=[("ss8", u)], writes=[("ss8", u)])
                P.op("dve", lambda e, r=r, u=u, bank=bank: e.tensor_tensor(
                    out=tmp1[u][0:r, :].rearrange("p (b d) -> p b d", b=8),
                    in0=ps[bank][0:r, :].rearrange("p (b d) -> p b d", b=8),
                    in1=fr(ss8[u][0:r, :], [[1, 8], [0, HD]]), op=ALU.mult),
                     reads=[("ps", bank), ("ss8", u)], writes=[("tmp1", u)])
                def cpart(r=r, u=u, t=t, gb=gb):
                    gsl = gb[0:r, l, :]
                    P.op("pool", lambda e, r=r, u=u, gsl=gsl: e.tensor_tensor(
                        out=tmp3[u][0:r, :].rearrange("p (b d) -> p b d", b=8),
                        in0=tmp1[u][0:r, :].rearrange("p (b d) -> p b d", b=8),
                        in1=fr(gsl, [[0, 8], [1, HD]]), op=ALU.mult),
                         reads=[("tmp1", u), "gq_b", "gk_b"], writes=[("tmp3", u)])
                    t3 = tmp3[u][0:r, :]
                    rp = rope[0:r, t, :]
                    P.op("pool", lambda e, r=r, u=u, t3=t3, rp=rp: e.tensor_tensor(
                        out=tA[u][0:r, :].rearrange("p (b d) -> p b d", b=8),
                        in0=fr(t3, [[HD, 8], [1, 16]]), in1=fr(rp, [[0, 8], [1, 16]]), op=ALU.mult),
                         reads=[("tmp3", u), "rope"], writes=[("tA", u)])
                    P.op("pool", lambda e, r=r, u=u, t3=t3, rp=rp: e.tensor_tensor(
                        out=fr(tB[u][0:r, :], [[16, 8], [1, 8]]),
                        in0=fro(t3, 8, [[HD, 8], [1, 8]]),
                        in1=fro(rp, 16, [[0, 8], [1, 8]]), op=ALU.mult),
                         reads=[("tmp3", u), "rope"], writes=[("tB", u, 0)])
                    P.op("pool", lambda e, r=r, u=u, t3=t3, rp=rp: e.tensor_tensor(
                        out=fro(tB[u][0:r, :], 8, [[16, 8], [1, 8]]),
                        in0=fr(t3, [[HD, 8], [1, 8]]),
                        in1=fro(rp, 24, [[0, 8], [1, 8]]), op=ALU.mult),
                         reads=[("tmp3", u), "rope"], writes=[("tB", u, 1)])
                    P.op("pool", lambda e, r=r, u=u: e.tensor_tensor(
                        out=fr(qb[u][0:r, :], [[HD, 8], [1, 16]]),
                        in0=tA[u][0:r, :].rearrange("p (b d) -> p b d", b=8),
                        in1=tB[u][0:r, :].rearrange("p (b d) -> p b d", b=8), op=ALU.add),
                         reads=[("tA", u), ("tB", u, 0), ("tB", u, 1)], writes=[("qb", u, 0)])
                deferredCP.append(cpart)

                def cpart_act(r=r, u=u):
                    t3 = tmp3[u][0:r, :]
                    P.op("act", lambda e, r=r, u=u, t3=t3: e.activation(
                        out=fro(qb[u][0:r, :], 16, [[HD, 8], [1, HD - 16]]), in_=fro(t3, 16, [[HD, 8], [1, HD - 16]]), func=AF.Copy),
                         reads=[("tmp3", u)], writes=[("qb", u, 1)])
                deferredC.append(cpart_act)
                deferred.append((r, u, t, dstT))
            proj(4, 4)
            for (bank, (m0, m1)) in ((0, (0, 2)), (1, (1, 0))):
                P.op("pe", lambda e, r=r, s=s, bank=bank, m0=m0: e.matmul(ps[bank][0:r, :], lhsT=bd[0:r, m0, 0:r], rhs=pst[s][0:r, 0:512],
                                                                         start=True, stop=False),
                     reads=[("pst", s, 0), "bd"], writes=[("ps", bank)])
                P.op("pe", lambda e, r=r, s=s, bank=bank, m1=m1: e.matmul(ps[bank][0:r, :], lhsT=bd[0:r, m1, 0:r], rhs=pst[s][0:r, 512:1024],
                                                                         start=False, stop=True),
                     reads=[("pst", s, 1), "bd"], writes=[("ps", bank)])
            P.op("dve", lambda e, r=r, s=s: e.tensor_copy(hst[s][0:r, 0:512], ps[0][0:r, :]),
                 reads=[("ps", 0)], writes=[("hst", s, 0)])
            P.op("dve", lambda e, r=r, s=s: e.tensor_copy(hst[s][0:r, 512:1024], ps[1][0:r, :]),
                 reads=[("ps", 1)], writes=[("hst", s, 1)])
            P.op("act", lambda e, r=r, t=t: e.activation(out=vall[0:r, t, :, 0:128],
                                                         in_=ps[4][0:r, :].rearrange("p (h d) -> p h d", h=NH), func=AF.Copy),
                 reads=[("ps", 4)], writes=[("v", t)])
            proj(5, 5)
            proj(6, 4)
            P.op("act", lambda e, r=r, s=s: e.activation(out=sg[s][0:r, 0:512], in_=ps[5][0:r, :], func=AF.Silu),
                 reads=[("ps", 5)], writes=[("sg", s, 0)])
            P.op("act", lambda e, r=r, s=s: e.activation(out=sg[s][0:r, 512:1024], in_=ps[4][0:r, :], func=AF.Silu),
                 reads=[("ps", 4)], writes=[("sg", s, 1)])
            P.op("act", lambda e, t=t, r=r, s=s: e.dma_start(out=G_d[t * 128:t * 128 + r, :], in_=sg[s][0:r, :]),
                 reads=[("sg", s, 0), ("sg", s, 1)], writes=[("Gd", t)], dma="sg%d" % s)
            P.op("act", lambda e, t=t, r=r, s=s: e.dma_start(out=P_d[t * 128:t * 128 + r, :], in_=hst[s][0:r, :]),
                 reads=[("hst", s, 0), ("hst", s, 1)], writes=[("Pd", t)], dma="hst%d" % s)
            for f in deferredCP:
                f()
            del deferredCP[:]
        flush_qkT()
        P.barrier()
        if debug == "p1":
            return finish()

        def hall_loads(quarters):
            for qd in quarters:
                for kt in range(NK2):
                    kr = k2rows[kt]
                    src = P_d[kt * 2048:kt * 2048 + kr * 16, :].rearrange("(p a) c -> p a c", a=16)[:, qd * 4:qd * 4 + 4, :]
                    P.op("sp" if qd % 2 == 0 else "pool", lambda e, kt=kt, kr=kr, qd=qd, src=src: e.dma_start(
                        out=HallQ[qd][0:kr, kt, :, :], in_=src),
                         reads=[("Pd", t) for t in range(NT)], writes=[("Hall", kt, qd)], dma="Hall%d_%d" % (kt, qd))

        hall_loads((0, 1))
        P.op("sp", lambda e, l=l: e.dma_start(out=Wo[:, :, :], in_=Wo_d[l]), reads=[("Wo", l)], writes=["Wo"], dma="WoL")

        nch = (L + 511) // 512
        bounds = [(L * ci) // nch for ci in range(nch + 1)]
        chunks = []
        for ci in range(nch):
            q0c, qnc = bounds[ci], bounds[ci + 1] - bounds[ci]
            chunks.append((q0c, qnc, [(qo, min(128, qnc - qo)) for qo in range(0, qnc, 128)]))
        steps = [(h, ch, kt) for h in range(NH) for ch in chunks for kt in range(NT)]
        pp = [0]

        def qk_exp(i):
            h, (q0, qn, qtiles), kt = steps[i]
            kr = _rows(kt, NT)
            sl = i % NPT
            sb = (i % 2) * 2
            qtok = list(range(q0 // 128, (q0 + qn - 1) // 128 + 1))
            for c in range(2):
                P.op("pe", lambda e, c=c: e.matmul(
                    ps[sb + c][0:kr, 0:qn], lhsT=kT[c * 64:(c + 1) * 64, h, kt * 128:kt * 128 + kr],
                    rhs=qT[c * 64:(c + 1) * 64, h, q0:q0 + qn], start=True, stop=True),
                     reads=[("qkT", 1, kt)] + [("qkT", 0, t) for t in qtok], writes=[("ps", sb + c)])
            P.op("act", lambda e, l=l: e.activation(
                out=fr(PT[sl][0:kr, :], [[512, 2], [1, qn]]), in_=fr(ps[sb][0:kr, :], [[512, 2], [1, qn]]), func=AF.Exp,
                bias=nshift[0:kr, l:l + 1], scale=1.0),
                 reads=[("ps", sb), ("ps", sb + 1), "nshift"], writes=[("PT", sl)])

        def av(i):
            h, (q0, qn, qtiles), kt = steps[i]
            kr = _rows(kt, NT)
            sl = i % NPT
            for j, (qo, qr) in enumerate(qtiles):
                for c in range(2):
                    P.op("pe", lambda e, c=c, j=j, qr=qr, qo=qo: e.matmul(
                        ps[4 + j][0:qr, c * 129:(c + 1) * 129], lhsT=PT[sl][0:kr, c * 512 + qo:c * 512 + qo + qr],
                        rhs=vall[0:kr, kt, h, 0:129], start=(kt == 0 and c == 0), stop=(kt == NT - 1 and c == 1),
                        skip_group_check=True),
                         reads=[("PT", sl), ("v", kt), "vones"], writes=[("acc", j)])

        def post(i):
            h, (q0, qn, qtiles), kt = steps[i]
            info = []
            for j, (qo, qr) in enumerate(qtiles):
                tq = q0 + qo
                w = pp[0] % 4
                pp[0] += 1
                info.append((j, tq, qr, w))
                P.op("dve", lambda e, qr=qr, j=j, w=w: e.tensor_copy(accS[w][0:qr, :], ps[4 + j][0:qr, 0:258]),
                     reads=[("acc", j)], writes=[("accS", w)])
            for (j, tq, qr, w) in info:
                acc = accS[w]
                P.op("sp", lambda e, tq=tq, qr=qr, w=w: e.dma_start(
                    out=gl[w][0:qr, :], in_=G_d[tq:tq + qr, 512 + h * 128:512 + (h + 1) * 128]),
                     reads=[("Gd", t_) for t_ in range(NT)], writes=[("gl", w)], dma="gl%d" % w)
                P.op("dve", lambda e, qr=qr, acc=acc, w=w: e.reciprocal(
                    out=rz[w][0:qr, 0:2], in_=fr(acc[0:qr, 128:129], [[129, 2]])),
                     reads=[("accS", w)], writes=[("rz", w)])
                P.op("dve", lambda e, qr=qr, w=w, l=l: e.tensor_tensor(out=rz[w][0:qr, 2:3], in0=rz[w][0:qr, 1:2],
                                                                       in1=neglam[0:qr, l:l + 1], op=ALU.mult),
                     reads=[("rz", w), "neglam"], writes=[("rz2", w)])
                P.op("dve", lambda e, qr=qr, acc=acc, w=w: e.tensor_scalar(
                    out=o1[w][0:qr, :], in0=acc[0:qr, 0:128], scalar1=rz[w][0:qr, 0:1], scalar2=None, op0=ALU.mult),
                     reads=[("accS", w), ("rz", w)], writes=[("o1", w)])
                P.op("dve", lambda e, qr=qr, acc=acc, w=w: e.scalar_tensor_tensor(
                    out=o2[w][0:qr, :], in0=acc[0:qr, 129:257], scalar=rz[w][0:qr, 2:3], in1=o1[w][0:qr, :],
                    op0=ALU.mult, op1=ALU.add),
                     reads=[("accS", w), ("rz2", w), ("o1", w)], writes=[("o2", w)])
            for (j, tq, qr, w) in info:
                P.op("pool", lambda e, qr=qr, w=w: e.tensor_tensor(out=o3[w][0:qr, :], in0=o2[w][0:qr, :], in1=o2[w][0:qr, :],
                                                                   op=ALU.mult),
                     reads=[("o2", w)], writes=[("o3", w)])
                P.op("dve", lambda e, qr=qr, w=w: e.tensor_reduce(out=rz[w][0:qr, 3:4], in_=o3[w][0:qr, :], axis=AX.X, op=ALU.add),
                     reads=[("o3", w)], writes=[("rz3", w)])
                P.op("dve", lambda e, qr=qr, w=w: e.tensor_scalar(out=rz[w][0:qr, 3:4], in0=rz[w][0:qr, 3:4], scalar1=1.0 / 128,
                                                                  scalar2=EPS, op0=ALU.mult, op1=ALU.add),
                     reads=[("rz3", w)], writes=[("rz3", w)])
                P.op("pool", lambda e, qr=qr, w=w: e.tensor_tensor(out=rz[w][0:qr, 3:4], in0=rz[w][0:qr, 3:4], in1=mhalf[0:qr, 0:1], op=ALU.pow),
                     reads=[("rz3", w)], writes=[("rz3", w)])
                P.op("dve", lambda e, qr=qr, w=w, l=l: e.scalar_tensor_tensor(
                    out=o3[w][0:qr, :], in0=o2[w][0:qr, :], scalar=rz[w][0:qr, 3:4], in1=slg_b[0:qr, l, :],
                    op0=ALU.mult, op1=ALU.mult),
                     reads=[("o2", w), ("rz3", w), "slg_b", ("o3", w)], writes=[("o3", w)])
                P.op("pool", lambda e, qr=qr, w=w: e.tensor_tensor(out=ya[w][0:qr, :], in0=o3[w][0:qr, :],
                                                                   in1=gl[w][0:qr, :], op=ALU.mult),
                     reads=[("o3", w), ("gl", w)], writes=[("ya", w)])
                P.op("sp", lambda e, tq=tq, qr=qr, w=w: e.dma_start(
                    out=Y_d[tq:tq + qr, 512 + h * 128:512 + (h + 1) * 128], in_=ya[w][0:qr, :]),
                     reads=[("ya", w)], writes=[("YdA", tq, h)], dma="ya%d" % w)

        qk_exp(0)
        if len(steps) > 1:
            qk_exp(1)
        for i in range(len(steps)):
            if i + 2 < len(steps):
                qk_exp(i + 2)
            av(i)
            if steps[i][2] == NT - 1:
                post(i)
        P.barrier()
        if debug == "p2":
            return finish()

        P.op("sp", lambda e: e.dma_start(out=T2[:, :, :, :], in_=t2_d), writes=["T2"], dma="T2")
        hall_loads((2, 3))
        groups = [(a, mb) for a in range(16) for mb in range(NK2)]

        def rows_ap(base, a, mb, r, c0, c1):
            t_ = base[(mb * 128) * 16 + a:(mb * 128) * 16 + a + 1, c0:c1]
            return bass.AP(t_.tensor, t_.offset, [[16 * D, r], [1, c1 - c0]])

        def loads3(gi):
            a, mb = groups[gi]
            r = k2rows[mb]
            s3 = gi % NCS
            P.op("sp", lambda e, r=r, s3=s3, src=rows_ap(G_d, a, mb, r, 0, 512): e.dma_start(out=gt[s3][0:r, :], in_=src),
                 reads=[("Gd", t) for t in range(NT)], writes=[("gt", s3)], dma="gt%d" % s3)

        loads3(0)
        loads3(1)
        for gi, (a, mb) in enumerate(groups):
            r = k2rows[mb]
            s3 = gi % NCS
            s = gi % 2
            if gi + 2 < len(groups):
                loads3(gi + 2)
            if not last:
                half = len(groups) // 2
                for k in range(8):
                    if half + k * (len(groups) - half) // 8 == gi:
                        P.op("pool", lambda e, l=l, k=k: e.dma_start(out=Wsb[:, k, :], in_=Wp_d[l + 1, :, k, :]),
                             reads=[("Wp", l + 1)],
                             writes=[("WsbK", k)] + [("Hall", kt_, qd_) for kt_ in range(NK2) for qd_ in (0, 1)],
                             dma="WsbK%d" % k)
            bank = gi % 4
            n = 0
            for kt in range(NK2):
                kr = k2rows[kt]
                for x in range(2):
                    P.op("pe", lambda e, kt=kt, kr=kr, x=x, r=r, mb=mb, a=a, bank=bank, n=n: e.matmul(
                        ps[bank][0:r, :], lhsT=T2[0:kr, kt, x, mb * 128:mb * 128 + r], rhs=HallQ[a // 4][0:kr, kt, a % 4, x * 512:(x + 1) * 512],
                        start=(n == 0), stop=(n == 2 * NK2 - 1)),
                         reads=[("Hall", kt, a // 4), "T2"], writes=[("ps", bank)])
                    n += 1
            P.op("dve", lambda e, r=r, s=s, s3=s3, bank=bank: e.tensor_tensor(out=yf[s][0:r, :], in0=ps[bank][0:r, :], in1=gt[s3][0:r, :],
                                                                              op=ALU.mult),
                 reads=[("ps", bank), ("gt", s3)], writes=[("yf", s)])
            P.op("sp", lambda e, r=r, s=s, dst=rows_ap(Y_d, a, mb, r, 0, 512): e.dma_start(out=dst, in_=yf[s][0:r, :]),
                 reads=[("yf", s)], writes=[("YdF", gi)], dma="yf%d" % s)
        P.barrier()
        if debug == "p3":
            return finish()

        ntile4 = NT
        def loads4(t):
            r = _rows(t, NT)
            s = t % 2
            P.op("sp", lambda e, t=t, r=r, s=s: e.dma_start(out=yt[s][0:r, :], in_=Y_d[t * 128:t * 128 + r, :]),
                 reads=[("YdF", gi) for gi in range(16 * NK2)], writes=[("yt", s)], dma="yt%d" % s)
            P.op("sp", lambda e, r=r, s=s, src=res_src(t, r): e.dma_start(out=xr[s][0:r, :], in_=src),
                 reads=[("res", t)], writes=[("xr", s)], dma="xr%d" % s)

        def trans4(t):
            r = _rows(t, NT)
            s = t % 2
            for k in range(8):
                P.op("pe", lambda e, r=r, s=s, k=k: e.transpose(out=psb(7)[:, k * 128:k * 128 + r],
                                                                in_=yt[s][0:r, k * 128:(k + 1) * 128],
                                                                identity=ident_b[0:r, 0:r]),
                     reads=[("yt", s), "ident_b"], writes=[("ps", 7)])
            P.op("act", lambda e, r=r, s=s: e.activation(out=yT[s][:, :, 0:r],
                                                         in_=psb(7)[:, :].rearrange("p (k m) -> p k m", k=8)[:, :, 0:r], func=AF.Copy),
                 reads=[("ps", 7)], writes=[("yT", s)])

        loads4(0)
        if ntile4 > 1:
            loads4(1)
        trans4(0)
        for t in range(ntile4):
            r = _rows(t, NT)
            s = t % 2
            if t + 1 < ntile4:
                trans4(t + 1)
            for c in range(2):
                bank = (t % 2) * 2 + c
                for k in range(8):
                    P.op("pe", lambda e, r=r, s=s, k=k, c=c, bank=bank: e.matmul(
                        ps[bank][0:r, :], lhsT=yT[s][:, k, 0:r], rhs=Wo[:, k, c * 512:(c + 1) * 512],
                        start=(k == 0), stop=(k == 7)),
                         reads=[("yT", s), "Wo"], writes=[("ps", bank)])
                P.op("dve", lambda e, r=r, s=s, c=c, bank=bank: e.tensor_tensor(
                    out=ot[s][0:r, c * 512:(c + 1) * 512], in0=ps[bank][0:r, :], in1=xr[s][0:r, c * 512:(c + 1) * 512], op=ALU.add),
                     reads=[("ps", bank), ("xr", s)], writes=[("ot", s, c)])
            dst = y_d[t * 128:t * 128 + r, :] if last else res_d[t * 128:t * 128 + r, :]
            o = P.op("act", lambda e, r=r, s=s, dst=dst: e.dma_start(out=dst, in_=ot[s][0:r, :]),
                     reads=[("ot", s, 0), ("ot", s, 1)], writes=[("res", t)], dma="ot%d" % s)
            if t + 2 < ntile4:
                loads4(t + 2)
        P.barrier()
    return finish()


def _slot_pos(SEQ):
    L = SEQ + NMETA
    N2 = L // 16
    assert L == 16 * N2 and N2 % 16 == 1
    sidx = np.arange(L)
    n2, n1 = sidx // 16, sidx % 16
    return (N2 * n1 + 16 * n2) % L


def _host_tables(SEQ, DEPTH):
    NT = SEQ // 128 + 1
    LP = NT * 128
    L = SEQ + NMETA
    N2 = L // 16
    NK2 = (N2 + 127) // 128
    pos = np.zeros(LP, dtype=np.int64)
    pos[:L] = _slot_pos(SEQ)
    rot = HD // 4
    inv = (np.float32(500000.0) ** (-np.arange(0, rot, 2, dtype=np.float32) / np.float32(rot))).astype(np.float32)
    ang = pos.astype(np.float32)[:, None] * inv[None, :]
    c, s = np.cos(ang).astype(np.float32), np.sin(ang).astype(np.float32)
    tab = np.concatenate([c, c, -s, s], axis=1).astype(np.float32)
    rope = np.ascontiguousarray(tab.reshape(NT, 128, 32).transpose(1, 0, 2))
    jj = np.arange(128)
    a = 2.0 * np.pi * ((jj[:, None] * jj[None, :]) % 128) / 128.0
    sc = 1.0 / math.sqrt(L * 128.0)
    chan = np.concatenate([np.cos(a) * sc, np.sin(a) * sc], axis=1).astype(np.float32)
    j16 = np.arange(16)
    a16 = 2.0 * np.pi * ((j16[:, None] * j16[None, :]) % 16) / 16.0
    eye8 = np.eye(8)
    bd = np.stack([np.kron(eye8, np.cos(a16)), np.kron(eye8, np.sin(a16)), -np.kron(eye8, np.sin(a16))], axis=1)
    bd = np.ascontiguousarray(bd).astype(ml_dtypes.bfloat16)
    jn = np.arange(N2)
    an = 2.0 * np.pi * ((16 * jn[:, None] * jn[None, :]) % N2) / N2
    t2full = np.zeros((NK2 * 128, 2, N2 + 1), dtype=np.float64)
    t2full[:N2, 0, :N2] = np.cos(an)
    t2full[:N2, 1, :N2] = -np.sin(an)
    t2 = np.ascontiguousarray(t2full.reshape(NK2, 128, 2, N2 + 1).transpose(1, 0, 2, 3)).astype(ml_dtypes.bfloat16)
    linit = np.array([[0.8 - 0.6 * math.exp(-0.3 * li) for li in range(DEPTH)]
                      + [1.0 - (0.8 - 0.6 * math.exp(-0.3 * li)) for li in range(DEPTH)]], dtype=np.float32)
    return rope, chan, bd, t2, linit


_CACHE = {}


def run(inputs, SEQ, DEPTH, n_cores, debug=None):
    key = (SEQ, DEPTH, debug)
    if key not in _CACHE:
        _CACHE[key] = (build_nc(SEQ, DEPTH, debug), _host_tables(SEQ, DEPTH))
    nc, (rope, chan, bd, t2, linit) = _CACHE[key]
    spos = _slot_pos(SEQ)
    f = lambda a: np.ascontiguousarray(np.asarray(a, dtype=np.float32))
    x = f(inputs["x"])
    meta = f(inputs["meta_tokens"])
    ng = np.ascontiguousarray(f(inputs["norm_gain"]).reshape(DEPTH, 8, 128).transpose(0, 2, 1))
    lam4 = np.ascontiguousarray(np.stack([f(inputs["lambda_q1"]), f(inputs["lambda_k1"]),
                                          f(inputs["lambda_q2"]), f(inputs["lambda_k2"])], axis=0))
    shared = {
        "ng": ng, "w_in": f(inputs["w_in"]), "w_f": f(inputs["w_fourier"]),
        "gq": f(inputs["q_norm_gain"]), "gk": f(inputs["k_norm_gain"]), "lam4": lam4, "subln": f(inputs["subln_gain"]),
        "w_out": f(inputs["w_out"]), "rope": rope, "chan": chan, "bd": bd, "t2": t2, "linit": linit,
    }
    in_maps = []
    for i in range(n_cores):
        m = dict(shared)
        full = np.concatenate([meta, x[i]], axis=0)
        m["x"] = np.ascontiguousarray(full[spos])
        in_maps.append(m)
    res = run_bass_kernel_spmd(nc, in_maps, core_ids=list(range(n_cores)))
    if debug:
        return res.results
    outs = []
    for r in res.results:
        ys = np.asarray(r["y"], dtype=np.float32)
        full = np.empty_like(ys)
        full[spos] = ys
        outs.append(full[NMETA:])
    return np.stack(outs, axis=0)


def kernel(x, meta_tokens, norm_gain, w_in, w_fourier, q_norm_gain, k_norm_gain,
           lambda_q1, lambda_k1, lambda_q2, lambda_k2, subln_gain, w_out):
    inputs = dict(x=x, meta_tokens=meta_tokens, norm_gain=norm_gain, w_in=w_in, w_fourier=w_fourier,
                  q_norm_gain=q_norm_gain, k_norm_gain=k_norm_gain, lambda_q1=lambda_q1, lambda_k1=lambda_k1,
                  lambda_q2=lambda_q2, lambda_k2=lambda_k2, subln_gain=subln_gain, w_out=w_out)
    x = np.asarray(x)
    return run(inputs, x.shape[1], np.asarray(w_in).shape[0], x.shape[0])
```

```python
import math
import numpy as np
import ml_dtypes
import concourse.bass as bass
import concourse.mybir as mybir
from concourse.bass_utils import run_bass_kernel_spmd

F32 = mybir.dt.float32
BF16 = mybir.dt.bfloat16
AF = mybir.ActivationFunctionType
ALU = mybir.AluOpType
AX = mybir.AxisListType

D = 1024
NMETA = 16
HD = 64
NH = 4
EPS = 1e-6
WCOLS = 3584
SAME_ENG_SYNC = True


class Prog:
    ENG = ("pe", "act", "dve", "pool", "sp")

    def __init__(self, nc):
        self.nc = nc
        self.ops = {e: [] for e in self.ENG}
        self.lastw = {}
        self.readers = {}
        self.dma_count = {}
        self.dma_since = {}

    def op(self, eng, fn, reads=(), writes=(), dma=None):
        o = dict(eng=eng, fn=fn, deps=[], dma=dma, inc=False)
        if dma is not None:
            self.dma_count[dma] = self.dma_count.get(dma, 0) + 1
            o["dma_val"] = 16 * self.dma_count[dma]
            self.dma_since[dma] = o
        for k in reads:
            w = self.lastw.get(k)
            if w is not None:
                self._dep(o, w)
        for k in writes:
            w = self.lastw.get(k)
            if w is not None:
                self._dep(o, w)
            for r in self.readers.get(k, {}).values():
                self._dep(o, r)
        for k in reads:
            rk = ("d", dma) if dma is not None else ("e", eng)
            self.readers.setdefault(k, {})[rk] = o
        for k in writes:
            self.lastw[k] = o
            self.readers[k] = {}
        self.ops[eng].append(o)
        return o

    def _dep(self, o, p):
        if p is o:
            return
        if p["dma"] is None and o["dma"] is None and p["eng"] == o["eng"]:
            if o["eng"] in ("pe", "sp") or not SAME_ENG_SYNC:
                return
        if p["dma"] is None:
            p["inc"] = True
        o["deps"].append(p)

    def barrier(self):
        last = {e: (self.ops[e][-1] if self.ops[e] else None) for e in self.ENG}
        dmas = list(self.dma_since.values())
        for e in self.ENG:
            o = dict(eng=e, fn=None, deps=[], dma=None, inc=False)
            for e2 in self.ENG:
                p = last[e2]
                if p is not None and e2 != e and e2 != "sp":
                    if p["fn"] is None:
                        continue
                    p["inc"] = True
                    o["deps"].append(p)
            for p in dmas:
                o["deps"].append(p)
            self.ops[e].append(o)
        self.lastw = {}
        self.readers = {}
        self.dma_since = {}

    def emit_all(self):
        nc = self.nc
        sems = {}
        for e in ("pe", "act", "dve", "pool"):
            sems[e] = nc.alloc_semaphore(name="prog_" + e)
        dsem = {}
        for k in self.dma_count:
            dsem[k] = nc.alloc_semaphore(name="dma_" + str(k))
        for e in ("pe", "act", "dve", "pool"):
            c = 0
            for o in self.ops[e]:
                if o["inc"] and o["fn"] is not None and o["dma"] is None:
                    c += 1
                    o["cnt"] = c
        ops = self.ops

        def emit(ename, eng):
            waited = {}
            for o in ops[ename]:
                for p in o["deps"]:
                    if p["dma"] is not None:
                        s, v = dsem[p["dma"]], p["dma_val"]
                    else:
                        s, v = sems[p["eng"]], p["cnt"]
                    key = id(s)
                    if waited.get(key, 0) >= v:
                        continue
                    waited[key] = v
                    eng.wait_ge(s, v)
                if o["fn"] is None:
                    continue
                ins = o["fn"](eng)
                if o["dma"] is not None:
                    ins.then_inc(dsem[o["dma"]], 16)
                elif o["inc"]:
                    ins.then_inc(sems[ename], 1)

        with nc.Block() as block:
            @block.tensor
            def _(e):
                emit("pe", e)

            @block.scalar
            def _(e):
                emit("act", e)

            @block.vector
            def _(e):
                emit("dve", e)

            @block.gpsimd
            def _(e):
                emit("pool", e)

            @block.sync
            def _(e):
                emit("sp", e)


def _rows(t, nt):
    return 128 if t < nt - 1 else NMETA


def fr(ap, dims):
    return bass.AP(ap.tensor, ap.offset, [list(ap.ap[0])] + [list(d) for d in dims])


def fro(ap, off, dims):
    return bass.AP(ap.tensor, ap.offset + off, [list(ap.ap[0])] + [list(d) for d in dims])


def build_nc(SEQ, DEPTH, debug=None):
    NT = SEQ // 128 + 1
    LP = NT * 128
    nc = bass.Bass("TRN2", target_bir_lowering=False)
    P = Prog(nc)

    def din(name, shape, dt=F32):
        return nc.dram_tensor(name, list(shape), dt, kind="ExternalInput").ap()

    def dscr(name, shape, dt):
        return nc.dram_tensor(name, list(shape), dt, kind="ExternalOutput" if debug else "Internal").ap()

    def finish():
        if debug:
            print("ops:", {e: len(v) for e, v in P.ops.items()}, "dma sems:", len(P.dma_count))
        P.emit_all()
        return nc

    L = SEQ + NMETA
    N2 = L // 16
    NK2 = (N2 + 127) // 128
    k2rows = [min(128, N2 - 128 * i) for i in range(NK2)]
    x_d = din("x", [L, D])
    ng_d = din("ng", [DEPTH, 128, 8])
    win_d = din("w_in", [DEPTH, D, 3072])
    wf_d = din("w_f", [DEPTH, 4, 128, 128])
    gq_d = din("gq", [DEPTH, HD])
    gk_d = din("gk", [DEPTH, HD])
    lam_d = din("lam4", [4, DEPTH, HD])
    sl_d = din("subln", [DEPTH, 128])
    wout_d = din("w_out", [DEPTH, D, D])
    rope_d = din("rope", [128, NT, 32])
    chan_d = din("chan", [128, 256])
    bd_d = din("bd", [128, 3, 128], BF16)
    t2_d = din("t2", [128, NK2, 2, N2 + 1], BF16)
    linit_d = din("linit", [1, 2 * DEPTH])
    y_d = nc.dram_tensor("y", [L, D], F32, kind="ExternalOutput").ap()

    Wp_d = dscr("Wp", [DEPTH, 128, 8, WCOLS], BF16)
    Wo_d = dscr("Wo", [DEPTH, 128, 8, D], BF16)
    res_d = dscr("res", [LP, D], F32)
    P_d = dscr("Pd", [LP, D], BF16)
    G_d = dscr("Gd", [LP, D], BF16)
    Y_d = dscr("Yd", [LP, D], BF16)

    base0 = (nc.sbuf_base + 63) // 64 * 64
    top = nc.sbuf_top
    cur = [base0]
    cnt = [0]

    def alloc(shape, dt):
        nbytes = int(np.prod(shape)) * (4 if dt == F32 else 2)
        off = cur[0]
        cur[0] = (off + nbytes + 63) // 64 * 64
        assert cur[0] <= top, ("SBUF overflow", cur[0], top)
        cnt[0] += 1
        return nc.alloc_sbuf_tensor_at("t%d" % cnt[0], [128] + list(shape), dt, offset=off)

    psall = nc.alloc_psum_tensor("psall", [128, 4096], F32)
    ps = [psall[:, i * 512:(i + 1) * 512] for i in range(8)]

    def psb(i):
        return ps[i].bitcast(BF16)

    ident_b = alloc([128], BF16)
    gq_b = alloc([DEPTH, HD], F32)
    gk_b = alloc([DEPTH, HD], F32)
    slg_b = alloc([DEPTH, 128], F32)
    lsum = alloc([2 * DEPTH], F32)
    linit = alloc([2 * DEPTH], F32)
    neglam = alloc([DEPTH], F32)
    nshift = alloc([DEPTH], F32)
    gmax = alloc([2 * DEPTH], F32)
    rope = alloc([NT, 32], F32)
    bd = alloc([3, 128], BF16)
    mhalf = alloc([8], F32)
    const_end = cur[0]
    ident_f = alloc([128], F32)
    chan = alloc([256], F32)
    identsrc = alloc([128], F32)
    lamv = alloc([4, DEPTH, HD], F32)
    lprod = alloc([2, DEPTH, HD], F32)
    p0_start = cur[0]

    def bcast_rows(ap2d):
        return bass.AP(ap2d.tensor, ap2d.offset, [[0, 128]] + [list(d) for d in ap2d.ap[1:]])

    def mk_ident(e):
        return e.memset(identsrc[:, :], 0.0)
    P.op("pool", mk_ident, writes=["identsrc"])
    P.op("pool", lambda e: e.iota(identsrc[:, :], [[1, 128]], channel_multiplier=-1,
                                  allow_small_or_imprecise_dtypes=True),
         writes=["identsrc"])
    P.op("dve", lambda e: e.tensor_single_scalar(ident_f[:, :], identsrc[:, :], 0.0, ALU.is_equal),
         reads=["identsrc"], writes=["ident_f"])
    P.op("dve", lambda e: e.tensor_copy(ident_b[:, :], ident_f[:, :]), reads=["ident_f"], writes=["ident_b"])

    P.op("pool", lambda e: e.memset(mhalf[:, :], -0.5), writes=["mhalf"])

    def cload(dst, src, key, q="sp"):
        P.op(q, lambda e: e.dma_start(out=dst, in_=src), writes=[key], dma="c_" + key)

    cload(gq_b[:, :, :], bcast_rows(gq_d.rearrange("(o l) d -> o l d", o=1)), "gq_b")
    cload(gk_b[:, :, :], bcast_rows(gk_d.rearrange("(o l) d -> o l d", o=1)), "gk_b")
    cload(slg_b[:, :, :], bcast_rows(sl_d.rearrange("(o l) d -> o l d", o=1)), "slg_b")
    cload(lamv[:, :, :, :], bcast_rows(lam_d.rearrange("(o f) l d -> o f l d", o=1)), "lamv")
    cload(linit[:, :], bcast_rows(linit_d), "linit")
    cload(rope[:, :, :], rope_d, "rope")
    cload(chan[:, :], chan_d, "chan")
    cload(bd[:, :, :], bd_d, "bd")

    P.op("dve", lambda e: e.tensor_tensor(out=lprod[:, 0, :, :], in0=lamv[:, 0, :, :], in1=lamv[:, 1, :, :], op=ALU.mult),
         reads=["lamv"], writes=["lprod0"])
    P.op("dve", lambda e: e.tensor_tensor(out=lprod[:, 1, :, :], in0=lamv[:, 2, :, :], in1=lamv[:, 3, :, :], op=ALU.mult),
         reads=["lamv"], writes=["lprod1"])
    P.op("dve", lambda e: e.tensor_reduce(out=lsum[:, :], in_=lprod[:, :, :, :].rearrange("p a l d -> p (a l) d"),
                                          axis=AX.X, op=ALU.add),
         reads=["lprod0", "lprod1"], writes=["lsum"])
    P.op("act", lambda e: e.activation(out=lsum[:, :], in_=lsum[:, :], func=AF.Exp), reads=["lsum"], writes=["lsum"])
    P.op("dve", lambda e: e.tensor_tensor(out=neglam[:, :], in0=lsum[:, DEPTH:2 * DEPTH], in1=lsum[:, 0:DEPTH], op=ALU.subtract),
         reads=["lsum"], writes=["neglam"])
    P.op("dve", lambda e: e.tensor_tensor(out=neglam[:, :], in0=neglam[:, :], in1=linit[:, 0:DEPTH], op=ALU.subtract),
         reads=["neglam", "linit"], writes=["neglam"])
    P.op("dve", lambda e: e.tensor_reduce(out=gmax[:, 0:DEPTH], in_=gq_b[:, :, :], axis=AX.X, op=ALU.max,
                                          apply_absolute_value=True),
         reads=["gq_b"], writes=["gmaxq"])
    P.op("dve", lambda e: e.tensor_reduce(out=gmax[:, DEPTH:2 * DEPTH], in_=gk_b[:, :, :], axis=AX.X, op=ALU.max,
                                          apply_absolute_value=True),
         reads=["gk_b"], writes=["gmaxk"])
    P.op("dve", lambda e: e.scalar_tensor_tensor(out=nshift[:, :], in0=gmax[:, 0:DEPTH], scalar=-8.0,
                                                 in1=gmax[:, DEPTH:2 * DEPTH], op0=ALU.mult, op1=ALU.mult),
         reads=["gmaxq", "gmaxk"], writes=["nshift"])
    P.op("dve", lambda e: e.tensor_scalar(out=gq_b[:, :, :], in0=gq_b[:, :, :], scalar1=HD ** -0.5, scalar2=None, op0=ALU.mult),
         reads=["gq_b", "gmaxq"], writes=["gq_b"])
    for l in range(DEPTH):
        P.op("dve", lambda e, l=l: e.tensor_scalar(out=slg_b[:, l, :], in0=slg_b[:, l, :],
                                                   scalar1=linit[:, DEPTH + l:DEPTH + l + 1], scalar2=None, op0=ALU.mult),
             reads=["slg_b", "linit"], writes=["slg_b"])

    cur[0] = const_end
    qT = alloc([NH, LP], BF16)
    kT = alloc([NH, LP], BF16)
    vall = alloc([NT, NH, 130], BF16)
    qkv_end = cur[0]
    Wsb = alloc([8, WCOLS], BF16)
    w_end = cur[0]

    cur[0] = p0_start
    ngt = alloc([DEPTH, 8], F32)
    wst = [alloc([3072], F32) for _ in range(2)]
    wpb = [alloc([2560], BF16) for _ in range(2)]
    wcs = [alloc([1024], BF16) for _ in range(2)]
    wfT = [alloc([128], F32) for _ in range(2)]
    MM = alloc([4, 256], F32)
    wfl = alloc([4, 128], F32)
    wos = [alloc([1024], F32) for _ in range(2)]
    wob = [alloc([1024], BF16) for _ in range(2)]
    early_wsb = cur[0] <= qkv_end
    P.op("sp", lambda e: e.dma_start(out=ngt[:, :, :], in_=ng_d.rearrange("l p k -> p l k")), writes=["ngt"], dma="ngt")
    def loads0(l, kt):
        s = kt % 2
        P.op("sp", lambda e, l=l, kt=kt, s=s: e.dma_start(out=wst[s][:, :], in_=win_d[l, kt * 128:(kt + 1) * 128, :]),
             writes=[("wst", s)], dma="wst%d" % s)
        P.op("pool", lambda e, l=l, kt=kt, s=s: e.dma_start(out=wos[s][:, :], in_=wout_d[l, kt * 128:(kt + 1) * 128, :]),
             writes=[("wos", s)], dma="wos%d" % s)

    for l in range(DEPTH):
        P.op("sp", lambda e, l=l: e.dma_start(out=wfl[:, :, :], in_=wf_d[l].rearrange("g c e -> c g e")),
             writes=["wfl"], dma="wfl")
        for g in range(4):
            for cs in range(2):
                P.op("pe", lambda e, g=g, cs=cs: e.matmul(ps[g][:, cs * 128:(cs + 1) * 128], lhsT=chan[:, cs * 128:(cs + 1) * 128],
                                                          rhs=wfl[:, g, :], start=True, stop=True),
                     reads=["chan", "wfl"], writes=[("ps", g, cs)])
            P.op("dve", lambda e, g=g: e.tensor_copy(MM[:, g, :], ps[g][:, 0:256]),
                 reads=[("ps", g, 0), ("ps", g, 1)], writes=[("MM", g)])
        if early_wsb and l == min(1, DEPTH - 1):
            P.op("pool", lambda e: e.dma_start(out=Wsb[:, :, :], in_=Wp_d[0]), reads=[("Wp", 0)], writes=["Wsb"], dma="Wsb")
        for kt in range(8):
            s = kt % 2
            if l == 0 and kt == 0:
                loads0(0, 0)
            nxt = l * 8 + kt + 1
            if nxt < DEPTH * 8:
                loads0(nxt // 8, nxt % 8)
            sc = ngt[:, l, kt:kt + 1]
            P.op("act", lambda e, s=s, sc=sc: e.activation(out=wpb[s][:, 0:1280], in_=wst[s][:, 512:1792], func=AF.Copy, scale=sc),
                 reads=[("wst", s), "ngt"], writes=[("wpb", s, 0)])
            P.op("dve", lambda e, s=s, sc=sc: e.tensor_scalar(out=wpb[s][:, 1280:2560], in0=wst[s][:, 1792:3072], scalar1=sc,
                                                              scalar2=None, op0=ALU.mult),
                 reads=[("wst", s), "ngt"], writes=[("wpb", s, 1)])
            P.op("sp", lambda e, l=l, kt=kt, s=s: e.dma_start(out=Wp_d[l, :, kt, 1024:WCOLS], in_=wpb[s][:, :]),
                 reads=[("wpb", s, 0), ("wpb", s, 1)], writes=[("Wp", l)], dma="wpbst%d" % s)
            P.op("dve", lambda e, s=s, sc=sc: e.tensor_scalar(out=wst[s][:, 0:512], in0=wst[s][:, 0:512], scalar1=sc,
                                                              scalar2=None, op0=ALU.mult),
                 reads=[("wst", s), "ngt"], writes=[("wst", s)])
            def tr0(g, s=s):
                s2 = g % 2
                P.op("pe", lambda e, s=s, g=g, s2=s2: e.transpose(out=ps[4 + s2][:, 0:128], in_=wst[s][:, g * 128:(g + 1) * 128],
                                                                 identity=ident_f[:, :]),
                     reads=[("wst", s), "ident_f"], writes=[("psT", s2)])
                P.op("act", lambda e, s2=s2: e.activation(out=wfT[s2][:, :], in_=ps[4 + s2][:, 0:128], func=AF.Copy),
                     reads=[("psT", s2)], writes=[("wfT", s2)])

            tr0(0)
            for g in range(4):
                s2 = g % 2
                if g + 1 < 4:
                    tr0(g + 1)
                P.op("pe", lambda e, s2=s2, g=g: e.matmul(ps[6 + s2][:, 0:256], lhsT=wfT[s2][:, :], rhs=MM[:, g, :],
                                                          start=True, stop=True),
                     reads=[("wfT", s2), ("MM", g)], writes=[("psM", s2)])
                P.op("dve", lambda e, s=s, s2=s2, g=g: e.tensor_copy(
                    fro(wcs[s][:, :], g * 128, [[512, 2], [1, 128]]),
                    ps[6 + s2][:, 0:256].rearrange("p (a b) -> p a b", a=2)),
                     reads=[("psM", s2)], writes=[("wcs", s, g)])
            P.op("sp", lambda e, l=l, kt=kt, s=s: e.dma_start(out=Wp_d[l, :, kt, 0:1024], in_=wcs[s][:, :]),
                 reads=[("wcs", s, g) for g in range(4)], writes=[("Wp", l)], dma="wcsst%d" % s)
            P.op("act", lambda e, s=s: e.activation(out=wob[s][:, :], in_=wos[s][:, :], func=AF.Copy),
                 reads=[("wos", s)], writes=[("wob", s)])
            P.op("sp", lambda e, l=l, kt=kt, s=s: e.dma_start(out=Wo_d[l, :, kt, :], in_=wob[s][:, :]),
                 reads=[("wob", s)], writes=[("Wo", l)], dma="wobst%d" % s)
    P.barrier()
    if debug == "p0":
        return finish()


    cur[0] = w_end
    xt = [alloc([D], F32) for _ in range(2)]
    hb = [alloc([D], BF16) for _ in range(2)]
    hT = [alloc([8, 128], BF16) for _ in range(2)]
    pst = [alloc([D], BF16) for _ in range(2)]
    hst = [alloc([D], BF16) for _ in range(2)]
    sg = [alloc([D], BF16) for _ in range(2)]
    tmp1 = [alloc([512], F32) for _ in range(2)]
    tmp3 = [alloc([512], F32) for _ in range(2)]
    qb = [alloc([512], BF16) for _ in range(2)]
    ssx = [alloc([4], F32) for _ in range(2)]
    ss8 = [alloc([8], F32) for _ in range(2)]
    tA = [alloc([128], F32) for _ in range(2)]
    tB = [alloc([128], F32) for _ in range(2)]

    cur[0] = w_end
    NPT = 3
    PT = [alloc([1024], BF16) for _ in range(NPT)]
    gl = [alloc([128], BF16) for _ in range(4)]
    rz = [alloc([4], F32) for _ in range(4)]
    o1 = [alloc([128], F32) for _ in range(4)]
    o2 = [alloc([128], F32) for _ in range(4)]
    o3 = [alloc([128], F32) for _ in range(4)]
    ya = [alloc([128], BF16) for _ in range(4)]
    accS = [alloc([258], F32) for _ in range(4)]
    p2_end = cur[0]
    cur[0] = qkv_end
    HallQ = [alloc([NK2, 4, D], BF16) for _ in range(2)]
    assert cur[0] <= w_end
    cur[0] = p2_end
    Wo = alloc([8, D], BF16)
    wo_end = cur[0]
    NCS = 3
    need3 = 2 * NK2 * 4 * D * 2 + (NCS + 2) * 1024 + NK2 * 2 * (N2 + 1) * 2 + 1024
    cur[0] = const_end if const_end + need3 <= qkv_end else wo_end
    HallQ += [alloc([NK2, 4, D], BF16) for _ in range(2)]
    gt = [alloc([512], BF16) for _ in range(NCS)]
    yf = [alloc([512], BF16) for _ in range(2)]
    T2 = alloc([NK2, 2, N2 + 1], BF16)
    p3_end = cur[0]
    need4 = 2 * (2 * D + 2 * D + 4 * D + 4 * D) + 512
    cur[0] = const_end if const_end + need4 <= qkv_end else p3_end
    yt = [alloc([D], BF16) for _ in range(2)]
    yT = [alloc([8, 128], BF16) for _ in range(2)]
    xr = [alloc([D], F32) for _ in range(2)]
    ot = [alloc([D], F32) for _ in range(2)]
    for l in range(DEPTH):
        last = (l == DEPTH - 1)

        def res_src(t, r, l=l):
            if l == 0:
                return x_d[t * 128:t * 128 + r, :]
            return res_d[t * 128:t * 128 + r, :]

        if l == 0 and not early_wsb:
            P.op("sp", lambda e: e.dma_start(out=Wsb[:, :, :], in_=Wp_d[0]), reads=[("Wp", 0)], writes=["Wsb"], dma="Wsb")
        P.op("pool", lambda e: e.memset(vall[:, :, :, 128:130], 1.0), writes=["vones"])

        deferred = []
        deferredC = []
        deferredCP = []

        def flushC():
            for f in deferredC:
                f()
            del deferredC[:]

        def flush_qkT():
            flushC()
            for (r_, u_, t_, dstT_) in deferred:
                for h in range(NH):
                    P.op("pe", lambda e, r=r_, u=u_, h=h: e.transpose(out=psb(6)[:, u * 512 + h * 128:u * 512 + h * 128 + r],
                                                                    in_=qb[u][0:r, h * 128:(h + 1) * 128],
                                                                    identity=ident_b[0:r, 0:r]),
                         reads=[("qb", u_, 0), ("qb", u_, 1), "ident_b"], writes=[("ps", 6)])
                P.op("dve", lambda e, r=r_, u=u_, t=t_, dstT=dstT_: e.tensor_copy(
                    dstT[:, :, t * 128:t * 128 + r],
                    psb(6)[:, u * 512:(u + 1) * 512].rearrange("p (h m) -> p h m", h=NH)[:, :, 0:r]),
                     reads=[("ps", 6)], writes=[("qkT", u_, t_)])
            del deferred[:]

        def head_a(t):
            r = _rows(t, NT)
            s = t % 2
            P.op("sp", lambda e, r=r, s=s, src=res_src(t, r): e.dma_start(out=xt[s][0:r, :], in_=src),
                 reads=[("res", t)], writes=[("xt", s)], dma="xt%d" % s)
            P.op("act", lambda e, r=r, s=s: e.activation(out=hb[s][0:r, :], in_=xt[s][0:r, :], func=AF.Square,
                                                         accum_out=ssx[s][0:r, 0:1]),
                 reads=[("xt", s)], writes=[("hb", s), ("ssx", s)])
            P.op("dve", lambda e, r=r, s=s: e.tensor_scalar(out=ssx[s][0:r, 1:2], in0=ssx[s][0:r, 0:1], scalar1=1.0 / D,
                                                            scalar2=EPS, op0=ALU.mult, op1=ALU.add),
                 reads=[("ssx", s)], writes=[("ssx1", s)])
            P.op("pool", lambda e, r=r, s=s: e.tensor_tensor(out=ssx[s][0:r, 2:3], in0=ssx[s][0:r, 1:2], in1=mhalf[0:r, 0:1], op=ALU.pow),
                 reads=[("ssx1", s)], writes=[("rstd", s)])
            P.op("act", lambda e, r=r, s=s: e.activation(out=hb[s][0:r, :], in_=xt[s][0:r, :], func=AF.Copy,
                                                         scale=ssx[s][0:r, 2:3]),
                 reads=[("xt", s), ("rstd", s)], writes=[("hb", s)])

        def head_b(t):
            r = _rows(t, NT)
            s = t % 2
            for k in range(8):
                P.op("pe", lambda e, r=r, s=s, k=k: e.transpose(out=psb(7)[:, k * 128:k * 128 + r],
                                                                in_=hb[s][0:r, k * 128:(k + 1) * 128],
                                                                identity=ident_b[0:r, 0:r]),
                     reads=[("hb", s), "ident_b"], writes=[("ps", 7)])
            P.op("dve", lambda e, r=r, s=s: e.tensor_copy(hT[s][:, :, 0:r],
                                                          psb(7)[:, :].rearrange("p (k m) -> p k m", k=8)[:, :, 0:r]),
                 reads=[("ps", 7)], writes=[("hT", s)])


        head_a(0)
        head_b(0)
        if NT > 1:
            head_a(1)
        for t in range(NT):
            r = _rows(t, NT)
            s = t % 2
            flushC()
            if t + 2 < NT:
                head_a(t + 2)

            def proj(c, bank, r=r, s=s):
                for k in range(8):
                    P.op("pe", lambda e, k=k: e.matmul(ps[bank][0:r, :], lhsT=hT[s][:, k, 0:r],
                                                       rhs=Wsb[:, k, c * 512:(c + 1) * 512], start=(k == 0), stop=(k == 7)),
                         reads=[("hT", s), "Wsb"], writes=[("ps", bank)])

            proj(0, 0)
            proj(1, 1)
            P.op("act", lambda e, r=r, s=s: e.activation(out=pst[s][0:r, 0:512], in_=ps[0][0:r, :], func=AF.Copy),
                 reads=[("ps", 0)], writes=[("pst", s, 0)])
            P.op("dve", lambda e, r=r, s=s: e.tensor_copy(pst[s][0:r, 512:1024], ps[1][0:r, :]),
                 reads=[("ps", 1)], writes=[("pst", s, 1)])
            if t + 1 < NT:
                head_b(t + 1)
            flush_qkT()
            for qi, (c, bank, gb, dstT) in enumerate(((2, 2, gq_b, qT), (3, 3, gk_b, kT))):
                proj(c, bank)
                u = qi
                P.op("act", lambda e, r=r, bank=bank, u=u: e.activation(out=tmp1[u][0:r, :], in_=ps[bank][0:r, :], func=AF.Square),
                     reads=[("ps", bank)], writes=[("tmp1", u)])
                P.op("dve", lambda e, r=r, u=u: e.tensor_reduce(out=ss8[u][0:r, :],
                                                                in_=tmp1[u][0:r, :].rearrange("p (b d) -> p b d", b=8),
                                                                axis=AX.X, op=ALU.add),
                     reads=[("tmp1", u)], writes=[("ss8", u)])
                P.op("dve", lambda e, r=r, u=u: e.tensor_scalar(out=ss8[u][0:r, :], in0=ss8[u][0:r, :], scalar1=1.0 / HD,
                                                                scalar2=EPS, op0=ALU.mult, op1=ALU.add),
                     reads=[("ss8", u)], writes=[("ss8", u)])
                P.op("pool", lambda e, r=r, u=u: e.tensor_tensor(out=ss8[u][0:r, :], in0=ss8[u][0:r, :], in1=mhalf[0:r, 0:8], op=ALU.pow),
                     reads=[("ss8", u)], writes=[("ss8", u)])
                P.op("dve", lambda e, r=r, u=u, bank=bank: e.tensor_tensor(
                    out=tmp1[u][0:r, :].rearrange("p (b d) -> p b d", b=8),
                    in0=ps[bank][0:r, :].rearrange("p (b d) -> p b d", b=8),
                    in1=fr(ss8[u][0:r, :], [[1, 8], [0, HD]]), op=ALU.mult),
                     reads=[("ps", bank), ("ss8", u)], writes=[("tmp1", u)])
                def cpart(r=r, u=u, t=t, gb=gb):
                    gsl = gb[0:r, l, :]
                    P.op("pool", lambda e, r=r, u=u, gsl=gsl: e.tensor_tensor(
                        out=tmp3[u][0:r, :].rearrange("p (b d) -> p b d", b=8),
                        in0=tmp1[u][0:r, :].rearrange("p (b d) -> p b d", b=8),
                        in1=fr(gsl, [[0, 8], [1, HD]]), op=ALU.mult),
                         reads=[("tmp1", u), "gq_b", "gk_b"], writes=[("tmp3", u)])
                    t3 = tmp3[u][0:r, :]
                    rp = rope[0:r, t, :]
                    P.op("pool", lambda e, r=r, u=u, t3=t3, rp=rp: e.tensor_tensor(
                        out=tA[u][0:r, :].rearrange("p (b d) -> p b d", b=8),
                        in0=fr(t3, [[HD, 8], [1, 16]]), in1=fr(rp, [[0, 8], [1, 16]]), op=ALU.mult),
                         reads=[("tmp3", u), "rope"], writes=[("tA", u)])
                    P.op("pool", lambda e, r=r, u=u, t3=t3, rp=rp: e.tensor_tensor(
                        out=fr(tB[u][0:r, :], [[16, 8], [1, 8]]),
                        in0=fro(t3, 8, [[HD, 8], [1, 8]]),
                        in1=fro(rp, 16, [[0, 8], [1, 8]]), op=ALU.mult),
                         reads=[("tmp3", u), "rope"], writes=[("tB", u, 0)])
                    P.op("pool", lambda e, r=r, u=u, t3=t3, rp=rp: e.tensor_tensor(
                        out=fro(tB[u][0:r, :], 8, [[16, 8], [1, 8]]),
                        in0=fr(t3, [[HD, 8], [1, 8]]),
                        in1=fro(rp, 24, [[0, 8], [1, 8]]), op=ALU.mult),
                         reads=[("tmp3", u), "rope"], writes=[("tB", u, 1)])
                    P.op("pool", lambda e, r=r, u=u: e.tensor_tensor(
                        out=fr(qb[u][0:r, :], [[HD, 8], [1, 16]]),
                        in0=tA[u][0:r, :].rearrange("p (b d) -> p b d", b=8),
                        in1=tB[u][0:r, :].rearrange("p (b d) -> p b d", b=8), op=ALU.add),
                         reads=[("tA", u), ("tB", u, 0), ("tB", u, 1)], writes=[("qb", u, 0)])
                deferredCP.append(cpart)

                def cpart_act(r=r, u=u):
                    t3 = tmp3[u][0:r, :]
                    P.op("act", lambda e, r=r, u=u, t3=t3: e.activation(
                        out=fro(qb[u][0:r, :], 16, [[HD, 8], [1, HD - 16]]), in_=fro(t3, 16, [[HD, 8], [1, HD - 16]]), func=AF.Copy),
                         reads=[("tmp3", u)], writes=[("qb", u, 1)])
                deferredC.append(cpart_act)
                deferred.append((r, u, t, dstT))
            proj(4, 4)
            for (bank, (m0, m1)) in ((0, (0, 2)), (1, (1, 0))):
                P.op("pe", lambda e, r=r, s=s, bank=bank, m0=m0: e.matmul(ps[bank][0:r, :], lhsT=bd[0:r, m0, 0:r], rhs=pst[s][0:r, 0:512],
                                                                         start=True, stop=False),
                     reads=[("pst", s, 0), "bd"], writes=[("ps", bank)])
                P.op("pe", lambda e, r=r, s=s, bank=bank, m1=m1: e.matmul(ps[bank][0:r, :], lhsT=bd[0:r, m1, 0:r], rhs=pst[s][0:r, 512:1024],
                                                                         start=False, stop=True),
                     reads=[("pst", s, 1), "bd"], writes=[("ps", bank)])
            P.op("dve", lambda e, r=r, s=s: e.tensor_copy(hst[s][0:r, 0:512], ps[0][0:r, :]),
                 reads=[("ps", 0)], writes=[("hst", s, 0)])
            P.op("dve", lambda e, r=r, s=s: e.tensor_copy(hst[s][0:r, 512:1024], ps[1][0:r, :]),
                 reads=[("ps", 1)], writes=[("hst", s, 1)])
            P.op("act", lambda e, r=r, t=t: e.activation(out=vall[0:r, t, :, 0:128],
                                                         in_=ps[4][0:r, :].rearrange("p (h d) -> p h d", h=NH), func=AF.Copy),
                 reads=[("ps", 4)], writes=[("v", t)])
            proj(5, 5)
            proj(6, 4)
            P.op("act", lambda e, r=r, s=s: e.activation(out=sg[s][0:r, 0:512], in_=ps[5][0:r, :], func=AF.Silu),
                 reads=[("ps", 5)], writes=[("sg", s, 0)])
            P.op("act", lambda e, r=r, s=s: e.activation(out=sg[s][0:r, 512:1024], in_=ps[4][0:r, :], func=AF.Silu),
                 reads=[("ps", 4)], writes=[("sg", s, 1)])
            P.op("act", lambda e, t=t, r=r, s=s: e.dma_start(out=G_d[t * 128:t * 128 + r, :], in_=sg[s][0:r, :]),
                 reads=[("sg", s, 0), ("sg", s, 1)], writes=[("Gd", t)], dma="sg%d" % s)
            P.op("act", lambda e, t=t, r=r, s=s: e.dma_start(out=P_d[t * 128:t * 128 + r, :], in_=hst[s][0:r, :]),
                 reads=[("hst", s, 0), ("hst", s, 1)], writes=[("Pd", t)], dma="hst%d" % s)
            for f in deferredCP:
                f()
            del deferredCP[:]
        flush_qkT()
        P.barrier()
        if debug == "p1":
            return finish()

        def hall_loads(quarters):
            for qd in quarters:
                for kt in range(NK2):
                    kr = k2rows[kt]
                    src = P_d[kt * 2048:kt * 2048 + kr * 16, :].rearrange("(p a) c -> p a c", a=16)[:, qd * 4:qd * 4 + 4, :]
                    P.op("sp" if qd % 2 == 0 else "pool", lambda e, kt=kt, kr=kr, qd=qd, src=src: e.dma_start(
                        out=HallQ[qd][0:kr, kt, :, :], in_=src),
                         reads=[("Pd", t) for t in range(NT)], writes=[("Hall", kt, qd)], dma="Hall%d_%d" % (kt, qd))

        hall_loads((0, 1))
        P.op("sp", lambda e, l=l: e.dma_start(out=Wo[:, :, :], in_=Wo_d[l]), reads=[("Wo", l)], writes=["Wo"], dma="WoL")

        nch = (L + 511) // 512
        bounds = [(L * ci) // nch for ci in range(nch + 1)]
        chunks = []
        for ci in range(nch):
            q0c, qnc = bounds[ci], bounds[ci + 1] - bounds[ci]
            chunks.append((q0c, qnc, [(qo, min(128, qnc - qo)) for qo in range(0, qnc, 128)]))
        steps = [(h, ch, kt) for h in range(NH) for ch in chunks for kt in range(NT)]
        pp = [0]

        def qk_exp(i):
            h, (q0, qn, qtiles), kt = steps[i]
            kr = _rows(kt, NT)
            sl = i % NPT
            sb = (i % 2) * 2
            qtok = list(range(q0 // 128, (q0 + qn - 1) // 128 + 1))
            for c in range(2):
                P.op("pe", lambda e, c=c: e.matmul(
                    ps[sb + c][0:kr, 0:qn], lhsT=kT[c * 64:(c + 1) * 64, h, kt * 128:kt * 128 + kr],
                    rhs=qT[c * 64:(c + 1) * 64, h, q0:q0 + qn], start=True, stop=True),
                     reads=[("qkT", 1, kt)] + [("qkT", 0, t) for t in qtok], writes=[("ps", sb + c)])
            P.op("act", lambda e, l=l: e.activation(
                out=fr(PT[sl][0:kr, :], [[512, 2], [1, qn]]), in_=fr(ps[sb][0:kr, :], [[512, 2], [1, qn]]), func=AF.Exp,
                bias=nshift[0:kr, l:l + 1], scale=1.0),
                 reads=[("ps", sb), ("ps", sb + 1), "nshift"], writes=[("PT", sl)])

        def av(i):
            h, (q0, qn, qtiles), kt = steps[i]
            kr = _rows(kt, NT)
            sl = i % NPT
            for j, (qo, qr) in enumerate(qtiles):
                for c in range(2):
                    P.op("pe", lambda e, c=c, j=j, qr=qr, qo=qo: e.matmul(
                        ps[4 + j][0:qr, c * 129:(c + 1) * 129], lhsT=PT[sl][0:kr, c * 512 + qo:c * 512 + qo + qr],
                        rhs=vall[0:kr, kt, h, 0:129], start=(kt == 0 and c == 0), stop=(kt == NT - 1 and c == 1),
                        skip_group_check=True),
                         reads=[("PT", sl), ("v", kt), "vones"], writes=[("acc", j)])

        def post(i):
            h, (q0, qn, qtiles), kt = steps[i]
            info = []
            for j, (qo, qr) in enumerate(qtiles):
                tq = q0 + qo
                w = pp[0] % 4
                pp[0] += 1
                info.append((j, tq, qr, w))
                P.op("dve", lambda e, qr=qr, j=j, w=w: e.tensor_copy(accS[w][0:qr, :], ps[4 + j][0:qr, 0:258]),
                     reads=[("acc", j)], writes=[("accS", w)])
            for (j, tq, qr, w) in info:
                acc = accS[w]
                P.op("sp", lambda e, tq=tq, qr=qr, w=w: e.dma_start(
                    out=gl[w][0:qr, :], in_=G_d[tq:tq + qr, 512 + h * 128:512 + (h + 1) * 128]),
                     reads=[("Gd", t_) for t_ in range(NT)], writes=[("gl", w)], dma="gl%d" % w)
                P.op("dve", lambda e, qr=qr, acc=acc, w=w: e.reciprocal(
                    out=rz[w][0:qr, 0:2], in_=fr(acc[0:qr, 128:129], [[129, 2]])),
                     reads=[("accS", w)], writes=[("rz", w)])
                P.op("dve", lambda e, qr=qr, w=w, l=l: e.tensor_tensor(out=rz[w][0:qr, 2:3], in0=rz[w][0:qr, 1:2],
                                                                       in1=neglam[0:qr, l:l + 1], op=ALU.mult),
                     reads=[("rz", w), "neglam"], writes=[("rz2", w)])
                P.op("dve", lambda e, qr=qr, acc=acc, w=w: e.tensor_scalar(
                    out=o1[w][0:qr, :], in0=acc[0:qr, 0:128], scalar1=rz[w][0:qr, 0:1], scalar2=None, op0=ALU.mult),
                     reads=[("accS", w), ("rz", w)], writes=[("o1", w)])
                P.op("dve", lambda e, qr=qr, acc=acc, w=w: e.scalar_tensor_tensor(
                    out=o2[w][0:qr, :], in0=acc[0:qr, 129:257], scalar=rz[w][0:qr, 2:3], in1=o1[w][0:qr, :],
                    op0=ALU.mult, op1=ALU.add),
                     reads=[("accS", w), ("rz2", w), ("o1", w)], writes=[("o2", w)])
            for (j, tq, qr, w) in info:
                P.op("pool", lambda e, qr=qr, w=w: e.tensor_tensor(out=o3[w][0:qr, :], in0=o2[w][0:qr, :], in1=o2[w][0:qr, :],
                                                                   op=ALU.mult),
                     reads=[("o2", w)], writes=[("o3", w)])
                P.op("dve", lambda e, qr=qr, w=w: e.tensor_reduce(out=rz[w][0:qr, 3:4], in_=o3[w][0:qr, :], axis=AX.X, op=ALU.add),
                     reads=[("o3", w)], writes=[("rz3", w)])
                P.op("dve", lambda e, qr=qr, w=w: e.tensor_scalar(out=rz[w][0:qr, 3:4], in0=rz[w][0:qr, 3:4], scalar1=1.0 / 128,
                                                                  scalar2=EPS, op0=ALU.mult, op1=ALU.add),
                     reads=[("rz3", w)], writes=[("rz3", w)])
                P.op("pool", lambda e, qr=qr, w=w: e.tensor_tensor(out=rz[w][0:qr, 3:4], in0=rz[w][0:qr, 3:4], in1=mhalf[0:qr, 0:1], op=ALU.pow),
                     reads=[("rz3", w)], writes=[("rz3", w)])
                P.op("dve", lambda e, qr=qr, w=w, l=l: e.scalar_tensor_tensor(
                    out=o3[w][0:qr, :], in0=o2[w][0:qr, :], scalar=rz[w][0:qr, 3:4], in1=slg_b[0:qr, l, :],
                    op0=ALU.mult, op1=ALU.mult),
                     reads=[("o2", w), ("rz3", w), "slg_b", ("o3", w)], writes=[("o3", w)])
                P.op("pool", lambda e, qr=qr, w=w: e.tensor_tensor(out=ya[w][0:qr, :], in0=o3[w][0:qr, :],
                                                                   in1=gl[w][0:qr, :], op=ALU.mult),
                     reads=[("o3", w), ("gl", w)], writes=[("ya", w)])
                P.op("sp", lambda e, tq=tq, qr=qr, w=w: e.dma_start(
                    out=Y_d[tq:tq + qr, 512 + h * 128:512 + (h + 1) * 128], in_=ya[w][0:qr, :]),
                     reads=[("ya", w)], writes=[("YdA", tq, h)], dma="ya%d" % w)

        qk_exp(0)
        if len(steps) > 1:
            qk_exp(1)
        for i in range(len(steps)):
            if i + 2 < len(steps):
                qk_exp(i + 2)
            av(i)
            if steps[i][2] == NT - 1:
                post(i)
        P.barrier()
        if debug == "p2":
            return finish()

        P.op("sp", lambda e: e.dma_start(out=T2[:, :, :, :], in_=t2_d), writes=["T2"], dma="T2")
        hall_loads((2, 3))
        groups = [(a, mb) for a in range(16) for mb in range(NK2)]

        def rows_ap(base, a, mb, r, c0, c1):
            t_ = base[(mb * 128) * 16 + a:(mb * 128) * 16 + a + 1, c0:c1]
            return bass.AP(t_.tensor, t_.offset, [[16 * D, r], [1, c1 - c0]])

        def loads3(gi):
            a, mb = groups[gi]
            r = k2rows[mb]
            s3 = gi % NCS
            P.op("sp", lambda e, r=r, s3=s3, src=rows_ap(G_d, a, mb, r, 0, 512): e.dma_start(out=gt[s3][0:r, :], in_=src),
                 reads=[("Gd", t) for t in range(NT)], writes=[("gt", s3)], dma="gt%d" % s3)

        loads3(0)
        loads3(1)
        for gi, (a, mb) in enumerate(groups):
            r = k2rows[mb]
            s3 = gi % NCS
            s = gi % 2
            if gi + 2 < len(groups):
                loads3(gi + 2)
            if not last:
                half = len(groups) // 2
                for k in range(8):
                    if half + k * (len(groups) - half) // 8 == gi:
                        P.op("pool", lambda e, l=l, k=k: e.dma_start(out=Wsb[:, k, :], in_=Wp_d[l + 1, :, k, :]),
                             reads=[("Wp", l + 1)],
                             writes=[("WsbK", k)] + [("Hall", kt_, qd_) for kt_ in range(NK2) for qd_ in (0, 1)],
                             dma="WsbK%d" % k)
            bank = gi % 4
            n = 0
            for kt in range(NK2):
                kr = k2rows[kt]
                for x in range(2):
                    P.op("pe", lambda e, kt=kt, kr=kr, x=x, r=r, mb=mb, a=a, bank=bank, n=n: e.matmul(
                        ps[bank][0:r, :], lhsT=T2[0:kr, kt, x, mb * 128:mb * 128 + r], rhs=HallQ[a // 4][0:kr, kt, a % 4, x * 512:(x + 1) * 512],
                        start=(n == 0), stop=(n == 2 * NK2 - 1)),
                         reads=[("Hall", kt, a // 4), "T2"], writes=[("ps", bank)])
                    n += 1
            P.op("dve", lambda e, r=r, s=s, s3=s3, bank=bank: e.tensor_tensor(out=yf[s][0:r, :], in0=ps[bank][0:r, :], in1=gt[s3][0:r, :],
                                                                              op=ALU.mult),
                 reads=[("ps", bank), ("gt", s3)], writes=[("yf", s)])
            P.op("sp", lambda e, r=r, s=s, dst=rows_ap(Y_d, a, mb, r, 0, 512): e.dma_start(out=dst, in_=yf[s][0:r, :]),
                 reads=[("yf", s)], writes=[("YdF", gi)], dma="yf%d" % s)
        P.barrier()
        if debug == "p3":
            return finish()

        ntile4 = NT
        def loads4(t):
            r = _rows(t, NT)
            s = t % 2
            P.op("sp", lambda e, t=t, r=r, s=s: e.dma_start(out=yt[s][0:r, :], in_=Y_d[t * 128:t * 128 + r, :]),
                 reads=[("YdF", gi) for gi in range(16 * NK2)], writes=[("yt", s)], dma="yt%d" % s)
            P.op("sp", lambda e, r=r, s=s, src=res_src(t, r): e.dma_start(out=xr[s][0:r, :], in_=src),
                 reads=[("res", t)], writes=[("xr", s)], dma="xr%d" % s)

        def trans4(t):
            r = _rows(t, NT)
            s = t % 2
            for k in range(8):
                P.op("pe", lambda e, r=r, s=s, k=k: e.transpose(out=psb(7)[:, k * 128:k * 128 + r],
                                                                in_=yt[s][0:r, k * 128:(k + 1) * 128],
                                                                identity=ident_b[0:r, 0:r]),
                     reads=[("yt", s), "ident_b"], writes=[("ps", 7)])
            P.op("act", lambda e, r=r, s=s: e.activation(out=yT[s][:, :, 0:r],
                                                         in_=psb(7)[:, :].rearrange("p (k m) -> p k m", k=8)[:, :, 0:r], func=AF.Copy),
                 reads=[("ps", 7)], writes=[("yT", s)])

        loads4(0)
        if ntile4 > 1:
            loads4(1)
        trans4(0)
        for t in range(ntile4):
            r = _rows(t, NT)
            s = t % 2
            if t + 1 < ntile4:
                trans4(t + 1)
            for c in range(2):
                bank = (t % 2) * 2 + c
                for k in range(8):
                    P.op("pe", lambda e, r=r, s=s, k=k, c=c, bank=bank: e.matmul(
                        ps[bank][0:r, :], lhsT=yT[s][:, k, 0:r], rhs=Wo[:, k, c * 512:(c + 1) * 512],
                        start=(k == 0), stop=(k == 7)),
                         reads=[("yT", s), "Wo"], writes=[("ps", bank)])
                P.op("dve", lambda e, r=r, s=s, c=c, bank=bank: e.tensor_tensor(
                    out=ot[s][0:r, c * 512:(c + 1) * 512], in0=ps[bank][0:r, :], in1=xr[s][0:r, c * 512:(c + 1) * 512], op=ALU.add),
                     reads=[("ps", bank), ("xr", s)], writes=[("ot", s, c)])
            dst = y_d[t * 128:t * 128 + r, :] if last else res_d[t * 128:t * 128 + r, :]
            o = P.op("act", lambda e, r=r, s=s, dst=dst: e.dma_start(out=dst, in_=ot[s][0:r, :]),
                     reads=[("ot", s, 0), ("ot", s, 1)], writes=[("res", t)], dma="ot%d" % s)
            if t + 2 < ntile4:
                loads4(t + 2)
        P.barrier()
    return finish()


def _slot_pos(SEQ):
    L = SEQ + NMETA
    N2 = L // 16
    assert L == 16 * N2 and N2 % 16 == 1
    sidx = np.arange(L)
    n2, n1 = sidx // 16, sidx % 16
    return (N2 * n1 + 16 * n2) % L


def _host_tables(SEQ, DEPTH):
    NT = SEQ // 128 + 1
    LP = NT * 128
    L = SEQ + NMETA
    N2 = L // 16
    NK2 = (N2 + 127) // 128
    pos = np.zeros(LP, dtype=np.int64)
    pos[:L] = _slot_pos(SEQ)
    rot = HD // 4
    inv = (np.float32(500000.0) ** (-np.arange(0, rot, 2, dtype=np.float32) / np.float32(rot))).astype(np.float32)
    ang = pos.astype(np.float32)[:, None] * inv[None, :]
    c, s = np.cos(ang).astype(np.float32), np.sin(ang).astype(np.float32)
    tab = np.concatenate([c, c, -s, s], axis=1).astype(np.float32)
    rope = np.ascontiguousarray(tab.reshape(NT, 128, 32).transpose(1, 0, 2))
    jj = np.arange(128)
    a = 2.0 * np.pi * ((jj[:, None] * jj[None, :]) % 128) / 128.0
    sc = 1.0 / math.sqrt(L * 128.0)
    chan = np.concatenate([np.cos(a) * sc, np.sin(a) * sc], axis=1).astype(np.float32)
    j16 = np.arange(16)
    a16 = 2.0 * np.pi * ((j16[:, None] * j16[None, :]) % 16) / 16.0
    eye8 = np.eye(8)
    bd = np.stack([np.kron(eye8, np.cos(a16)), np.kron(eye8, np.sin(a16)), -np.kron(eye8, np.sin(a16))], axis=1)
    bd = np.ascontiguousarray(bd).astype(ml_dtypes.bfloat16)
    jn = np.arange(N2)
    an = 2.0 * np.pi * ((16 * jn[:, None] * jn[None, :]) % N2) / N2
    t2full = np.zeros((NK2 * 128, 2, N2 + 1), dtype=np.float64)
    t2full[:N2, 0, :N2] = np.cos(an)
    t2full[:N2, 1, :N2] = -np.sin(an)
    t2 = np.ascontiguousarray(t2full.reshape(NK2, 128, 2, N2 + 1).transpose(1, 0, 2, 3)).astype(ml_dtypes.bfloat16)
    linit = np.array([[0.8 - 0.6 * math.exp(-0.3 * li) for li in range(DEPTH)]
                      + [1.0 - (0.8 - 0.6 * math.exp(-0.3 * li)) for li in range(DEPTH)]], dtype=np.float32)
    return rope, chan, bd, t2, linit


_CACHE = {}


def run(inputs, SEQ, DEPTH, n_cores, debug=None):
    key = (SEQ, DEPTH, debug)
    if key not in _CACHE:
        _CACHE[key] = (build_nc(SEQ, DEPTH, debug), _host_tables(SEQ, DEPTH))
    nc, (rope, chan, bd, t2, linit) = _CACHE[key]
    spos = _slot_pos(SEQ)
    f = lambda a: np.ascontiguousarray(np.asarray(a, dtype=np.float32))
    x = f(inputs["x"])
    meta = f(inputs["meta_tokens"])
    ng = np.ascontiguousarray(f(inputs["norm_gain"]).reshape(DEPTH, 8, 128).transpose(0, 2, 1))
    lam4 = np.ascontiguousarray(np.stack([f(inputs["lambda_q1"]), f(inputs["lambda_k1"]),
                                          f(inputs["lambda_q2"]), f(inputs["lambda_k2"])], axis=0))
    shared = {
        "ng": ng, "w_in": f(inputs["w_in"]), "w_f": f(inputs["w_fourier"]),
        "gq": f(inputs["q_norm_gain"]), "gk": f(inputs["k_norm_gain"]), "lam4": lam4, "subln": f(inputs["subln_gain"]),
        "w_out": f(inputs["w_out"]), "rope": rope, "chan": chan, "bd": bd, "t2": t2, "linit": linit,
    }
    in_maps = []
    for i in range(n_cores):
        m = dict(shared)
        full = np.concatenate([meta, x[i]], axis=0)
        m["x"] = np.ascontiguousarray(full[spos])
        in_maps.append(m)
    res = run_bass_kernel_spmd(nc, in_maps, core_ids=list(range(n_cores)))
    if debug:
        return res.results
    outs = []
    for r in res.results:
        ys = np.asarray(r["y"], dtype=np.float32)
        full = np.empty_like(ys)
        full[spos] = ys
        outs.append(full[NMETA:])
    return np.stack(outs, axis=0)


def kernel(x, meta_tokens, norm_gain, w_in, w_fourier, q_norm_gain, k_norm_gain,
           lambda_q1, lambda_k1, lambda_q2, lambda_k2, subln_gain, w_out):
    inputs = dict(x=x, meta_tokens=meta_tokens, norm_gain=norm_gain, w_in=w_in, w_fourier=w_fourier,
                  q_norm_gain=q_norm_gain, k_norm_gain=k_norm_gain, lambda_q1=lambda_q1, lambda_k1=lambda_k1,
                  lambda_q2=lambda_q2, lambda_k2=lambda_k2, subln_gain=subln_gain, w_out=w_out)
    x = np.asarray(x)
    return run(inputs, x.shape[1], np.asarray(w_in).shape[0], x.shape[0])
```

```python
import math
import numpy as np
import ml_dtypes
import concourse.bass as bass
import concourse.mybir as mybir
from concourse.bass_utils import run_bass_kernel_spmd

F32 = mybir.dt.float32
BF16 = mybir.dt.bfloat16
AF = mybir.ActivationFunctionType
ALU = mybir.AluOpType
AX = mybir.AxisListType

D = 1024
NMETA = 16
HD = 64
NH = 4
EPS = 1e-6
WCOLS = 3584
SAME_ENG_SYNC = True


class Prog:
    ENG = ("pe", "act", "dve", "pool", "sp")

    def __init__(self, nc):
        self.nc = nc
        self.ops = {e: [] for e in self.ENG}
        self.lastw = {}
        self.readers = {}
        self.dma_count = {}
        self.dma_since = {}

    def op(self, eng, fn, reads=(), writes=(), dma=None):
        o = dict(eng=eng, fn=fn, deps=[], dma=dma, inc=False)
        if dma is not None:
            self.dma_count[dma] = self.dma_count.get(dma, 0) + 1
            o["dma_val"] = 16 * self.dma_count[dma]
            self.dma_since[dma] = o
        for k in reads:
            w = self.lastw.get(k)
            if w is not None:
                self._dep(o, w)
        for k in writes:
            w = self.lastw.get(k)
            if w is not None:
                self._dep(o, w)
            for r in self.readers.get(k, {}).values():
                self._dep(o, r)
        for k in reads:
            rk = ("d", dma) if dma is not None else ("e", eng)
            self.readers.setdefault(k, {})[rk] = o
        for k in writes:
            self.lastw[k] = o
            self.readers[k] = {}
        self.ops[eng].append(o)
        return o

    def _dep(self, o, p):
        if p is o:
            return
        if p["dma"] is None and o["dma"] is None and p["eng"] == o["eng"]:
            if o["eng"] in ("pe", "sp") or not SAME_ENG_SYNC:
                return
        if p["dma"] is None:
            p["inc"] = True
        o["deps"].append(p)

    def barrier(self):
        last = {e: (self.ops[e][-1] if self.ops[e] else None) for e in self.ENG}
        dmas = list(self.dma_since.values())
        for e in self.ENG:
            o = dict(eng=e, fn=None, deps=[], dma=None, inc=False)
            for e2 in self.ENG:
                p = last[e2]
                if p is not None and e2 != e and e2 != "sp":
                    if p["fn"] is None:
                        continue
                    p["inc"] = True
                    o["deps"].append(p)
            for p in dmas:
                o["deps"].append(p)
            self.ops[e].append(o)
        self.lastw = {}
        self.readers = {}
        self.dma_since = {}

    def emit_all(self):
        nc = self.nc
        sems = {}
        for e in ("pe", "act", "dve", "pool"):
            sems[e] = nc.alloc_semaphore(name="prog_" + e)
        dsem = {}
        for k in self.dma_count:
            dsem[k] = nc.alloc_semaphore(name="dma_" + str(k))
        for e in ("pe", "act", "dve", "pool"):
            c = 0
            for o in self.ops[e]:
                if o["inc"] and o["fn"] is not None and o["dma"] is None:
                    c += 1
                    o["cnt"] = c
        ops = self.ops

        def emit(ename, eng):
            waited = {}
            for o in ops[ename]:
                for p in o["deps"]:
                    if p["dma"] is not None:
                        s, v = dsem[p["dma"]], p["dma_val"]
                    else:
                        s, v = sems[p["eng"]], p["cnt"]
                    key = id(s)
                    if waited.get(key, 0) >= v:
                        continue
                    waited[key] = v
                    eng.wait_ge(s, v)
                if o["fn"] is None:
                    continue
                ins = o["fn"](eng)
                if o["dma"] is not None:
                    ins.then_inc(dsem[o["dma"]], 16)
                elif o["inc"]:
                    ins.then_inc(sems[ename], 1)

        with nc.Block() as block:
            @block.tensor
            def _(e):
                emit("pe", e)

            @block.scalar
            def _(e):
                emit("act", e)

            @block.vector
            def _(e):
                emit("dve", e)

            @block.gpsimd
            def _(e):
                emit("pool", e)

            @block.sync
            def _(e):
                emit("sp", e)


def _rows(t, nt):
    return 128 if t < nt - 1 else NMETA


def fr(ap, dims):
    return bass.AP(ap.tensor, ap.offset, [list(ap.ap[0])] + [list(d) for d in dims])


def fro(ap, off, dims):
    return bass.AP(ap.tensor, ap.offset + off, [list(ap.ap[0])] + [list(d) for d in dims])


def build_nc(SEQ, DEPTH, debug=None):
    NT = SEQ // 128 + 1
    LP = NT * 128
    nc = bass.Bass("TRN2", target_bir_lowering=False)
    P = Prog(nc)

    def din(name, shape, dt=F32):
        return nc.dram_tensor(name, list(shape), dt, kind="ExternalInput").ap()

    def dscr(name, shape, dt):
        return nc.dram_tensor(name, list(shape), dt, kind="ExternalOutput" if debug else "Internal").ap()

    def finish():
        if debug:
            print("ops:", {e: len(v) for e, v in P.ops.items()}, "dma sems:", len(P.dma_count))
        P.emit_all()
        return nc

    L = SEQ + NMETA
    N2 = L // 16
    NK2 = (N2 + 127) // 128
    k2rows = [min(128, N2 - 128 * i) for i in range(NK2)]
    x_d = din("x", [L, D])
    ng_d = din("ng", [DEPTH, 128, 8])
    win_d = din("w_in", [DEPTH, D, 3072])
    wf_d = din("w_f", [DEPTH, 4, 128, 128])
    gq_d = din("gq", [DEPTH, HD])
    gk_d = din("gk", [DEPTH, HD])
    lam_d = din("lam4", [4, DEPTH, HD])
    sl_d = din("subln", [DEPTH, 128])
    wout_d = din("w_out", [DEPTH, D, D])
    rope_d = din("rope", [128, NT, 32])
    chan_d = din("chan", [128, 256])
    bd_d = din("bd", [128, 3, 128], BF16)
    t2_d = din("t2", [128, NK2, 2, N2 + 1], BF16)
    linit_d = din("linit", [1, 2 * DEPTH])
    y_d = nc.dram_tensor("y", [L, D], F32, kind="ExternalOutput").ap()

    Wp_d = dscr("Wp", [DEPTH, 128, 8, WCOLS], BF16)
    Wo_d = dscr("Wo", [DEPTH, 128, 8, D], BF16)
    res_d = dscr("res", [LP, D], F32)
    P_d = dscr("Pd", [LP, D], BF16)
    G_d = dscr("Gd", [LP, D], BF16)
    Y_d = dscr("Yd", [LP, D], BF16)

    base0 = (nc.sbuf_base + 63) // 64 * 64
    top = nc.sbuf_top
    cur = [base0]
    cnt = [0]

    def alloc(shape, dt):
        nbytes = int(np.prod(shape)) * (4 if dt == F32 else 2)
        off = cur[0]
        cur[0] = (off + nbytes + 63) // 64 * 64
        assert cur[0] <= top, ("SBUF overflow", cur[0], top)
        cnt[0] += 1
        return nc.alloc_sbuf_tensor_at("t%d" % cnt[0], [128] + list(shape), dt, offset=off)

    psall = nc.alloc_psum_tensor("psall", [128, 4096], F32)
    ps = [psall[:, i * 512:(i + 1) * 512] for i in range(8)]

    def psb(i):
        return ps[i].bitcast(BF16)

    ident_b = alloc([128], BF16)
    gq_b = alloc([DEPTH, HD], F32)
    gk_b = alloc([DEPTH, HD], F32)
    slg_b = alloc([DEPTH, 128], F32)
    lsum = alloc([2 * DEPTH], F32)
    linit = alloc([2 * DEPTH], F32)
    neglam = alloc([DEPTH], F32)
    nshift = alloc([DEPTH], F32)
    gmax = alloc([2 * DEPTH], F32)
    rope = alloc([NT, 32], F32)
    bd = alloc([3, 128], BF16)
    mhalf = alloc([8], F32)
    const_end = cur[0]
    ident_f = alloc([128], F32)
    chan = alloc([256], F32)
    identsrc = alloc([128], F32)
    lamv = alloc([4, DEPTH, HD], F32)
    lprod = alloc([2, DEPTH, HD], F32)
    p0_start = cur[0]

    def bcast_rows(ap2d):
        return bass.AP(ap2d.tensor, ap2d.offset, [[0, 128]] + [list(d) for d in ap2d.ap[1:]])

    def mk_ident(e):
        return e.memset(identsrc[:, :], 0.0)
    P.op("pool", mk_ident, writes=["identsrc"])
    P.op("pool", lambda e: e.iota(identsrc[:, :], [[1, 128]], channel_multiplier=-1,
                                  allow_small_or_imprecise_dtypes=True),
         writes=["identsrc"])
    P.op("dve", lambda e: e.tensor_single_scalar(ident_f[:, :], identsrc[:, :], 0.0, ALU.is_equal),
         reads=["identsrc"], writes=["ident_f"])
    P.op("dve", lambda e: e.tensor_copy(ident_b[:, :], ident_f[:, :]), reads=["ident_f"], writes=["ident_b"])

    P.op("pool", lambda e: e.memset(mhalf[:, :], -0.5), writes=["mhalf"])

    def cload(dst, src, key, q="sp"):
        P.op(q, lambda e: e.dma_start(out=dst, in_=src), writes=[key], dma="c_" + key)

    cload(gq_b[:, :, :], bcast_rows(gq_d.rearrange("(o l) d -> o l d", o=1)), "gq_b")
    cload(gk_b[:, :, :], bcast_rows(gk_d.rearrange("(o l) d -> o l d", o=1)), "gk_b")
    cload(slg_b[:, :, :], bcast_rows(sl_d.rearrange("(o l) d -> o l d", o=1)), "slg_b")
    cload(lamv[:, :, :, :], bcast_rows(lam_d.rearrange("(o f) l d -> o f l d", o=1)), "lamv")
    cload(linit[:, :], bcast_rows(linit_d), "linit")
    cload(rope[:, :, :], rope_d, "rope")
    cload(chan[:, :], chan_d, "chan")
    cload(bd[:, :, :], bd_d, "bd")

    P.op("dve", lambda e: e.tensor_tensor(out=lprod[:, 0, :, :], in0=lamv[:, 0, :, :], in1=lamv[:, 1, :, :], op=ALU.mult),
         reads=["lamv"], writes=["lprod0"])
    P.op("dve", lambda e: e.tensor_tensor(out=lprod[:, 1, :, :], in0=lamv[:, 2, :, :], in1=lamv[:, 3, :, :], op=ALU.mult),
         reads=["lamv"], writes=["lprod1"])
    P.op("dve", lambda e: e.tensor_reduce(out=lsum[:, :], in_=lprod[:, :, :, :].rearrange("p a l d -> p (a l) d"),
                                          axis=AX.X, op=ALU.add),
         reads=["lprod0", "lprod1"], writes=["lsum"])
    P.op("act", lambda e: e.activation(out=lsum[:, :], in_=lsum[:, :], func=AF.Exp), reads=["lsum"], writes=["lsum"])
    P.op("dve", lambda e: e.tensor_tensor(out=neglam[:, :], in0=lsum[:, DEPTH:2 * DEPTH], in1=lsum[:, 0:DEPTH], op=ALU.subtract),
         reads=["lsum"], writes=["neglam"])
    P.op("dve", lambda e: e.tensor_tensor(out=neglam[:, :], in0=neglam[:, :], in1=linit[:, 0:DEPTH], op=ALU.subtract),
         reads=["neglam", "linit"], writes=["neglam"])
    P.op("dve", lambda e: e.tensor_reduce(out=gmax[:, 0:DEPTH], in_=gq_b[:, :, :], axis=AX.X, op=ALU.max,
                                          apply_absolute_value=True),
         reads=["gq_b"], writes=["gmaxq"])
    P.op("dve", lambda e: e.tensor_reduce(out=gmax[:, DEPTH:2 * DEPTH], in_=gk_b[:, :, :], axis=AX.X, op=ALU.max,
                                          apply_absolute_value=True),
         reads=["gk_b"], writes=["gmaxk"])
    P.op("dve", lambda e: e.scalar_tensor_tensor(out=nshift[:, :], in0=gmax[:, 0:DEPTH], scalar=-8.0,
                                                 in1=gmax[:, DEPTH:2 * DEPTH], op0=ALU.mult, op1=ALU.mult),
         reads=["gmaxq", "gmaxk"], writes=["nshift"])
    P.op("dve", lambda e: e.tensor_scalar(out=gq_b[:, :, :], in0=gq_b[:, :, :], scalar1=HD ** -0.5, scalar2=None, op0=ALU.mult),
         reads=["gq_b", "gmaxq"], writes=["gq_b"])
    for l in range(DEPTH):
        P.op("dve", lambda e, l=l: e.tensor_scalar(out=slg_b[:, l, :], in0=slg_b[:, l, :],
                                                   scalar1=linit[:, DEPTH + l:DEPTH + l + 1], scalar2=None, op0=ALU.mult),
             reads=["slg_b", "linit"], writes=["slg_b"])

    cur[0] = const_end
    qT = alloc([NH, LP], BF16)
    kT = alloc([NH, LP], BF16)
    vall = alloc([NT, NH, 130], BF16)
    qkv_end = cur[0]
    Wsb = alloc([8, WCOLS], BF16)
    w_end = cur[0]

    cur[0] = p0_start
    ngt = alloc([DEPTH, 8], F32)
    wst = [alloc([3072], F32) for _ in range(2)]
    wpb = [alloc([2560], BF16) for _ in range(2)]
    wcs = [alloc([1024], BF16) for _ in range(2)]
    wfT = [alloc([128], F32) for _ in range(2)]
    MM = alloc([4, 256], F32)
    wfl = alloc([4, 128], F32)
    wos = [alloc([1024], F32) for _ in range(2)]
    wob = [alloc([1024], BF16) for _ in range(2)]
    early_wsb = cur[0] <= qkv_end
    P.op("sp", lambda e: e.dma_start(out=ngt[:, :, :], in_=ng_d.rearrange("l p k -> p l k")), writes=["ngt"], dma="ngt")
    def loads0(l, kt):
        s = kt % 2
        P.op("sp", lambda e, l=l, kt=kt, s=s: e.dma_start(out=wst[s][:, :], in_=win_d[l, kt * 128:(kt + 1) * 128, :]),
             writes=[("wst", s)], dma="wst%d" % s)
        P.op("pool", lambda e, l=l, kt=kt, s=s: e.dma_start(out=wos[s][:, :], in_=wout_d[l, kt * 128:(kt + 1) * 128, :]),
             writes=[("wos", s)], dma="wos%d" % s)

    for l in range(DEPTH):
        P.op("sp", lambda e, l=l: e.dma_start(out=wfl[:, :, :], in_=wf_d[l].rearrange("g c e -> c g e")),
             writes=["wfl"], dma="wfl")
        for g in range(4):
            for cs in range(2):
                P.op("pe", lambda e, g=g, cs=cs: e.matmul(ps[g][:, cs * 128:(cs + 1) * 128], lhsT=chan[:, cs * 128:(cs + 1) * 128],
                                                          rhs=wfl[:, g, :], start=True, stop=True),
                     reads=["chan", "wfl"], writes=[("ps", g, cs)])
            P.op("dve", lambda e, g=g: e.tensor_copy(MM[:, g, :], ps[g][:, 0:256]),
                 reads=[("ps", g, 0), ("ps", g, 1)], writes=[("MM", g)])
        if early_wsb and l == min(1, DEPTH - 1):
            P.op("pool", lambda e: e.dma_start(out=Wsb[:, :, :], in_=Wp_d[0]), reads=[("Wp", 0)], writes=["Wsb"], dma="Wsb")
        for kt in range(8):
            s = kt % 2
            if l == 0 and kt == 0:
                loads0(0, 0)
            nxt = l * 8 + kt + 1
            if nxt < DEPTH * 8:
                loads0(nxt // 8, nxt % 8)
            sc = ngt[:, l, kt:kt + 1]
            P.op("act", lambda e, s=s, sc=sc: e.activation(out=wpb[s][:, 0:1280], in_=wst[s][:, 512:1792], func=AF.Copy, scale=sc),
                 reads=[("wst", s), "ngt"], writes=[("wpb", s, 0)])
            P.op("dve", lambda e, s=s, sc=sc: e.tensor_scalar(out=wpb[s][:, 1280:2560], in0=wst[s][:, 1792:3072], scalar1=sc,
                                                              scalar2=None, op0=ALU.mult),
                 reads=[("wst", s), "ngt"], writes=[("wpb", s, 1)])
            P.op("sp", lambda e, l=l, kt=kt, s=s: e.dma_start(out=Wp_d[l, :, kt, 1024:WCOLS], in_=wpb[s][:, :]),
                 reads=[("wpb", s, 0), ("wpb", s, 1)], writes=[("Wp", l)], dma="wpbst%d" % s)
            P.op("dve", lambda e, s=s, sc=sc: e.tensor_scalar(out=wst[s][:, 0:512], in0=wst[s][:, 0:512], scalar1=sc,
                                                              scalar2=None, op0=ALU.mult),
                 reads=[("wst", s), "ngt"], writes=[("wst", s)])
            def tr0(g, s=s):
                s2 = g % 2
                P.op("pe", lambda e, s=s, g=g, s2=s2: e.transpose(out=ps[4 + s2][:, 0:128], in_=wst[s][:, g * 128:(g + 1) * 128],
                                                                 identity=ident_f[:, :]),
                     reads=[("wst", s), "ident_f"], writes=[("psT", s2)])
                P.op("act", lambda e, s2=s2: e.activation(out=wfT[s2][:, :], in_=ps[4 + s2][:, 0:128], func=AF.Copy),
                     reads=[("psT", s2)], writes=[("wfT", s2)])

            tr0(0)
            for g in range(4):
                s2 = g % 2
                if g + 1 < 4:
                    tr0(g + 1)
                P.op("pe", lambda e, s2=s2, g=g: e.matmul(ps[6 + s2][:, 0:256], lhsT=wfT[s2][:, :], rhs=MM[:, g, :],
                                                          start=True, stop=True),
                     reads=[("wfT", s2), ("MM", g)], writes=[("psM", s2)])
                P.op("dve", lambda e, s=s, s2=s2, g=g: e.tensor_copy(
                    fro(wcs[s][:, :], g * 128, [[512, 2], [1, 128]]),
                    ps[6 + s2][:, 0:256].rearrange("p (a b) -> p a b", a=2)),
                     reads=[("psM", s2)], writes=[("wcs", s, g)])
            P.op("sp", lambda e, l=l, kt=kt, s=s: e.dma_start(out=Wp_d[l, :, kt, 0:1024], in_=wcs[s][:, :]),
                 reads=[("wcs", s, g) for g in range(4)], writes=[("Wp", l)], dma="wcsst%d" % s)
            P.op("act", lambda e, s=s: e.activation(out=wob[s][:, :], in_=wos[s][:, :], func=AF.Copy),
                 reads=[("wos", s)], writes=[("wob", s)])
            P.op("sp", lambda e, l=l, kt=kt, s=s: e.dma_start(out=Wo_d[l, :, kt, :], in_=wob[s][:, :]),
                 reads=[("wob", s)], writes=[("Wo", l)], dma="wobst%d" % s)
    P.barrier()
    if debug == "p0":
        return finish()


    cur[0] = w_end
    xt = [alloc([D], F32) for _ in range(2)]
    hb = [alloc([D], BF16) for _ in range(2)]
    hT = [alloc([8, 128], BF16) for _ in range(2)]
    pst = [alloc([D], BF16) for _ in range(2)]
    hst = [alloc([D], BF16) for _ in range(2)]
    sg = [alloc([D], BF16) for _ in range(2)]
    tmp1 = [alloc([512], F32) for _ in range(2)]
    tmp3 = [alloc([512], F32) for _ in range(2)]
    qb = [alloc([512], BF16) for _ in range(2)]
    ssx = [alloc([4], F32) for _ in range(2)]
    ss8 = [alloc([8], F32) for _ in range(2)]
    tA = [alloc([128], F32) for _ in range(2)]
    tB = [alloc([128], F32) for _ in range(2)]

    cur[0] = w_end
    NPT = 3
    PT = [alloc([1024], BF16) for _ in range(NPT)]
    gl = [alloc([128], BF16) for _ in range(4)]
    rz = [alloc([4], F32) for _ in range(4)]
    o1 = [alloc([128], F32) for _ in range(4)]
    o2 = [alloc([128], F32) for _ in range(4)]
    o3 = [alloc([128], F32) for _ in range(4)]
    ya = [alloc([128], BF16) for _ in range(4)]
    accS = [alloc([258], F32) for _ in range(4)]
    p2_end = cur[0]
    cur[0] = qkv_end
    HallQ = [alloc([NK2, 4, D], BF16) for _ in range(2)]
    assert cur[0] <= w_end
    cur[0] = p2_end
    Wo = alloc([8, D], BF16)
    wo_end = cur[0]
    NCS = 3
    need3 = 2 * NK2 * 4 * D * 2 + (NCS + 2) * 1024 + NK2 * 2 * (N2 + 1) * 2 + 1024
    cur[0] = const_end if const_end + need3 <= qkv_end else wo_end
    HallQ += [alloc([NK2, 4, D], BF16) for _ in range(2)]
    gt = [alloc([512], BF16) for _ in range(NCS)]
    yf = [alloc([512], BF16) for _ in range(2)]
    T2 = alloc([NK2, 2, N2 + 1], BF16)
    p3_end = cur[0]
    need4 = 2 * (2 * D + 2 * D + 4 * D + 4 * D) + 512
    cur[0] = const_end if const_end + need4 <= qkv_end else p3_end
    yt = [alloc([D], BF16) for _ in range(2)]
    yT = [alloc([8, 128], BF16) for _ in range(2)]
    xr = [alloc([D], F32) for _ in range(2)]
    ot = [alloc([D], F32) for _ in range(2)]
    for l in range(DEPTH):
        last = (l == DEPTH - 1)

        def res_src(t, r, l=l):
            if l == 0:
                return x_d[t * 128:t * 128 + r, :]
            return res_d[t * 128:t * 128 + r, :]

        if l == 0 and not early_wsb:
            P.op("sp", lambda e: e.dma_start(out=Wsb[:, :, :], in_=Wp_d[0]), reads=[("Wp", 0)], writes=["Wsb"], dma="Wsb")
        P.op("pool", lambda e: e.memset(vall[:, :, :, 128:130], 1.0), writes=["vones"])

        deferred = []
        deferredC = []
        deferredCP = []

        def flushC():
            for f in deferredC:
                f()
            del deferredC[:]

        def flush_qkT(which=None):
            flushC()
            todo = [d_ for d_ in deferred if which is None or d_[1] == which]
            for d_ in todo:
                deferred.remove(d_)
            for (r_, u_, t_, dstT_) in todo:
                for h in range(NH):
                    P.op("pe", lambda e, r=r_, u=u_, h=h: e.transpose(out=psb(6)[:, u * 512 + h * 128:u * 512 + h * 128 + r],
                                                                    in_=qb[u][0:r, h * 128:(h + 1) * 128],
                                                                    identity=ident_b[0:r, 0:r]),
                         reads=[("qb", u_, 0), ("qb", u_, 1), "ident_b"], writes=[("ps", 6)])
                P.op("dve", lambda e, r=r_, u=u_, t=t_, dstT=dstT_: e.tensor_copy(
                    dstT[:, :, t * 128:t * 128 + r],
                    psb(6)[:, u * 512:(u + 1) * 512].rearrange("p (h m) -> p h m", h=NH)[:, :, 0:r]),
                     reads=[("ps", 6)], writes=[("qkT", u_, t_)])

        def head_a(t):
            r = _rows(t, NT)
            s = t % 2
            P.op("sp", lambda e, r=r, s=s, src=res_src(t, r): e.dma_start(out=xt[s][0:r, :], in_=src),
                 reads=[("res", t)], writes=[("xt", s)], dma="xt%d" % s)
            P.op("act", lambda e, r=r, s=s: e.activation(out=hb[s][0:r, :], in_=xt[s][0:r, :], func=AF.Square,
                                                         accum_out=ssx[s][0:r, 0:1]),
                 reads=[("xt", s)], writes=[("hb", s), ("ssx", s)])
            P.op("dve", lambda e, r=r, s=s: e.tensor_scalar(out=ssx[s][0:r, 1:2], in0=ssx[s][0:r, 0:1], scalar1=1.0 / D,
                                                            scalar2=EPS, op0=ALU.mult, op1=ALU.add),
                 reads=[("ssx", s)], writes=[("ssx1", s)])
            P.op("pool", lambda e, r=r, s=s: e.tensor_tensor(out=ssx[s][0:r, 2:3], in0=ssx[s][0:r, 1:2], in1=mhalf[0:r, 0:1], op=ALU.pow),
                 reads=[("ssx1", s)], writes=[("rstd", s)])
            P.op("act", lambda e, r=r, s=s: e.activation(out=hb[s][0:r, :], in_=xt[s][0:r, :], func=AF.Copy,
                                                         scale=ssx[s][0:r, 2:3]),
                 reads=[("xt", s), ("rstd", s)], writes=[("hb", s)])

        def head_b(t):
            r = _rows(t, NT)
            s = t % 2
            for k in range(8):
                P.op("pe", lambda e, r=r, s=s, k=k: e.transpose(out=psb(7)[:, k * 128:k * 128 + r],
                                                                in_=hb[s][0:r, k * 128:(k + 1) * 128],
                                                                identity=ident_b[0:r, 0:r]),
                     reads=[("hb", s), "ident_b"], writes=[("ps", 7)])
            P.op("dve", lambda e, r=r, s=s: e.tensor_copy(hT[s][:, :, 0:r],
                                                          psb(7)[:, :].rearrange("p (k m) -> p k m", k=8)[:, :, 0:r]),
                 reads=[("ps", 7)], writes=[("hT", s)])


        head_a(0)
        head_b(0)
        if NT > 1:
            head_a(1)
        for t in range(NT):
            r = _rows(t, NT)
            s = t % 2
            flushC()
            if t + 2 < NT:
                head_a(t + 2)

            def proj(c, bank, r=r, s=s):
                for k in range(8):
                    P.op("pe", lambda e, k=k: e.matmul(ps[bank][0:r, :], lhsT=hT[s][:, k, 0:r],
                                                       rhs=Wsb[:, k, c * 512:(c + 1) * 512], start=(k == 0), stop=(k == 7)),
                         reads=[("hT", s), "Wsb"], writes=[("ps", bank)])

            proj(0, 0)
            proj(1, 1)
            P.op("act", lambda e, r=r, s=s: e.activation(out=pst[s][0:r, 0:512], in_=ps[0][0:r, :], func=AF.Copy),
                 reads=[("ps", 0)], writes=[("pst", s, 0)])
            P.op("dve", lambda e, r=r, s=s: e.tensor_copy(pst[s][0:r, 512:1024], ps[1][0:r, :]),
                 reads=[("ps", 1)], writes=[("pst", s, 1)])
            flush_qkT(0)
            if t + 1 < NT:
                head_b(t + 1)
            flush_qkT(1)
            for qi, (c, bank, gb, dstT) in enumerate(((2, 2, gq_b, qT), (3, 3, gk_b, kT))):
                proj(c, bank)
                u = qi
                P.op("act", lambda e, r=r, bank=bank, u=u: e.activation(out=tmp1[u][0:r, :], in_=ps[bank][0:r, :], func=AF.Square),
                     reads=[("ps", bank)], writes=[("tmp1", u)])
                P.op("dve", lambda e, r=r, u=u: e.tensor_reduce(out=ss8[u][0:r, :],
                                                                in_=tmp1[u][0:r, :].rearrange("p (b d) -> p b d", b=8),
                                                                axis=AX.X, op=ALU.add),
                     reads=[("tmp1", u)], writes=[("ss8", u)])
                P.op("dve", lambda e, r=r, u=u: e.tensor_scalar(out=ss8[u][0:r, :], in0=ss8[u][0:r, :], scalar1=1.0 / HD,
                                                                scalar2=EPS, op0=ALU.mult, op1=ALU.add),
                     reads=[("ss8", u)], writes=[("ss8", u)])
                P.op("pool", lambda e, r=r, u=u: e.tensor_tensor(out=ss8[u][0:r, :], in0=ss8[u][0:r, :], in1=mhalf[0:r, 0:8], op=ALU.pow),
                     reads=[("ss8", u)], writes=[("ss8", u)])
                P.op("dve", lambda e, r=r, u=u, bank=bank: e.tensor_tensor(
                    out=tmp1[u][0:r, :].rearrange("p (b d) -> p b d", b=8),
                    in0=ps[bank][0:r, :].rearrange("p (b d) -> p b d", b=8),
                    in1=fr(ss8[u][0:r, :], [[1, 8], [0, HD]]), op=ALU.mult),
                     reads=[("ps", bank), ("ss8", u)], writes=[("tmp1", u)])
                def cpart(r=r, u=u, t=t, gb=gb):
                    gsl = gb[0:r, l, :]
                    P.op("pool", lambda e, r=r, u=u, gsl=gsl: e.tensor_tensor(
                        out=tmp3[u][0:r, :].rearrange("p (b d) -> p b d", b=8),
                        in0=tmp1[u][0:r, :].rearrange("p (b d) -> p b d", b=8),
                        in1=fr(gsl, [[0, 8], [1, HD]]), op=ALU.mult),
                         reads=[("tmp1", u), "gq_b", "gk_b"], writes=[("tmp3", u)])
                    t3 = tmp3[u][0:r, :]
                    rp = rope[0:r, t, :]
                    P.op("pool", lambda e, r=r, u=u, t3=t3, rp=rp: e.tensor_tensor(
                        out=tA[u][0:r, :].rearrange("p (b d) -> p b d", b=8),
                        in0=fr(t3, [[HD, 8], [1, 16]]), in1=fr(rp, [[0, 8], [1, 16]]), op=ALU.mult),
                         reads=[("tmp3", u), "rope"], writes=[("tA", u)])
                    P.op("pool", lambda e, r=r, u=u, t3=t3, rp=rp: e.tensor_tensor(
                        out=fr(tB[u][0:r, :], [[16, 8], [1, 8]]),
                        in0=fro(t3, 8, [[HD, 8], [1, 8]]),
                        in1=fro(rp, 16, [[0, 8], [1, 8]]), op=ALU.mult),
                         reads=[("tmp3", u), "rope"], writes=[("tB", u, 0)])
                    P.op("pool", lambda e, r=r, u=u, t3=t3, rp=rp: e.tensor_tensor(
                        out=fro(tB[u][0:r, :], 8, [[16, 8], [1, 8]]),
                        in0=fr(t3, [[HD, 8], [1, 8]]),
                        in1=fro(rp, 24, [[0, 8], [1, 8]]), op=ALU.mult),
                         reads=[("tmp3", u), "rope"], writes=[("tB", u, 1)])
                    P.op("pool", lambda e, r=r, u=u: e.tensor_tensor(
                        out=fr(qb[u][0:r, :], [[HD, 8], [1, 16]]),
                        in0=tA[u][0:r, :].rearrange("p (b d) -> p b d", b=8),
                        in1=tB[u][0:r, :].rearrange("p (b d) -> p b d", b=8), op=ALU.add),
                         reads=[("tA", u), ("tB", u, 0), ("tB", u, 1)], writes=[("qb", u, 0)])
                deferredCP.append(cpart)

                def cpart_act(r=r, u=u):
                    t3 = tmp3[u][0:r, :]
                    P.op("act", lambda e, r=r, u=u, t3=t3: e.activation(
                        out=fro(qb[u][0:r, :], 16, [[HD, 8], [1, HD - 16]]), in_=fro(t3, 16, [[HD, 8], [1, HD - 16]]), func=AF.Copy),
                         reads=[("tmp3", u)], writes=[("qb", u, 1)])
                deferredC.append(cpart_act)
                deferred.append((r, u, t, dstT))
            proj(4, 4)
            for (bank, (m0, m1)) in ((0, (0, 2)), (1, (1, 0))):
                P.op("pe", lambda e, r=r, s=s, bank=bank, m0=m0: e.matmul(ps[bank][0:r, :], lhsT=bd[0:r, m0, 0:r], rhs=pst[s][0:r, 0:512],
                                                                         start=True, stop=False),
                     reads=[("pst", s, 0), "bd"], writes=[("ps", bank)])
                P.op("pe", lambda e, r=r, s=s, bank=bank, m1=m1: e.matmul(ps[bank][0:r, :], lhsT=bd[0:r, m1, 0:r], rhs=pst[s][0:r, 512:1024],
                                                                         start=False, stop=True),
                     reads=[("pst", s, 1), "bd"], writes=[("ps", bank)])
            P.op("dve", lambda e, r=r, s=s: e.tensor_copy(hst[s][0:r, 0:512], ps[0][0:r, :]),
                 reads=[("ps", 0)], writes=[("hst", s, 0)])
            P.op("dve", lambda e, r=r, s=s: e.tensor_copy(hst[s][0:r, 512:1024], ps[1][0:r, :]),
                 reads=[("ps", 1)], writes=[("hst", s, 1)])
            P.op("act", lambda e, r=r, t=t: e.activation(out=vall[0:r, t, :, 0:128],
                                                         in_=ps[4][0:r, :].rearrange("p (h d) -> p h d", h=NH), func=AF.Copy),
                 reads=[("ps", 4)], writes=[("v", t)])
            proj(5, 5)
            proj(6, 4)
            P.op("act", lambda e, r=r, s=s: e.activation(out=sg[s][0:r, 0:512], in_=ps[5][0:r, :], func=AF.Silu),
                 reads=[("ps", 5)], writes=[("sg", s, 0)])
            P.op("act", lambda e, r=r, s=s: e.activation(out=sg[s][0:r, 512:1024], in_=ps[4][0:r, :], func=AF.Silu),
                 reads=[("ps", 4)], writes=[("sg", s, 1)])
            P.op("act", lambda e, t=t, r=r, s=s: e.dma_start(out=G_d[t * 128:t * 128 + r, :], in_=sg[s][0:r, :]),
                 reads=[("sg", s, 0), ("sg", s, 1)], writes=[("Gd", t)], dma="sg%d" % s)
            P.op("act", lambda e, t=t, r=r, s=s: e.dma_start(out=P_d[t * 128:t * 128 + r, :], in_=hst[s][0:r, :]),
                 reads=[("hst", s, 0), ("hst", s, 1)], writes=[("Pd", t)], dma="hst%d" % s)
            for f in deferredCP:
                f()
            del deferredCP[:]
        flush_qkT()
        P.barrier()
        if debug == "p1":
            return finish()

        def hall_loads(quarters):
            for qd in quarters:
                for kt in range(NK2):
                    kr = k2rows[kt]
                    src = P_d[kt * 2048:kt * 2048 + kr * 16, :].rearrange("(p a) c -> p a c", a=16)[:, qd * 4:qd * 4 + 4, :]
                    P.op("sp" if qd % 2 == 0 else "pool", lambda e, kt=kt, kr=kr, qd=qd, src=src: e.dma_start(
                        out=HallQ[qd][0:kr, kt, :, :], in_=src),
                         reads=[("Pd", t) for t in range(NT)], writes=[("Hall", kt, qd)], dma="Hall%d_%d" % (kt, qd))

        hall_loads((0, 1))
        P.op("sp", lambda e, l=l: e.dma_start(out=Wo[:, :, :], in_=Wo_d[l]), reads=[("Wo", l)], writes=["Wo"], dma="WoL")

        nch = (L + 511) // 512
        bounds = [(L * ci) // nch for ci in range(nch + 1)]
        chunks = []
        for ci in range(nch):
            q0c, qnc = bounds[ci], bounds[ci + 1] - bounds[ci]
            chunks.append((q0c, qnc, [(qo, min(128, qnc - qo)) for qo in range(0, qnc, 128)]))
        steps = [(h, ch, kt) for h in range(NH) for ch in chunks for kt in range(NT)]
        pp = [0]

        def qk_exp(i):
            h, (q0, qn, qtiles), kt = steps[i]
            kr = _rows(kt, NT)
            sl = i % NPT
            sb = (i % 2) * 2
            qtok = list(range(q0 // 128, (q0 + qn - 1) // 128 + 1))
            for c in range(2):
                P.op("pe", lambda e, c=c: e.matmul(
                    ps[sb + c][0:kr, 0:qn], lhsT=kT[c * 64:(c + 1) * 64, h, kt * 128:kt * 128 + kr],
                    rhs=qT[c * 64:(c + 1) * 64, h, q0:q0 + qn], start=True, stop=True),
                     reads=[("qkT", 1, kt)] + [("qkT", 0, t) for t in qtok], writes=[("ps", sb + c)])
            P.op("act", lambda e, l=l: e.activation(
                out=fr(PT[sl][0:kr, :], [[512, 2], [1, qn]]), in_=fr(ps[sb][0:kr, :], [[512, 2], [1, qn]]), func=AF.Exp,
                bias=nshift[0:kr, l:l + 1], scale=1.0),
                 reads=[("ps", sb), ("ps", sb + 1), "nshift"], writes=[("PT", sl)])

        def av(i):
            h, (q0, qn, qtiles), kt = steps[i]
            kr = _rows(kt, NT)
            sl = i % NPT
            for j, (qo, qr) in enumerate(qtiles):
                for c in range(2):
                    P.op("pe", lambda e, c=c, j=j, qr=qr, qo=qo: e.matmul(
                        ps[4 + j][0:qr, c * 129:(c + 1) * 129], lhsT=PT[sl][0:kr, c * 512 + qo:c * 512 + qo + qr],
                        rhs=vall[0:kr, kt, h, 0:129], start=(kt == 0 and c == 0), stop=(kt == NT - 1 and c == 1),
                        skip_group_check=True),
                         reads=[("PT", sl), ("v", kt), "vones"], writes=[("acc", j)])

        def post(i):
            h, (q0, qn, qtiles), kt = steps[i]
            info = []
            for j, (qo, qr) in enumerate(qtiles):
                tq = q0 + qo
                w = pp[0] % 4
                pp[0] += 1
                info.append((j, tq, qr, w))
                P.op("dve", lambda e, qr=qr, j=j, w=w: e.tensor_copy(accS[w][0:qr, :], ps[4 + j][0:qr, 0:258]),
                     reads=[("acc", j)], writes=[("accS", w)])
            for (j, tq, qr, w) in info:
                acc = accS[w]
                P.op("sp", lambda e, tq=tq, qr=qr, w=w: e.dma_start(
                    out=gl[w][0:qr, :], in_=G_d[tq:tq + qr, 512 + h * 128:512 + (h + 1) * 128]),
                     reads=[("Gd", t_) for t_ in range(NT)], writes=[("gl", w)], dma="gl%d" % w)
                P.op("dve", lambda e, qr=qr, acc=acc, w=w: e.reciprocal(
                    out=rz[w][0:qr, 0:2], in_=fr(acc[0:qr, 128:129], [[129, 2]])),
                     reads=[("accS", w)], writes=[("rz", w)])
                P.op("dve", lambda e, qr=qr, w=w, l=l: e.tensor_tensor(out=rz[w][0:qr, 2:3], in0=rz[w][0:qr, 1:2],
                                                                       in1=neglam[0:qr, l:l + 1], op=ALU.mult),
                     reads=[("rz", w), "neglam"], writes=[("rz2", w)])
                P.op("dve", lambda e, qr=qr, acc=acc, w=w: e.tensor_scalar(
                    out=o1[w][0:qr, :], in0=acc[0:qr, 0:128], scalar1=rz[w][0:qr, 0:1], scalar2=None, op0=ALU.mult),
                     reads=[("accS", w), ("rz", w)], writes=[("o1", w)])
                P.op("dve", lambda e, qr=qr, acc=acc, w=w: e.scalar_tensor_tensor(
                    out=o2[w][0:qr, :], in0=acc[0:qr, 129:257], scalar=rz[w][0:qr, 2:3], in1=o1[w][0:qr, :],
                    op0=ALU.mult, op1=ALU.add),
                     reads=[("accS", w), ("rz2", w), ("o1", w)], writes=[("o2", w)])
            for (j, tq, qr, w) in info:
                P.op("pool", lambda e, qr=qr, w=w: e.tensor_tensor(out=o3[w][0:qr, :], in0=o2[w][0:qr, :], in1=o2[w][0:qr, :],
                                                                   op=ALU.mult),
                     reads=[("o2", w)], writes=[("o3", w)])
                P.op("dve", lambda e, qr=qr, w=w: e.tensor_reduce(out=rz[w][0:qr, 3:4], in_=o3[w][0:qr, :], axis=AX.X, op=ALU.add),
                     reads=[("o3", w)], writes=[("rz3", w)])
                P.op("dve", lambda e, qr=qr, w=w: e.tensor_scalar(out=rz[w][0:qr, 3:4], in0=rz[w][0:qr, 3:4], scalar1=1.0 / 128,
                                                                  scalar2=EPS, op0=ALU.mult, op1=ALU.add),
                     reads=[("rz3", w)], writes=[("rz3", w)])
                P.op("pool", lambda e, qr=qr, w=w: e.tensor_tensor(out=rz[w][0:qr, 3:4], in0=rz[w][0:qr, 3:4], in1=mhalf[0:qr, 0:1], op=ALU.pow),
                     reads=[("rz3", w)], writes=[("rz3", w)])
                P.op("dve", lambda e, qr=qr, w=w, l=l: e.scalar_tensor_tensor(
                    out=o3[w][0:qr, :], in0=o2[w][0:qr, :], scalar=rz[w][0:qr, 3:4], in1=slg_b[0:qr, l, :],
                    op0=ALU.mult, op1=ALU.mult),
                     reads=[("o2", w), ("rz3", w), "slg_b", ("o3", w)], writes=[("o3", w)])
                P.op("pool", lambda e, qr=qr, w=w: e.tensor_tensor(out=ya[w][0:qr, :], in0=o3[w][0:qr, :],
                                                                   in1=gl[w][0:qr, :], op=ALU.mult),
                     reads=[("o3", w), ("gl", w)], writes=[("ya", w)])
                P.op("sp", lambda e, tq=tq, qr=qr, w=w: e.dma_start(
                    out=Y_d[tq:tq + qr, 512 + h * 128:512 + (h + 1) * 128], in_=ya[w][0:qr, :]),
                     reads=[("ya", w)], writes=[("YdA", tq, h)], dma="ya%d" % w)

        qk_exp(0)
        if len(steps) > 1:
            qk_exp(1)
        for i in range(len(steps)):
            if i + 2 < len(steps):
                qk_exp(i + 2)
            av(i)
            if steps[i][2] == NT - 1:
                post(i)
        P.barrier()
        if debug == "p2":
            return finish()

        P.op("sp", lambda e: e.dma_start(out=T2[:, :, :, :], in_=t2_d), writes=["T2"], dma="T2")
        hall_loads((2, 3))
        groups = [(a, mb) for a in range(16) for mb in range(NK2)]

        def rows_ap(base, a, mb, r, c0, c1):
            t_ = base[(mb * 128) * 16 + a:(mb * 128) * 16 + a + 1, c0:c1]
            return bass.AP(t_.tensor, t_.offset, [[16 * D, r], [1, c1 - c0]])

        def loads3(gi):
            a, mb = groups[gi]
            r = k2rows[mb]
            s3 = gi % NCS
            P.op("sp", lambda e, r=r, s3=s3, src=rows_ap(G_d, a, mb, r, 0, 512): e.dma_start(out=gt[s3][0:r, :], in_=src),
                 reads=[("Gd", t) for t in range(NT)], writes=[("gt", s3)], dma="gt%d" % s3)

        loads3(0)
        loads3(1)
        for gi, (a, mb) in enumerate(groups):
            r = k2rows[mb]
            s3 = gi % NCS
            s = gi % 2
            if gi + 2 < len(groups):
                loads3(gi + 2)
            if not last:
                half = len(groups) // 2
                for k in range(8):
                    if half + k * (len(groups) - half) // 8 == gi:
                        P.op("pool", lambda e, l=l, k=k: e.dma_start(out=Wsb[:, k, :], in_=Wp_d[l + 1, :, k, :]),
                             reads=[("Wp", l + 1)],
                             writes=[("WsbK", k)] + [("Hall", kt_, qd_) for kt_ in range(NK2) for qd_ in (0, 1)],
                             dma="WsbK%d" % k)
            bank = gi % 4
            n = 0
            for kt in range(NK2):
                kr = k2rows[kt]
                for x in range(2):
                    P.op("pe", lambda e, kt=kt, kr=kr, x=x, r=r, mb=mb, a=a, bank=bank, n=n: e.matmul(
                        ps[bank][0:r, :], lhsT=T2[0:kr, kt, x, mb * 128:mb * 128 + r], rhs=HallQ[a // 4][0:kr, kt, a % 4, x * 512:(x + 1) * 512],
                        start=(n == 0), stop=(n == 2 * NK2 - 1)),
                         reads=[("Hall", kt, a // 4), "T2"], writes=[("ps", bank)])
                    n += 1
            P.op("dve", lambda e, r=r, s=s, s3=s3, bank=bank: e.tensor_tensor(out=yf[s][0:r, :], in0=ps[bank][0:r, :], in1=gt[s3][0:r, :],
                                                                              op=ALU.mult),
                 reads=[("ps", bank), ("gt", s3)], writes=[("yf", s)])
            P.op("sp", lambda e, r=r, s=s, dst=rows_ap(Y_d, a, mb, r, 0, 512): e.dma_start(out=dst, in_=yf[s][0:r, :]),
                 reads=[("yf", s)], writes=[("YdF", gi)], dma="yf%d" % s)
        P.barrier()
        if debug == "p3":
            return finish()

        ntile4 = NT
        def loads4(t):
            r = _rows(t, NT)
            s = t % 2
            P.op("sp", lambda e, t=t, r=r, s=s: e.dma_start(out=yt[s][0:r, :], in_=Y_d[t * 128:t * 128 + r, :]),
                 reads=[("YdF", gi) for gi in range(16 * NK2)], writes=[("yt", s)], dma="yt%d" % s)
            P.op("sp", lambda e, r=r, s=s, src=res_src(t, r): e.dma_start(out=xr[s][0:r, :], in_=src),
                 reads=[("res", t)], writes=[("xr", s)], dma="xr%d" % s)

        def trans4(t):
            r = _rows(t, NT)
            s = t % 2
            for k in range(8):
                P.op("pe", lambda e, r=r, s=s, k=k: e.transpose(out=psb(7)[:, k * 128:k * 128 + r],
                                                                in_=yt[s][0:r, k * 128:(k + 1) * 128],
                                                                identity=ident_b[0:r, 0:r]),
                     reads=[("yt", s), "ident_b"], writes=[("ps", 7)])
            P.op("act", lambda e, r=r, s=s: e.activation(out=yT[s][:, :, 0:r],
                                                         in_=psb(7)[:, :].rearrange("p (k m) -> p k m", k=8)[:, :, 0:r], func=AF.Copy),
                 reads=[("ps", 7)], writes=[("yT", s)])

        loads4(0)
        if ntile4 > 1:
            loads4(1)
        trans4(0)
        for t in range(ntile4):
            r = _rows(t, NT)
            s = t % 2
            if t + 1 < ntile4:
                trans4(t + 1)
            for c in range(2):
                bank = (t % 2) * 2 + c
                for k in range(8):
                    P.op("pe", lambda e, r=r, s=s, k=k, c=c, bank=bank: e.matmul(
                        ps[bank][0:r, :], lhsT=yT[s][:, k, 0:r], rhs=Wo[:, k, c * 512:(c + 1) * 512],
                        start=(k == 0), stop=(k == 7)),
                         reads=[("yT", s), "Wo"], writes=[("ps", bank)])
                P.op("dve", lambda e, r=r, s=s, c=c, bank=bank: e.tensor_tensor(
                    out=ot[s][0:r, c * 512:(c + 1) * 512], in0=ps[bank][0:r, :], in1=xr[s][0:r, c * 512:(c + 1) * 512], op=ALU.add),
                     reads=[("ps", bank), ("xr", s)], writes=[("ot", s, c)])
            dst = y_d[t * 128:t * 128 + r, :] if last else res_d[t * 128:t * 128 + r, :]
            o = P.op("act", lambda e, r=r, s=s, dst=dst: e.dma_start(out=dst, in_=ot[s][0:r, :]),
                     reads=[("ot", s, 0), ("ot", s, 1)], writes=[("res", t)], dma="ot%d" % s)
            if t + 2 < ntile4:
                loads4(t + 2)
        P.barrier()
    return finish()


def _slot_pos(SEQ):
    L = SEQ + NMETA
    N2 = L // 16
    assert L == 16 * N2 and N2 % 16 == 1
    sidx = np.arange(L)
    n2, n1 = sidx // 16, sidx % 16
    return (N2 * n1 + 16 * n2) % L


def _host_tables(SEQ, DEPTH):
    NT = SEQ // 128 + 1
    LP = NT * 128
    L = SEQ + NMETA
    N2 = L // 16
    NK2 = (N2 + 127) // 128
    pos = np.zeros(LP, dtype=np.int64)
    pos[:L] = _slot_pos(SEQ)
    rot = HD // 4
    inv = (np.float32(500000.0) ** (-np.arange(0, rot, 2, dtype=np.float32) / np.float32(rot))).astype(np.float32)
    ang = pos.astype(np.float32)[:, None] * inv[None, :]
    c, s = np.cos(ang).astype(np.float32), np.sin(ang).astype(np.float32)
    tab = np.concatenate([c, c, -s, s], axis=1).astype(np.float32)
    rope = np.ascontiguousarray(tab.reshape(NT, 128, 32).transpose(1, 0, 2))
    jj = np.arange(128)
    a = 2.0 * np.pi * ((jj[:, None] * jj[None, :]) % 128) / 128.0
    sc = 1.0 / math.sqrt(L * 128.0)
    chan = np.concatenate([np.cos(a) * sc, np.sin(a) * sc], axis=1).astype(np.float32)
    j16 = np.arange(16)
    a16 = 2.0 * np.pi * ((j16[:, None] * j16[None, :]) % 16) / 16.0
    eye8 = np.eye(8)
    bd = np.stack([np.kron(eye8, np.cos(a16)), np.kron(eye8, np.sin(a16)), -np.kron(eye8, np.sin(a16))], axis=1)
    bd = np.ascontiguousarray(bd).astype(ml_dtypes.bfloat16)
    jn = np.arange(N2)
    an = 2.0 * np.pi * ((16 * jn[:, None] * jn[None, :]) % N2) / N2
    t2full = np.zeros((NK2 * 128, 2, N2 + 1), dtype=np.float64)
    t2full[:N2, 0, :N2] = np.cos(an)
    t2full[:N2, 1, :N2] = -np.sin(an)
    t2 = np.ascontiguousarray(t2full.reshape(NK2, 128, 2, N2 + 1).transpose(1, 0, 2, 3)).astype(ml_dtypes.bfloat16)
    linit = np.array([[0.8 - 0.6 * math.exp(-0.3 * li) for li in range(DEPTH)]
                      + [1.0 - (0.8 - 0.6 * math.exp(-0.3 * li)) for li in range(DEPTH)]], dtype=np.float32)
    return rope, chan, bd, t2, linit


_CACHE = {}


def run(inputs, SEQ, DEPTH, n_cores, debug=None):
    key = (SEQ, DEPTH, debug)
    if key not in _CACHE:
        _CACHE[key] = (build_nc(SEQ, DEPTH, debug), _host_tables(SEQ, DEPTH))
    nc, (rope, chan, bd, t2, linit) = _CACHE[key]
    spos = _slot_pos(SEQ)
    f = lambda a: np.ascontiguousarray(np.asarray(a, dtype=np.float32))
    x = f(inputs["x"])
    meta = f(inputs["meta_tokens"])
    ng = np.ascontiguousarray(f(inputs["norm_gain"]).reshape(DEPTH, 8, 128).transpose(0, 2, 1))
    lam4 = np.ascontiguousarray(np.stack([f(inputs["lambda_q1"]), f(inputs["lambda_k1"]),
                                          f(inputs["lambda_q2"]), f(inputs["lambda_k2"])], axis=0))
    shared = {
        "ng": ng, "w_in": f(inputs["w_in"]), "w_f": f(inputs["w_fourier"]),
        "gq": f(inputs["q_norm_gain"]), "gk": f(inputs["k_norm_gain"]), "lam4": lam4, "subln": f(inputs["subln_gain"]),
        "w_out": f(inputs["w_out"]), "rope": rope, "chan": chan, "bd": bd, "t2": t2, "linit": linit,
    }
    in_maps = []
    for i in range(n_cores):
        m = dict(shared)
        full = np.concatenate([meta, x[i]], axis=0)
        m["x"] = np.ascontiguousarray(full[spos])
        in_maps.append(m)
    res = run_bass_kernel_spmd(nc, in_maps, core_ids=list(range(n_cores)))
    if debug:
        return res.results
    outs = []
    for r in res.results:
        ys = np.asarray(r["y"], dtype=np.float32)
        full = np.empty_like(ys)
        full[spos] = ys
        outs.append(full[NMETA:])
    return np.stack(outs, axis=0)


def kernel(x, meta_tokens, norm_gain, w_in, w_fourier, q_norm_gain, k_norm_gain,
           lambda_q1, lambda_k1, lambda_q2, lambda_k2, subln_gain, w_out):
    inputs = dict(x=x, meta_tokens=meta_tokens, norm_gain=norm_gain, w_in=w_in, w_fourier=w_fourier,
                  q_norm_gain=q_norm_gain, k_norm_gain=k_norm_gain, lambda_q1=lambda_q1, lambda_k1=lambda_k1,
                  lambda_q2=lambda_q2, lambda_k2=lambda_k2, subln_gain=subln_gain, w_out=w_out)
    x = np.asarray(x)
    return run(inputs, x.shape[1], np.asarray(w_in).shape[0], x.shape[0])
```

```python
import math
import numpy as np
import ml_dtypes
import concourse.bass as bass
import concourse.mybir as mybir
from concourse.bass_utils import run_bass_kernel_spmd

F32 = mybir.dt.float32
BF16 = mybir.dt.bfloat16
AF = mybir.ActivationFunctionType
ALU = mybir.AluOpType
AX = mybir.AxisListType

D = 1024
NMETA = 16
HD = 64
NH = 4
EPS = 1e-6
WCOLS = 3584
SAME_ENG_SYNC = True


class Prog:
    ENG = ("pe", "act", "dve", "pool", "sp")

    def __init__(self, nc):
        self.nc = nc
        self.ops = {e: [] for e in self.ENG}
        self.lastw = {}
        self.readers = {}
        self.dma_count = {}
        self.dma_since = {}

    def op(self, eng, fn, reads=(), writes=(), dma=None):
        o = dict(eng=eng, fn=fn, deps=[], dma=dma, inc=False)
        if dma is not None:
            self.dma_count[dma] = self.dma_count.get(dma, 0) + 1
            o["dma_val"] = 16 * self.dma_count[dma]
            self.dma_since[dma] = o
        for k in reads:
            w = self.lastw.get(k)
            if w is not None:
                self._dep(o, w)
        for k in writes:
            w = self.lastw.get(k)
            if w is not None:
                self._dep(o, w)
            for r in self.readers.get(k, {}).values():
                self._dep(o, r)
        for k in reads:
            rk = ("d", dma) if dma is not None else ("e", eng)
            self.readers.setdefault(k, {})[rk] = o
        for k in writes:
            self.lastw[k] = o
            self.readers[k] = {}
        self.ops[eng].append(o)
        return o

    def _dep(self, o, p):
        if p is o:
            return
        if p["dma"] is None and o["dma"] is None and p["eng"] == o["eng"]:
            if o["eng"] in ("pe", "sp") or not SAME_ENG_SYNC:
                return
        if p["dma"] is None:
            p["inc"] = True
        o["deps"].append(p)

    def barrier(self):
        last = {e: (self.ops[e][-1] if self.ops[e] else None) for e in self.ENG}
        dmas = list(self.dma_since.values())
        for e in self.ENG:
            o = dict(eng=e, fn=None, deps=[], dma=None, inc=False)
            for e2 in self.ENG:
                p = last[e2]
                if p is not None and e2 != e and e2 != "sp":
                    if p["fn"] is None:
                        continue
                    p["inc"] = True
                    o["deps"].append(p)
            for p in dmas:
                o["deps"].append(p)
            self.ops[e].append(o)
        self.lastw = {}
        self.readers = {}
        self.dma_since = {}

    def emit_all(self):
        nc = self.nc
        sems = {}
        for e in ("pe", "act", "dve", "pool"):
            sems[e] = nc.alloc_semaphore(name="prog_" + e)
        dsem = {}
        for k in self.dma_count:
            dsem[k] = nc.alloc_semaphore(name="dma_" + str(k))
        for e in ("pe", "act", "dve", "pool"):
            c = 0
            for o in self.ops[e]:
                if o["inc"] and o["fn"] is not None and o["dma"] is None:
                    c += 1
                    o["cnt"] = c
        ops = self.ops

        def emit(ename, eng):
            waited = {}
            for o in ops[ename]:
                for p in o["deps"]:
                    if p["dma"] is not None:
                        s, v = dsem[p["dma"]], p["dma_val"]
                    else:
                        s, v = sems[p["eng"]], p["cnt"]
                    key = id(s)
                    if waited.get(key, 0) >= v:
                        continue
                    waited[key] = v
                    eng.wait_ge(s, v)
                if o["fn"] is None:
                    continue
                ins = o["fn"](eng)
                if o["dma"] is not None:
                    ins.then_inc(dsem[o["dma"]], 16)
                elif o["inc"]:
                    ins.then_inc(sems[ename], 1)

        with nc.Block() as block:
            @block.tensor
            def _(e):
                emit("pe", e)

            @block.scalar
            def _(e):
                emit("act", e)

            @block.vector
            def _(e):
                emit("dve", e)

            @block.gpsimd
            def _(e):
                emit("pool", e)

            @block.sync
            def _(e):
                emit("sp", e)


def _rows(t, nt):
    return 128 if t < nt - 1 else NMETA


def fr(ap, dims):
    return bass.AP(ap.tensor, ap.offset, [list(ap.ap[0])] + [list(d) for d in dims])


def fro(ap, off, dims):
    return bass.AP(ap.tensor, ap.offset + off, [list(ap.ap[0])] + [list(d) for d in dims])


def build_nc(SEQ, DEPTH, debug=None):
    NT = SEQ // 128 + 1
    LP = NT * 128
    nc = bass.Bass("TRN2", target_bir_lowering=False)
    P = Prog(nc)

    def din(name, shape, dt=F32):
        return nc.dram_tensor(name, list(shape), dt, kind="ExternalInput").ap()

    def dscr(name, shape, dt):
        return nc.dram_tensor(name, list(shape), dt, kind="ExternalOutput" if debug else "Internal").ap()

    def finish():
        if debug:
            print("ops:", {e: len(v) for e, v in P.ops.items()}, "dma sems:", len(P.dma_count))
        P.emit_all()
        return nc

    L = SEQ + NMETA
    N2 = L // 16
    NK2 = (N2 + 127) // 128
    k2rows = [min(128, N2 - 128 * i) for i in range(NK2)]
    x_d = din("x", [L, D])
    ng_d = din("ng", [DEPTH, 128, 8])
    win_d = din("w_in", [DEPTH, D, 3072])
    wf_d = din("w_f", [DEPTH, 4, 128, 128])
    gq_d = din("gq", [DEPTH, HD])
    gk_d = din("gk", [DEPTH, HD])
    lam_d = din("lam4", [4, DEPTH, HD])
    sl_d = din("subln", [DEPTH, 128])
    wout_d = din("w_out", [DEPTH, D, D])
    rope_d = din("rope", [128, NT, 32])
    chan_d = din("chan", [128, 256])
    bd_d = din("bd", [128, 3, 128], BF16)
    t2_d = din("t2", [128, NK2, 2, N2 + 1], BF16)
    linit_d = din("linit", [1, 2 * DEPTH])
    y_d = nc.dram_tensor("y", [L, D], F32, kind="ExternalOutput").ap()

    Wp_d = dscr("Wp", [DEPTH, 128, 8, WCOLS], BF16)
    Wo_d = dscr("Wo", [DEPTH, 128, 8, D], BF16)
    res_d = dscr("res", [LP, D], F32)
    P_d = dscr("Pd", [LP, D], BF16)
    G_d = dscr("Gd", [LP, D], BF16)
    Y_d = dscr("Yd", [LP, D], BF16)

    base0 = (nc.sbuf_base + 63) // 64 * 64
    top = nc.sbuf_top
    cur = [base0]
    cnt = [0]

    def alloc(shape, dt):
        nbytes = int(np.prod(shape)) * (4 if dt == F32 else 2)
        off = cur[0]
        cur[0] = (off + nbytes + 63) // 64 * 64
        assert cur[0] <= top, ("SBUF overflow", cur[0], top)
        cnt[0] += 1
        return nc.alloc_sbuf_tensor_at("t%d" % cnt[0], [128] + list(shape), dt, offset=off)

    psall = nc.alloc_psum_tensor("psall", [128, 4096], F32)
    ps = [psall[:, i * 512:(i + 1) * 512] for i in range(8)]

    def psb(i):
        return ps[i].bitcast(BF16)

    ident_b = alloc([128], BF16)
    gq_b = alloc([DEPTH, HD], F32)
    gk_b = alloc([DEPTH, HD], F32)
    slg_b = alloc([DEPTH, 128], F32)
    lsum = alloc([2 * DEPTH], F32)
    linit = alloc([2 * DEPTH], F32)
    neglam = alloc([DEPTH], F32)
    nshift = alloc([DEPTH], F32)
    gmax = alloc([2 * DEPTH], F32)
    rope = alloc([NT, 32], F32)
    bd = alloc([3, 128], BF16)
    mhalf = alloc([8], F32)
    const_end = cur[0]
    ident_f = alloc([128], F32)
    chan = alloc([256], F32)
    identsrc = alloc([128], F32)
    lamv = alloc([4, DEPTH, HD], F32)
    lprod = alloc([2, DEPTH, HD], F32)
    p0_start = cur[0]

    def bcast_rows(ap2d):
        return bass.AP(ap2d.tensor, ap2d.offset, [[0, 128]] + [list(d) for d in ap2d.ap[1:]])

    def mk_ident(e):
        return e.memset(identsrc[:, :], 0.0)
    P.op("pool", mk_ident, writes=["identsrc"])
    P.op("pool", lambda e: e.iota(identsrc[:, :], [[1, 128]], channel_multiplier=-1,
                                  allow_small_or_imprecise_dtypes=True),
         writes=["identsrc"])
    P.op("dve", lambda e: e.tensor_single_scalar(ident_f[:, :], identsrc[:, :], 0.0, ALU.is_equal),
         reads=["identsrc"], writes=["ident_f"])
    P.op("dve", lambda e: e.tensor_copy(ident_b[:, :], ident_f[:, :]), reads=["ident_f"], writes=["ident_b"])

    P.op("pool", lambda e: e.memset(mhalf[:, :], -0.5), writes=["mhalf"])

    def cload(dst, src, key, q="sp"):
        P.op(q, lambda e: e.dma_start(out=dst, in_=src), writes=[key], dma="c_" + key)

    cload(gq_b[:, :, :], bcast_rows(gq_d.rearrange("(o l) d -> o l d", o=1)), "gq_b")
    cload(gk_b[:, :, :], bcast_rows(gk_d.rearrange("(o l) d -> o l d", o=1)), "gk_b")
    cload(slg_b[:, :, :], bcast_rows(sl_d.rearrange("(o l) d -> o l d", o=1)), "slg_b")
    cload(lamv[:, :, :, :], bcast_rows(lam_d.rearrange("(o f) l d -> o f l d", o=1)), "lamv")
    cload(linit[:, :], bcast_rows(linit_d), "linit")
    cload(rope[:, :, :], rope_d, "rope")
    cload(chan[:, :], chan_d, "chan")
    cload(bd[:, :, :], bd_d, "bd")

    P.op("dve", lambda e: e.tensor_tensor(out=lprod[:, 0, :, :], in0=lamv[:, 0, :, :], in1=lamv[:, 1, :, :], op=ALU.mult),
         reads=["lamv"], writes=["lprod0"])
    P.op("dve", lambda e: e.tensor_tensor(out=lprod[:, 1, :, :], in0=lamv[:, 2, :, :], in1=lamv[:, 3, :, :], op=ALU.mult),
         reads=["lamv"], writes=["lprod1"])
    P.op("dve", lambda e: e.tensor_reduce(out=lsum[:, :], in_=lprod[:, :, :, :].rearrange("p a l d -> p (a l) d"),
                                          axis=AX.X, op=ALU.add),
         reads=["lprod0", "lprod1"], writes=["lsum"])
    P.op("act", lambda e: e.activation(out=lsum[:, :], in_=lsum[:, :], func=AF.Exp), reads=["lsum"], writes=["lsum"])
    P.op("dve", lambda e: e.tensor_tensor(out=neglam[:, :], in0=lsum[:, DEPTH:2 * DEPTH], in1=lsum[:, 0:DEPTH], op=ALU.subtract),
         reads=["lsum"], writes=["neglam"])
    P.op("dve", lambda e: e.tensor_tensor(out=neglam[:, :], in0=neglam[:, :], in1=linit[:, 0:DEPTH], op=ALU.subtract),
         reads=["neglam", "linit"], writes=["neglam"])
    P.op("dve", lambda e: e.tensor_reduce(out=gmax[:, 0:DEPTH], in_=gq_b[:, :, :], axis=AX.X, op=ALU.max,
                                          apply_absolute_value=True),
         reads=["gq_b"], writes=["gmaxq"])
    P.op("dve", lambda e: e.tensor_reduce(out=gmax[:, DEPTH:2 * DEPTH], in_=gk_b[:, :, :], axis=AX.X, op=ALU.max,
                                          apply_absolute_value=True),
         reads=["gk_b"], writes=["gmaxk"])
    P.op("dve", lambda e: e.scalar_tensor_tensor(out=nshift[:, :], in0=gmax[:, 0:DEPTH], scalar=-8.0,
                                                 in1=gmax[:, DEPTH:2 * DEPTH], op0=ALU.mult, op1=ALU.mult),
         reads=["gmaxq", "gmaxk"], writes=["nshift"])
    P.op("dve", lambda e: e.tensor_scalar(out=gq_b[:, :, :], in0=gq_b[:, :, :], scalar1=HD ** -0.5, scalar2=None, op0=ALU.mult),
         reads=["gq_b", "gmaxq"], writes=["gq_b"])
    for l in range(DEPTH):
        P.op("dve", lambda e, l=l: e.tensor_scalar(out=slg_b[:, l, :], in0=slg_b[:, l, :],
                                                   scalar1=linit[:, DEPTH + l:DEPTH + l + 1], scalar2=None, op0=ALU.mult),
             reads=["slg_b", "linit"], writes=["slg_b"])

    cur[0] = const_end
    qT = alloc([NH, LP], BF16)
    kT = alloc([NH, LP], BF16)
    vall = alloc([NT, NH, 130], BF16)
    qkv_end = cur[0]
    Wsb = alloc([8, WCOLS], BF16)
    w_end = cur[0]

    cur[0] = p0_start
    ngt = alloc([DEPTH, 8], F32)
    wst = [alloc([3072], F32) for _ in range(2)]
    wpb = [alloc([2560], BF16) for _ in range(2)]
    wcs = [alloc([1024], BF16) for _ in range(2)]
    wfT = [alloc([128], F32) for _ in range(2)]
    MM = alloc([4, 256], F32)
    wfl = alloc([4, 128], F32)
    wos = [alloc([1024], F32) for _ in range(2)]
    wob = [alloc([1024], BF16) for _ in range(2)]
    early_wsb = cur[0] <= qkv_end
    P.op("sp", lambda e: e.dma_start(out=ngt[:, :, :], in_=ng_d.rearrange("l p k -> p l k")), writes=["ngt"], dma="ngt")
    def loads0(l, kt):
        s = kt % 2
        P.op("sp", lambda e, l=l, kt=kt, s=s: e.dma_start(out=wst[s][:, :], in_=win_d[l, kt * 128:(kt + 1) * 128, :]),
             writes=[("wst", s)], dma="wst%d" % s)
        P.op("pool", lambda e, l=l, kt=kt, s=s: e.dma_start(out=wos[s][:, :], in_=wout_d[l, kt * 128:(kt + 1) * 128, :]),
             writes=[("wos", s)], dma="wos%d" % s)

    for l in range(DEPTH):
        P.op("sp", lambda e, l=l: e.dma_start(out=wfl[:, :, :], in_=wf_d[l].rearrange("g c e -> c g e")),
             writes=["wfl"], dma="wfl")
        for g in range(4):
            for cs in range(2):
                P.op("pe", lambda e, g=g, cs=cs: e.matmul(ps[g][:, cs * 128:(cs + 1) * 128], lhsT=chan[:, cs * 128:(cs + 1) * 128],
                                                          rhs=wfl[:, g, :], start=True, stop=True),
                     reads=["chan", "wfl"], writes=[("ps", g, cs)])
            P.op("dve", lambda e, g=g: e.tensor_copy(MM[:, g, :], ps[g][:, 0:256]),
                 reads=[("ps", g, 0), ("ps", g, 1)], writes=[("MM", g)])
        if early_wsb and l == min(1, DEPTH - 1):
            P.op("pool", lambda e: e.dma_start(out=Wsb[:, :, :], in_=Wp_d[0]), reads=[("Wp", 0)], writes=["Wsb"], dma="Wsb")
        for kt in range(8):
            s = kt % 2
            if l == 0 and kt == 0:
                loads0(0, 0)
            nxt = l * 8 + kt + 1
            if nxt < DEPTH * 8:
                loads0(nxt // 8, nxt % 8)
            sc = ngt[:, l, kt:kt + 1]
            P.op("act", lambda e, s=s, sc=sc: e.activation(out=wpb[s][:, 0:1280], in_=wst[s][:, 512:1792], func=AF.Copy, scale=sc),
                 reads=[("wst", s), "ngt"], writes=[("wpb", s, 0)])
            P.op("dve", lambda e, s=s, sc=sc: e.tensor_scalar(out=wpb[s][:, 1280:2560], in0=wst[s][:, 1792:3072], scalar1=sc,
                                                              scalar2=None, op0=ALU.mult),
                 reads=[("wst", s), "ngt"], writes=[("wpb", s, 1)])
            P.op("sp", lambda e, l=l, kt=kt, s=s: e.dma_start(out=Wp_d[l, :, kt, 1024:WCOLS], in_=wpb[s][:, :]),
                 reads=[("wpb", s, 0), ("wpb", s, 1)], writes=[("Wp", l)], dma="wpbst%d" % s)
            P.op("dve", lambda e, s=s, sc=sc: e.tensor_scalar(out=wst[s][:, 0:512], in0=wst[s][:, 0:512], scalar1=sc,
                                                              scalar2=None, op0=ALU.mult),
                 reads=[("wst", s), "ngt"], writes=[("wst", s)])
            def tr0(g, s=s):
                s2 = g % 2
                P.op("pe", lambda e, s=s, g=g, s2=s2: e.transpose(out=ps[4 + s2][:, 0:128], in_=wst[s][:, g * 128:(g + 1) * 128],
                                                                 identity=ident_f[:, :]),
                     reads=[("wst", s), "ident_f"], writes=[("psT", s2)])
                P.op("act", lambda e, s2=s2: e.activation(out=wfT[s2][:, :], in_=ps[4 + s2][:, 0:128], func=AF.Copy),
                     reads=[("psT", s2)], writes=[("wfT", s2)])

            tr0(0)
            for g in range(4):
                s2 = g % 2
                if g + 1 < 4:
                    tr0(g + 1)
                P.op("pe", lambda e, s2=s2, g=g: e.matmul(ps[6 + s2][:, 0:256], lhsT=wfT[s2][:, :], rhs=MM[:, g, :],
                                                          start=True, stop=True),
                     reads=[("wfT", s2), ("MM", g)], writes=[("psM", s2)])
                P.op("dve", lambda e, s=s, s2=s2, g=g: e.tensor_copy(
                    fro(wcs[s][:, :], g * 128, [[512, 2], [1, 128]]),
                    ps[6 + s2][:, 0:256].rearrange("p (a b) -> p a b", a=2)),
                     reads=[("psM", s2)], writes=[("wcs", s, g)])
            P.op("sp", lambda e, l=l, kt=kt, s=s: e.dma_start(out=Wp_d[l, :, kt, 0:1024], in_=wcs[s][:, :]),
                 reads=[("wcs", s, g) for g in range(4)], writes=[("Wp", l)], dma="wcsst%d" % s)
            P.op("act", lambda e, s=s: e.activation(out=wob[s][:, :], in_=wos[s][:, :], func=AF.Copy),
                 reads=[("wos", s)], writes=[("wob", s)])
            P.op("sp", lambda e, l=l, kt=kt, s=s: e.dma_start(out=Wo_d[l, :, kt, :], in_=wob[s][:, :]),
                 reads=[("wob", s)], writes=[("Wo", l)], dma="wobst%d" % s)
    P.barrier()
    if debug == "p0":
        return finish()


    cur[0] = w_end
    xt = [alloc([D], F32) for _ in range(2)]
    hb = [alloc([D], BF16) for _ in range(2)]
    hT = [alloc([8, 128], BF16) for _ in range(2)]
    pst = [alloc([D], BF16) for _ in range(2)]
    hst = [alloc([D], BF16) for _ in range(2)]
    sg = [alloc([D], BF16) for _ in range(2)]
    tmp1 = [alloc([512], F32) for _ in range(2)]
    tmp3 = [alloc([512], F32) for _ in range(2)]
    qb = [alloc([512], BF16) for _ in range(2)]
    ssx = [alloc([4], F32) for _ in range(2)]
    ss8 = [alloc([8], F32) for _ in range(2)]
    tA = [alloc([128], F32) for _ in range(2)]
    tB = [alloc([128], F32) for _ in range(2)]

    cur[0] = w_end
    NPT = 3
    PT = [alloc([1024], BF16) for _ in range(NPT)]
    gl = [alloc([128], BF16) for _ in range(4)]
    rz = [alloc([4], F32) for _ in range(4)]
    o1 = [alloc([128], F32) for _ in range(4)]
    o2 = [alloc([128], F32) for _ in range(4)]
    o3 = [alloc([128], F32) for _ in range(4)]
    ya = [alloc([128], BF16) for _ in range(4)]
    accS = [alloc([258], F32) for _ in range(4)]
    p2_end = cur[0]
    cur[0] = qkv_end
    HallQ = [alloc([NK2, 4, D], BF16) for _ in range(2)]
    assert cur[0] <= w_end
    cur[0] = p2_end
    Wo = alloc([8, D], BF16)
    wo_end = cur[0]
    NCS = 3
    need3 = 2 * NK2 * 4 * D * 2 + (NCS + 2) * 1024 + NK2 * 2 * (N2 + 1) * 2 + 1024
    cur[0] = const_end if const_end + need3 <= qkv_end else wo_end
    HallQ += [alloc([NK2, 4, D], BF16) for _ in range(2)]
    gt = [alloc([512], BF16) for _ in range(NCS)]
    yf = [alloc([512], BF16) for _ in range(2)]
    T2 = alloc([NK2, 2, N2 + 1], BF16)
    p3_end = cur[0]
    need4 = 2 * (2 * D + 2 * D + 4 * D + 4 * D) + 512
    cur[0] = const_end if const_end + need4 <= qkv_end else p3_end
    yt = [alloc([D], BF16) for _ in range(2)]
    yT = [alloc([8, 128], BF16) for _ in range(2)]
    xr = [alloc([D], F32) for _ in range(2)]
    ot = [alloc([D], F32) for _ in range(2)]
    for l in range(DEPTH):
        last = (l == DEPTH - 1)

        def res_src(t, r, l=l):
            if l == 0:
                return x_d[t * 128:t * 128 + r, :]
            return res_d[t * 128:t * 128 + r, :]

        if l == 0 and not early_wsb:
            P.op("sp", lambda e: e.dma_start(out=Wsb[:, :, :], in_=Wp_d[0]), reads=[("Wp", 0)], writes=["Wsb"], dma="Wsb")
        P.op("pool", lambda e: e.memset(vall[:, :, :, 128:130], 1.0), writes=["vones"])

        deferred = []
        deferredC = []
        deferredCP = []

        def flushC():
            for f in deferredC:
                f()
            del deferredC[:]

        def flush_qkT(which=None):
            flushC()
            todo = [d_ for d_ in deferred if which is None or d_[1] == which]
            for d_ in todo:
                deferred.remove(d_)
            for (r_, u_, t_, dstT_) in todo:
                for h in range(NH):
                    P.op("pe", lambda e, r=r_, u=u_, h=h: e.transpose(out=psb(6)[:, u * 512 + h * 128:u * 512 + h * 128 + r],
                                                                    in_=qb[u][0:r, h * 128:(h + 1) * 128],
                                                                    identity=ident_b[0:r, 0:r]),
                         reads=[("qb", u_, 0), ("qb", u_, 1), "ident_b"], writes=[("ps", 6)])
                P.op("dve", lambda e, r=r_, u=u_, t=t_, dstT=dstT_: e.tensor_copy(
                    dstT[:, :, t * 128:t * 128 + r],
                    psb(6)[:, u * 512:(u + 1) * 512].rearrange("p (h m) -> p h m", h=NH)[:, :, 0:r]),
                     reads=[("ps", 6)], writes=[("qkT", u_, t_)])

        def head_a(t):
            r = _rows(t, NT)
            s = t % 2
            P.op("sp", lambda e, r=r, s=s, src=res_src(t, r): e.dma_start(out=xt[s][0:r, :], in_=src),
                 reads=[("res", t)], writes=[("xt", s)], dma="xt%d" % s)
            P.op("act", lambda e, r=r, s=s: e.activation(out=hb[s][0:r, :], in_=xt[s][0:r, :], func=AF.Square,
                                                         accum_out=ssx[s][0:r, 0:1]),
                 reads=[("xt", s)], writes=[("hb", s), ("ssx", s)])
            P.op("dve", lambda e, r=r, s=s: e.tensor_scalar(out=ssx[s][0:r, 1:2], in0=ssx[s][0:r, 0:1], scalar1=1.0 / D,
                                                            scalar2=EPS, op0=ALU.mult, op1=ALU.add),
                 reads=[("ssx", s)], writes=[("ssx1", s)])
            P.op("pool", lambda e, r=r, s=s: e.tensor_tensor(out=ssx[s][0:r, 2:3], in0=ssx[s][0:r, 1:2], in1=mhalf[0:r, 0:1], op=ALU.pow),
                 reads=[("ssx1", s)], writes=[("rstd", s)])
            P.op("act", lambda e, r=r, s=s: e.activation(out=hb[s][0:r, :], in_=xt[s][0:r, :], func=AF.Copy,
                                                         scale=ssx[s][0:r, 2:3]),
                 reads=[("xt", s), ("rstd", s)], writes=[("hb", s)])

        def head_b(t):
            r = _rows(t, NT)
            s = t % 2
            for k in range(8):
                P.op("pe", lambda e, r=r, s=s, k=k: e.transpose(out=psb(7)[:, k * 128:k * 128 + r],
                                                                in_=hb[s][0:r, k * 128:(k + 1) * 128],
                                                                identity=ident_b[0:r, 0:r]),
                     reads=[("hb", s), "ident_b"], writes=[("ps", 7)])
            P.op("dve", lambda e, r=r, s=s: e.tensor_copy(hT[s][:, :, 0:r],
                                                          psb(7)[:, :].rearrange("p (k m) -> p k m", k=8)[:, :, 0:r]),
                 reads=[("ps", 7)], writes=[("hT", s)])


        head_a(0)
        head_b(0)
        if NT > 1:
            head_a(1)
        for t in range(NT):
            r = _rows(t, NT)
            s = t % 2
            flushC()
            if t + 2 < NT:
                head_a(t + 2)

            def proj(c, bank, r=r, s=s):
                for k in range(8):
                    P.op("pe", lambda e, k=k: e.matmul(ps[bank][0:r, :], lhsT=hT[s][:, k, 0:r],
                                                       rhs=Wsb[:, k, c * 512:(c + 1) * 512], start=(k == 0), stop=(k == 7)),
                         reads=[("hT", s), "Wsb"], writes=[("ps", bank)])

            proj(0, 0)
            proj(1, 1)
            P.op("act", lambda e, r=r, s=s: e.activation(out=pst[s][0:r, 0:512], in_=ps[0][0:r, :], func=AF.Copy),
                 reads=[("ps", 0)], writes=[("pst", s, 0)])
            P.op("dve", lambda e, r=r, s=s: e.tensor_copy(pst[s][0:r, 512:1024], ps[1][0:r, :]),
                 reads=[("ps", 1)], writes=[("pst", s, 1)])
            flush_qkT(0)
            if t + 1 < NT:
                head_b(t + 1)
            for qi, (c, bank, gb, dstT) in enumerate(((2, 2, gq_b, qT), (3, 3, gk_b, kT))):
                proj(c, bank)
                if qi == 0:
                    flush_qkT(1)
                u = qi
                P.op("act", lambda e, r=r, bank=bank, u=u: e.activation(out=tmp1[u][0:r, :], in_=ps[bank][0:r, :], func=AF.Square),
                     reads=[("ps", bank)], writes=[("tmp1", u)])
                P.op("dve", lambda e, r=r, u=u: e.tensor_reduce(out=ss8[u][0:r, :],
                                                                in_=tmp1[u][0:r, :].rearrange("p (b d) -> p b d", b=8),
                                                                axis=AX.X, op=ALU.add),
                     reads=[("tmp1", u)], writes=[("ss8", u)])
                P.op("dve", lambda e, r=r, u=u: e.tensor_scalar(out=ss8[u][0:r, :], in0=ss8[u][0:r, :], scalar1=1.0 / HD,
                                                                scalar2=EPS, op0=ALU.mult, op1=ALU.add),
                     reads=[("ss8", u)], writes=[("ss8", u)])
                P.op("pool", lambda e, r=r, u=u: e.tensor_tensor(out=ss8[u][0:r, :], in0=ss8[u][0:r, :], in1=mhalf[0:r, 0:8], op=ALU.pow),
                     reads=[("ss8", u)], writes=[("ss8", u)])
                P.op("dve", lambda e, r=r, u=u, bank=bank: e.tensor_tensor(
                    out=tmp1[u][0:r, :].rearrange("p (b d) -> p b d", b=8),
                    in0=ps[bank][0:r, :].rearrange("p (b d) -> p b d", b=8),
                    in1=fr(ss8[u][0:r, :], [[1, 8], [0, HD]]), op=ALU.mult),
                     reads=[("ps", bank), ("ss8", u)], writes=[("tmp1", u)])
                def cpart(r=r, u=u, t=t, gb=gb):
                    gsl = gb[0:r, l, :]
                    P.op("pool", lambda e, r=r, u=u, gsl=gsl: e.tensor_tensor(
                        out=tmp3[u][0:r, :].rearrange("p (b d) -> p b d", b=8),
                        in0=tmp1[u][0:r, :].rearrange("p (b d) -> p b d", b=8),
                        in1=fr(gsl, [[0, 8], [1, HD]]), op=ALU.mult),
                         reads=[("tmp1", u), "gq_b", "gk_b"], writes=[("tmp3", u)])
                    t3 = tmp3[u][0:r, :]
                    rp = rope[0:r, t, :]
                    P.op("pool", lambda e, r=r, u=u, t3=t3, rp=rp: e.tensor_tensor(
                        out=tA[u][0:r, :].rearrange("p (b d) -> p b d", b=8),
                        in0=fr(t3, [[HD, 8], [1, 16]]), in1=fr(rp, [[0, 8], [1, 16]]), op=ALU.mult),
                         reads=[("tmp3", u), "rope"], writes=[("tA", u)])
                    P.op("pool", lambda e, r=r, u=u, t3=t3, rp=rp: e.tensor_tensor(
                        out=fr(tB[u][0:r, :], [[16, 8], [1, 8]]),
                        in0=fro(t3, 8, [[HD, 8], [1, 8]]),
                        in1=fro(rp, 16, [[0, 8], [1, 8]]), op=ALU.mult),
                         reads=[("tmp3", u), "rope"], writes=[("tB", u, 0)])
                    P.op("pool", lambda e, r=r, u=u, t3=t3, rp=rp: e.tensor_tensor(
                        out=fro(tB[u][0:r, :], 8, [[16, 8], [1, 8]]),
                        in0=fr(t3, [[HD, 8], [1, 8]]),
                        in1=fro(rp, 24, [[0, 8], [1, 8]]), op=ALU.mult),
                         reads=[("tmp3", u), "rope"], writes=[("tB", u, 1)])
                    P.op("pool", lambda e, r=r, u=u: e.tensor_tensor(
                        out=fr(qb[u][0:r, :], [[HD, 8], [1, 16]]),
                        in0=tA[u][0:r, :].rearrange("p (b d) -> p b d", b=8),
                        in1=tB[u][0:r, :].rearrange("p (b d) -> p b d", b=8), op=ALU.add),
                         reads=[("tA", u), ("tB", u, 0), ("tB", u, 1)], writes=[("qb", u, 0)])
                deferredCP.append(cpart)

                def cpart_act(r=r, u=u):
                    t3 = tmp3[u][0:r, :]
                    P.op("act", lambda e, r=r, u=u, t3=t3: e.activation(
                        out=fro(qb[u][0:r, :], 16, [[HD, 8], [1, HD - 16]]), in_=fro(t3, 16, [[HD, 8], [1, HD - 16]]), func=AF.Copy),
                         reads=[("tmp3", u)], writes=[("qb", u, 1)])
                deferredC.append(cpart_act)
                deferred.append((r, u, t, dstT))
            proj(4, 4)
            for (bank, (m0, m1)) in ((0, (0, 2)), (1, (1, 0))):
                P.op("pe", lambda e, r=r, s=s, bank=bank, m0=m0: e.matmul(ps[bank][0:r, :], lhsT=bd[0:r, m0, 0:r], rhs=pst[s][0:r, 0:512],
                                                                         start=True, stop=False),
                     reads=[("pst", s, 0), "bd"], writes=[("ps", bank)])
                P.op("pe", lambda e, r=r, s=s, bank=bank, m1=m1: e.matmul(ps[bank][0:r, :], lhsT=bd[0:r, m1, 0:r], rhs=pst[s][0:r, 512:1024],
                                                                         start=False, stop=True),
                     reads=[("pst", s, 1), "bd"], writes=[("ps", bank)])
            P.op("dve", lambda e, r=r, s=s: e.tensor_copy(hst[s][0:r, 0:512], ps[0][0:r, :]),
                 reads=[("ps", 0)], writes=[("hst", s, 0)])
            P.op("dve", lambda e, r=r, s=s: e.tensor_copy(hst[s][0:r, 512:1024], ps[1][0:r, :]),
                 reads=[("ps", 1)], writes=[("hst", s, 1)])
            P.op("act", lambda e, r=r, t=t: e.activation(out=vall[0:r, t, :, 0:128],
                                                         in_=ps[4][0:r, :].rearrange("p (h d) -> p h d", h=NH), func=AF.Copy),
                 reads=[("ps", 4)], writes=[("v", t)])
            proj(5, 5)
            proj(6, 4)
            P.op("act", lambda e, r=r, s=s: e.activation(out=sg[s][0:r, 0:512], in_=ps[5][0:r, :], func=AF.Silu),
                 reads=[("ps", 5)], writes=[("sg", s, 0)])
            P.op("act", lambda e, r=r, s=s: e.activation(out=sg[s][0:r, 512:1024], in_=ps[4][0:r, :], func=AF.Silu),
                 reads=[("ps", 4)], writes=[("sg", s, 1)])
            P.op("act", lambda e, t=t, r=r, s=s: e.dma_start(out=G_d[t * 128:t * 128 + r, :], in_=sg[s][0:r, :]),
                 reads=[("sg", s, 0), ("sg", s, 1)], writes=[("Gd", t)], dma="sg%d" % s)
            P.op("act", lambda e, t=t, r=r, s=s: e.dma_start(out=P_d[t * 128:t * 128 + r, :], in_=hst[s][0:r, :]),
                 reads=[("hst", s, 0), ("hst", s, 1)], writes=[("Pd", t)], dma="hst%d" % s)
            for f in deferredCP:
                f()
            del deferredCP[:]
        flush_qkT()
        P.barrier()
        if debug == "p1":
            return finish()

        def hall_loads(quarters):
            for qd in quarters:
                for kt in range(NK2):
                    kr = k2rows[kt]
                    src = P_d[kt * 2048:kt * 2048 + kr * 16, :].rearrange("(p a) c -> p a c", a=16)[:, qd * 4:qd * 4 + 4, :]
                    P.op("sp" if qd % 2 == 0 else "pool", lambda e, kt=kt, kr=kr, qd=qd, src=src: e.dma_start(
                        out=HallQ[qd][0:kr, kt, :, :], in_=src),
                         reads=[("Pd", t) for t in range(NT)], writes=[("Hall", kt, qd)], dma="Hall%d_%d" % (kt, qd))

        hall_loads((0, 1))
        P.op("sp", lambda e, l=l: e.dma_start(out=Wo[:, :, :], in_=Wo_d[l]), reads=[("Wo", l)], writes=["Wo"], dma="WoL")

        nch = (L + 511) // 512
        bounds = [(L * ci) // nch for ci in range(nch + 1)]
        chunks = []
        for ci in range(nch):
            q0c, qnc = bounds[ci], bounds[ci + 1] - bounds[ci]
            chunks.append((q0c, qnc, [(qo, min(128, qnc - qo)) for qo in range(0, qnc, 128)]))
        steps = [(h, ch, kt) for h in range(NH) for ch in chunks for kt in range(NT)]
        pp = [0]

        def qk_exp(i):
            h, (q0, qn, qtiles), kt = steps[i]
            kr = _rows(kt, NT)
            sl = i % NPT
            sb = (i % 2) * 2
            qtok = list(range(q0 // 128, (q0 + qn - 1) // 128 + 1))
            for c in range(2):
                P.op("pe", lambda e, c=c: e.matmul(
                    ps[sb + c][0:kr, 0:qn], lhsT=kT[c * 64:(c + 1) * 64, h, kt * 128:kt * 128 + kr],
                    rhs=qT[c * 64:(c + 1) * 64, h, q0:q0 + qn], start=True, stop=True),
                     reads=[("qkT", 1, kt)] + [("qkT", 0, t) for t in qtok], writes=[("ps", sb + c)])
            P.op("act", lambda e, l=l: e.activation(
                out=fr(PT[sl][0:kr, :], [[512, 2], [1, qn]]), in_=fr(ps[sb][0:kr, :], [[512, 2], [1, qn]]), func=AF.Exp,
                bias=nshift[0:kr, l:l + 1], scale=1.0),
                 reads=[("ps", sb), ("ps", sb + 1), "nshift"], writes=[("PT", sl)])

        def av(i):
            h, (q0, qn, qtiles), kt = steps[i]
            kr = _rows(kt, NT)
            sl = i % NPT
            for j, (qo, qr) in enumerate(qtiles):
                for c in range(2):
                    P.op("pe", lambda e, c=c, j=j, qr=qr, qo=qo: e.matmul(
                        ps[4 + j][0:qr, c * 129:(c + 1) * 129], lhsT=PT[sl][0:kr, c * 512 + qo:c * 512 + qo + qr],
                        rhs=vall[0:kr, kt, h, 0:129], start=(kt == 0 and c == 0), stop=(kt == NT - 1 and c == 1),
                        skip_group_check=True),
                         reads=[("PT", sl), ("v", kt), "vones"], writes=[("acc", j)])

        def post(i):
            h, (q0, qn, qtiles), kt = steps[i]
            info = []
            for j, (qo, qr) in enumerate(qtiles):
                tq = q0 + qo
                w = pp[0] % 4
                pp[0] += 1
                info.append((j, tq, qr, w))
                P.op("dve", lambda e, qr=qr, j=j, w=w: e.tensor_copy(accS[w][0:qr, :], ps[4 + j][0:qr, 0:258]),
                     reads=[("acc", j)], writes=[("accS", w)])
            for (j, tq, qr, w) in info:
                acc = accS[w]
                P.op("sp", lambda e, tq=tq, qr=qr, w=w: e.dma_start(
                    out=gl[w][0:qr, :], in_=G_d[tq:tq + qr, 512 + h * 128:512 + (h + 1) * 128]),
                     reads=[("Gd", t_) for t_ in range(NT)], writes=[("gl", w)], dma="gl%d" % w)
                P.op("dve", lambda e, qr=qr, acc=acc, w=w: e.reciprocal(
                    out=rz[w][0:qr, 0:2], in_=fr(acc[0:qr, 128:129], [[129, 2]])),
                     reads=[("accS", w)], writes=[("rz", w)])
                P.op("dve", lambda e, qr=qr, w=w, l=l: e.tensor_tensor(out=rz[w][0:qr, 2:3], in0=rz[w][0:qr, 1:2],
                                                                       in1=neglam[0:qr, l:l + 1], op=ALU.mult),
                     reads=[("rz", w), "neglam"], writes=[("rz2", w)])
                P.op("dve", lambda e, qr=qr, acc=acc, w=w: e.tensor_scalar(
                    out=o1[w][0:qr, :], in0=acc[0:qr, 0:128], scalar1=rz[w][0:qr, 0:1], scalar2=None, op0=ALU.mult),
                     reads=[("accS", w), ("rz", w)], writes=[("o1", w)])
                P.op("dve", lambda e, qr=qr, acc=acc, w=w: e.scalar_tensor_tensor(
                    out=o2[w][0:qr, :], in0=acc[0:qr, 129:257], scalar=rz[w][0:qr, 2:3], in1=o1[w][0:qr, :],
                    op0=ALU.mult, op1=ALU.add),
                     reads=[("accS", w), ("rz2", w), ("o1", w)], writes=[("o2", w)])
            for (j, tq, qr, w) in info:
                P.op("pool", lambda e, qr=qr, w=w: e.tensor_tensor(out=o3[w][0:qr, :], in0=o2[w][0:qr, :], in1=o2[w][0:qr, :],
                                                                   op=ALU.mult),
                     reads=[("o2", w)], writes=[("o3", w)])
                P.op("dve", lambda e, qr=qr, w=w: e.tensor_reduce(out=rz[w][0:qr, 3:4], in_=o3[w][0:qr, :], axis=AX.X, op=ALU.add),
                     reads=[("o3", w)], writes=[("rz3", w)])
                P.op("dve", lambda e, qr=qr, w=w: e.tensor_scalar(out=rz[w][0:qr, 3:4], in0=rz[w][0:qr, 3:4], scalar1=1.0 / 128,
                                                                  scalar2=EPS, op0=ALU.mult, op1=ALU.add),
                     reads=[("rz3", w)], writes=[("rz3", w)])
                P.op("pool", lambda e, qr=qr, w=w: e.tensor_tensor(out=rz[w][0:qr, 3:4], in0=rz[w][0:qr, 3:4], in1=mhalf[0:qr, 0:1], op=ALU.pow),
                     reads=[("rz3", w)], writes=[("rz3", w)])
                P.op("dve", lambda e, qr=qr, w=w, l=l: e.scalar_tensor_tensor(
                    out=o3[w][0:qr, :], in0=o2[w][0:qr, :], scalar=rz[w][0:qr, 3:4], in1=slg_b[0:qr, l, :],
                    op0=ALU.mult, op1=ALU.mult),
                     reads=[("o2", w), ("rz3", w), "slg_b", ("o3", w)], writes=[("o3", w)])
                P.op("pool", lambda e, qr=qr, w=w: e.tensor_tensor(out=ya[w][0:qr, :], in0=o3[w][0:qr, :],
                                                                   in1=gl[w][0:qr, :], op=ALU.mult),
                     reads=[("o3", w), ("gl", w)], writes=[("ya", w)])
                P.op("sp", lambda e, tq=tq, qr=qr, w=w: e.dma_start(
                    out=Y_d[tq:tq + qr, 512 + h * 128:512 + (h + 1) * 128], in_=ya[w][0:qr, :]),
                     reads=[("ya", w)], writes=[("YdA", tq, h)], dma="ya%d" % w)

        qk_exp(0)
        if len(steps) > 1:
            qk_exp(1)
        for i in range(len(steps)):
            if i + 2 < len(steps):
                qk_exp(i + 2)
            av(i)
            if steps[i][2] == NT - 1:
                post(i)
        P.barrier()
        if debug == "p2":
            return finish()

        P.op("sp", lambda e: e.dma_start(out=T2[:, :, :, :], in_=t2_d), writes=["T2"], dma="T2")
        hall_loads((2, 3))
        groups = [(a, mb) for a in range(16) for mb in range(NK2)]

        def rows_ap(base, a, mb, r, c0, c1):
            t_ = base[(mb * 128) * 16 + a:(mb * 128) * 16 + a + 1, c0:c1]
            return bass.AP(t_.tensor, t_.offset, [[16 * D, r], [1, c1 - c0]])

        def loads3(gi):
            a, mb = groups[gi]
            r = k2rows[mb]
            s3 = gi % NCS
            P.op("sp", lambda e, r=r, s3=s3, src=rows_ap(G_d, a, mb, r, 0, 512): e.dma_start(out=gt[s3][0:r, :], in_=src),
                 reads=[("Gd", t) for t in range(NT)], writes=[("gt", s3)], dma="gt%d" % s3)

        loads3(0)
        loads3(1)
        for gi, (a, mb) in enumerate(groups):
            r = k2rows[mb]
            s3 = gi % NCS
            s = gi % 2
            if gi + 2 < len(groups):
                loads3(gi + 2)
            if not last:
                half = len(groups) // 2
                for k in range(8):
                    if half + k * (len(groups) - half) // 8 == gi:
                        P.op("pool", lambda e, l=l, k=k: e.dma_start(out=Wsb[:, k, :], in_=Wp_d[l + 1, :, k, :]),
                             reads=[("Wp", l + 1)],
                             writes=[("WsbK", k)] + [("Hall", kt_, qd_) for kt_ in range(NK2) for qd_ in (0, 1)],
                             dma="WsbK%d" % k)
            bank = gi % 4
            n = 0
            for kt in range(NK2):
                kr = k2rows[kt]
                for x in range(2):
                    P.op("pe", lambda e, kt=kt, kr=kr, x=x, r=r, mb=mb, a=a, bank=bank, n=n: e.matmul(
                        ps[bank][0:r, :], lhsT=T2[0:kr, kt, x, mb * 128:mb * 128 + r], rhs=HallQ[a // 4][0:kr, kt, a % 4, x * 512:(x + 1) * 512],
                        start=(n == 0), stop=(n == 2 * NK2 - 1)),
                         reads=[("Hall", kt, a // 4), "T2"], writes=[("ps", bank)])
                    n += 1
            P.op("dve", lambda e, r=r, s=s, s3=s3, bank=bank: e.tensor_tensor(out=yf[s][0:r, :], in0=ps[bank][0:r, :], in1=gt[s3][0:r, :],
                                                                              op=ALU.mult),
                 reads=[("ps", bank), ("gt", s3)], writes=[("yf", s)])
            P.op("sp", lambda e, r=r, s=s, dst=rows_ap(Y_d, a, mb, r, 0, 512): e.dma_start(out=dst, in_=yf[s][0:r, :]),
                 reads=[("yf", s)], writes=[("YdF", gi)], dma="yf%d" % s)
        P.barrier()
        if debug == "p3":
            return finish()

        ntile4 = NT
        def loads4(t):
            r = _rows(t, NT)
            s = t % 2
            P.op("sp", lambda e, t=t, r=r, s=s: e.dma_start(out=yt[s][0:r, :], in_=Y_d[t * 128:t * 128 + r, :]),
                 reads=[("YdF", gi) for gi in range(16 * NK2)], writes=[("yt", s)], dma="yt%d" % s)
            P.op("sp", lambda e, r=r, s=s, src=res_src(t, r): e.dma_start(out=xr[s][0:r, :], in_=src),
                 reads=[("res", t)], writes=[("xr", s)], dma="xr%d" % s)

        def trans4(t):
            r = _rows(t, NT)
            s = t % 2
            for k in range(8):
                P.op("pe", lambda e, r=r, s=s, k=k: e.transpose(out=psb(7)[:, k * 128:k * 128 + r],
                                                                in_=yt[s][0:r, k * 128:(k + 1) * 128],
                                                                identity=ident_b[0:r, 0:r]),
                     reads=[("yt", s), "ident_b"], writes=[("ps", 7)])
            P.op("act", lambda e, r=r, s=s: e.activation(out=yT[s][:, :, 0:r],
                                                         in_=psb(7)[:, :].rearrange("p (k m) -> p k m", k=8)[:, :, 0:r], func=AF.Copy),
                 reads=[("ps", 7)], writes=[("yT", s)])

        loads4(0)
        if ntile4 > 1:
            loads4(1)
        trans4(0)
        for t in range(ntile4):
            r = _rows(t, NT)
            s = t % 2
            if t + 1 < ntile4:
                trans4(t + 1)
            for c in range(2):
                bank = (t % 2) * 2 + c
                for k in range(8):
                    P.op("pe", lambda e, r=r, s=s, k=k, c=c, bank=bank: e.matmul(
                        ps[bank][0:r, :], lhsT=yT[s][:, k, 0:r], rhs=Wo[:, k, c * 512:(c + 1) * 512],
                        start=(k == 0), stop=(k == 7)),
                         reads=[("yT", s), "Wo"], writes=[("ps", bank)])
                P.op("dve", lambda e, r=r, s=s, c=c, bank=bank: e.tensor_tensor(
                    out=ot[s][0:r, c * 512:(c + 1) * 512], in0=ps[bank][0:r, :], in1=xr[s][0:r, c * 512:(c + 1) * 512], op=ALU.add),
                     reads=[("ps", bank), ("xr", s)], writes=[("ot", s, c)])
            dst = y_d[t * 128:t * 128 + r, :] if last else res_d[t * 128:t * 128 + r, :]
            o = P.op("act", lambda e, r=r, s=s, dst=dst: e.dma_start(out=dst, in_=ot[s][0:r, :]),
                     reads=[("ot", s, 0), ("ot", s, 1)], writes=[("res", t)], dma="ot%d" % s)
            if t + 2 < ntile4:
                loads4(t + 2)
        P.barrier()
    return finish()


def _slot_pos(SEQ):
    L = SEQ + NMETA
    N2 = L // 16
    assert L == 16 * N2 and N2 % 16 == 1
    sidx = np.arange(L)
    n2, n1 = sidx // 16, sidx % 16
    return (N2 * n1 + 16 * n2) % L


def _host_tables(SEQ, DEPTH):
    NT = SEQ // 128 + 1
    LP = NT * 128
    L = SEQ + NMETA
    N2 = L // 16
    NK2 = (N2 + 127) // 128
    pos = np.zeros(LP, dtype=np.int64)
    pos[:L] = _slot_pos(SEQ)
    rot = HD // 4
    inv = (np.float32(500000.0) ** (-np.arange(0, rot, 2, dtype=np.float32) / np.float32(rot))).astype(np.float32)
    ang = pos.astype(np.float32)[:, None] * inv[None, :]
    c, s = np.cos(ang).astype(np.float32), np.sin(ang).astype(np.float32)
    tab = np.concatenate([c, c, -s, s], axis=1).astype(np.float32)
    rope = np.ascontiguousarray(tab.reshape(NT, 128, 32).transpose(1, 0, 2))
    jj = np.arange(128)
    a = 2.0 * np.pi * ((jj[:, None] * jj[None, :]) % 128) / 128.0
    sc = 1.0 / math.sqrt(L * 128.0)
    chan = np.concatenate([np.cos(a) * sc, np.sin(a) * sc], axis=1).astype(np.float32)
    j16 = np.arange(16)
    a16 = 2.0 * np.pi * ((j16[:, None] * j16[None, :]) % 16) / 16.0
    eye8 = np.eye(8)
    bd = np.stack([np.kron(eye8, np.cos(a16)), np.kron(eye8, np.sin(a16)), -np.kron(eye8, np.sin(a16))], axis=1)
    bd = np.ascontiguousarray(bd).astype(ml_dtypes.bfloat16)
    jn = np.arange(N2)
    an = 2.0 * np.pi * ((16 * jn[:, None] * jn[None, :]) % N2) / N2
    t2full = np.zeros((NK2 * 128, 2, N2 + 1), dtype=np.float64)
    t2full[:N2, 0, :N2] = np.cos(an)
    t2full[:N2, 1, :N2] = -np.sin(an)
    t2 = np.ascontiguousarray(t2full.reshape(NK2, 128, 2, N2 + 1).transpose(1, 0, 2, 3)).astype(ml_dtypes.bfloat16)
    linit = np.array([[0.8 - 0.6 * math.exp(-0.3 * li) for li in range(DEPTH)]
                      + [1.0 - (0.8 - 0.6 * math.exp(-0.3 * li)) for li in range(DEPTH)]], dtype=np.float32)
    return rope, chan, bd, t2, linit


_CACHE = {}


def run(inputs, SEQ, DEPTH, n_cores, debug=None):
    key = (SEQ, DEPTH, debug)
    if key not in _CACHE:
        _CACHE[key] = (build_nc(SEQ, DEPTH, debug), _host_tables(SEQ, DEPTH))
    nc, (rope, chan, bd, t2, linit) = _CACHE[key]
    spos = _slot_pos(SEQ)
    f = lambda a: np.ascontiguousarray(np.asarray(a, dtype=np.float32))
    x = f(inputs["x"])
    meta = f(inputs["meta_tokens"])
    ng = np.ascontiguousarray(f(inputs["norm_gain"]).reshape(DEPTH, 8, 128).transpose(0, 2, 1))
    lam4 = np.ascontiguousarray(np.stack([f(inputs["lambda_q1"]), f(inputs["lambda_k1"]),
                                          f(inputs["lambda_q2"]), f(inputs["lambda_k2"])], axis=0))
    shared = {
        "ng": ng, "w_in": f(inputs["w_in"]), "w_f": f(inputs["w_fourier"]),
        "gq": f(inputs["q_norm_gain"]), "gk": f(inputs["k_norm_gain"]), "lam4": lam4, "subln": f(inputs["subln_gain"]),
        "w_out": f(inputs["w_out"]), "rope": rope, "chan": chan, "bd": bd, "t2": t2, "linit": linit,
    }
    in_maps = []
    for i in range(n_cores):
        m = dict(shared)
        full = np.concatenate([meta, x[i]], axis=0)
        m["x"] = np.ascontiguousarray(full[spos])
        in_maps.append(m)
    res = run_bass_kernel_spmd(nc, in_maps, core_ids=list(range(n_cores)))
    if debug:
        return res.results
    outs = []
    for r in res.results:
        ys = np.asarray(r["y"], dtype=np.float32)
        full = np.empty_like(ys)
        full[spos] = ys
        outs.append(full[NMETA:])
    return np.stack(outs, axis=0)


def kernel(x, meta_tokens, norm_gain, w_in, w_fourier, q_norm_gain, k_norm_gain,
           lambda_q1, lambda_k1, lambda_q2, lambda_k2, subln_gain, w_out):
    inputs = dict(x=x, meta_tokens=meta_tokens, norm_gain=norm_gain, w_in=w_in, w_fourier=w_fourier,
                  q_norm_gain=q_norm_gain, k_norm_gain=k_norm_gain, lambda_q1=lambda_q1, lambda_k1=lambda_k1,
                  lambda_q2=lambda_q2, lambda_k2=lambda_k2, subln_gain=subln_gain, w_out=w_out)
    x = np.asarray(x)
    return run(inputs, x.shape[1], np.asarray(w_in).shape[0], x.shape[0])
```
